# Optimizing a Trainium2 kernel written in Bass

```python
import math
import jax, jax.numpy as jnp
from jax import lax
import numpy as np

D_MODEL = 1024
BATCH = 2
SEQ = 8192
DEPTH = 4
DEC_BATCH = 8
DEC_SEQ = 16
PAST_LEN = 4096

CHUNK = 64
N_META = 16
CONV_K = 4
EPS = 1e-6
N_SSD = (DEPTH + 1) // 2
N_GDN = DEPTH // 2
SSD_INNER = 2 * D_MODEL
SSD_HEADDIM = 64
SSD_HEADS = SSD_INNER // SSD_HEADDIM
SSD_GROUPS = 4
SSD_STATE = 128
SSD_CONV_DIM = SSD_INNER + 2 * SSD_GROUPS * SSD_STATE
SSD_IN_DIM = SSD_INNER + SSD_CONV_DIM + SSD_HEADS
GDN_DK = 128
GDN_DV = 256
GDN_HEADS = D_MODEL // GDN_DK
GDN_KEY_DIM = GDN_HEADS * GDN_DK
GDN_VAL_DIM = GDN_HEADS * GDN_DV
GDN_CONV_DIM = 2 * GDN_KEY_DIM + GDN_VAL_DIM
GDN_IN_DIM = GDN_CONV_DIM + GDN_VAL_DIM + 2 * GDN_HEADS

kernel_name = 'hybrid_ssd_gdn_stream'


def rmsnorm(x, w):
    x32 = x.astype(jnp.float32)
    y = x32 * lax.rsqrt(jnp.mean(x32 * x32, axis=-1, keepdims=True) + EPS)
    return (y * w.astype(jnp.float32)).astype(x.dtype)


def causal_conv(u, buf, w, b):
    T = u.shape[1]
    xp = jnp.concatenate([buf.astype(u.dtype), u], axis=1)
    out = xp[:, 0:T] * w[0]
    for k in range(1, CONV_K):
        out = out + xp[:, k:k + T] * w[k]
    if b is not None:
        out = out + b
    return out, xp[:, -(CONV_K - 1):]


def _segments(T, with_meta):
    if with_meta:
        return ((0, N_META, N_META), (N_META, T, CHUNK))
    return ((0, T, min(T, CHUNK)),)


def ssd_scan(xs, dt, a, bm, cm, h0, L):
    Bz, T, H, P = xs.shape
    G, N = bm.shape[2], bm.shape[3]
    R = H // G
    C = T // L
    xc = xs.reshape(Bz, C, L, G, R, P)
    dtc = dt.reshape(Bz, C, L, G, R)
    bc = bm.reshape(Bz, C, L, G, N)
    cc = cm.reshape(Bz, C, L, G, N)
    cs = jnp.cumsum(dtc * a.reshape(G, R), axis=2)
    incl = jnp.tril(jnp.ones((L, L), dtype=bool))[:, :, None, None]
    seg = cs[:, :, :, None] - cs[:, :, None, :]
    decay = jnp.where(incl, jnp.exp(jnp.where(incl, seg, 0.0)), 0.0)
    xdt = xc * dtc[..., None]
    cb = jnp.einsum('bclgn,bcsgn->bclsg', cc, bc)
    y_diag = jnp.einsum('bclsg,bclsgr,bcsgrp->bclgrp', cb, decay, xdt)
    cs_last = cs[:, :, -1]
    chunk_states = jnp.einsum('bcsgn,bcsgr,bcsgrp->bcgrpn', bc, jnp.exp(cs_last[:, :, None] - cs), xdt)

    def step(h, inp):
        dec, add = inp
        return h * dec[..., None, None] + add, h

    h_fin, h_prev = lax.scan(step, h0.reshape(Bz, G, R, P, N),
                             (jnp.moveaxis(jnp.exp(cs_last), 1, 0), jnp.moveaxis(chunk_states, 1, 0)))
    h_prev = jnp.moveaxis(h_prev, 0, 1)
    y_off = jnp.einsum('bclgn,bcgrpn,bclgr->bclgrp', cc, h_prev, jnp.exp(cs))
    y = (y_diag + y_off).reshape(Bz, T, H, P)
    return y, h_fin.reshape(Bz, H, P, N)


def gdn_scan(q, k, v, g, beta, h0, L):
    Bz, T, H, K = q.shape
    V = v.shape[-1]
    C = T // L
    to_blk = lambda t: jnp.swapaxes(t.reshape(Bz, C, L, H, t.shape[-1]), 2, 3)
    qc, kc, vc = to_blk(q), to_blk(k), to_blk(v)
    gc = jnp.cumsum(jnp.swapaxes(g.reshape(Bz, C, L, H), 2, 3), axis=-1)
    bc = jnp.swapaxes(beta.reshape(Bz, C, L, H), 2, 3)[..., None]
    incl = jnp.tril(jnp.ones((L, L), dtype=bool))
    strict = jnp.tril(jnp.ones((L, L), dtype=bool), -1)
    gdiff = gc[..., :, None] - gc[..., None, :]
    dec = jnp.where(incl, jnp.exp(jnp.where(incl, gdiff, 0.0)), 0.0)
    kb = kc * bc
    a_mat = jnp.where(strict, jnp.einsum('bchlk,bchsk->bchls', kb, kc) * dec, 0.0) + jnp.eye(L, dtype=q.dtype)
    u = lax.linalg.triangular_solve(a_mat, vc * bc, left_side=True, lower=True, unit_diagonal=True)
    w = lax.linalg.triangular_solve(a_mat, kb * jnp.exp(gc)[..., None], left_side=True, lower=True, unit_diagonal=True)
    qk = jnp.einsum('bchlk,bchsk->bchls', qc, kc) * dec
    q_g = qc * jnp.exp(gc)[..., None]
    g_last = gc[..., -1]
    k_g = kc * jnp.exp(g_last[..., None] - gc)[..., None]

    def step(S, inp):
        u_c, w_c, qg_c, qk_c, kg_c, gl_c = inp
        v_new = u_c - jnp.einsum('bhlk,bhkv->bhlv', w_c, S)
        o = jnp.einsum('bhlk,bhkv->bhlv', qg_c, S) + jnp.einsum('bhls,bhsv->bhlv', qk_c, v_new)
        S = S * jnp.exp(gl_c)[..., None, None] + jnp.einsum('bhlk,bhlv->bhkv', kg_c, v_new)
        return S, o

    mv = lambda t: jnp.moveaxis(t, 1, 0)
    S_fin, o = lax.scan(step, h0, (mv(u), mv(w), mv(q_g), mv(qk), mv(k_g), mv(g_last)))
    o = jnp.swapaxes(jnp.moveaxis(o, 0, 1), 2, 3).reshape(Bz, T, H, V)
    return o, S_fin


def ssd_layer(h, conv_buf, ssm_state, with_meta, norm_w, w_in, conv_w, conv_b, dt_bias, a_log, d_skip, gnorm_w, w_out):
    Bz, T, _ = h.shape
    proj = rmsnorm(h, norm_w) @ w_in
    z = proj[..., :SSD_INNER]
    xbc = proj[..., SSD_INNER:SSD_INNER + SSD_CONV_DIM]
    dt_raw = proj[..., SSD_INNER + SSD_CONV_DIM:]
    xbc, new_buf = causal_conv(xbc, conv_buf, conv_w, conv_b)
    xbc = jax.nn.silu(xbc).astype(jnp.float32)
    xs = xbc[..., :SSD_INNER].reshape(Bz, T, SSD_HEADS, SSD_HEADDIM)
    bm = xbc[..., SSD_INNER:SSD_INNER + SSD_GROUPS * SSD_STATE].reshape(Bz, T, SSD_GROUPS, SSD_STATE)
    cm = xbc[..., SSD_INNER + SSD_GROUPS * SSD_STATE:].reshape(Bz, T, SSD_GROUPS, SSD_STATE)
    dt = jax.nn.softplus(dt_raw.astype(jnp.float32) + dt_bias.astype(jnp.float32))
    a = -jnp.exp(a_log.astype(jnp.float32))
    st = ssm_state.astype(jnp.float32)
    ys = []
    for (s, e, L) in _segments(T, with_meta):
        y_s, st = ssd_scan(xs[:, s:e], dt[:, s:e], a, bm[:, s:e], cm[:, s:e], st, L)
        ys.append(y_s)
    y = jnp.concatenate(ys, axis=1) + xs * d_skip.astype(jnp.float32)[:, None]
    yg = y.reshape(Bz, T, SSD_INNER) * jax.nn.silu(z.astype(jnp.float32))
    yg = yg.reshape(Bz, T, SSD_GROUPS, SSD_INNER // SSD_GROUPS)
    yg = yg * lax.rsqrt(jnp.mean(yg * yg, axis=-1, keepdims=True) + EPS)
    yg = (yg.reshape(Bz, T, SSD_INNER) * gnorm_w.astype(jnp.float32)).astype(h.dtype)
    return h + yg @ w_out, new_buf, st


def _l2norm(t):
    return t * lax.rsqrt(jnp.sum(t * t, axis=-1, keepdims=True) + EPS)


def gdn_layer(h, conv_buf, S0, with_meta, norm_w, w_in, conv_w, dt_bias, a_log, onorm_w, w_out):
    Bz, T, _ = h.shape
    proj = rmsnorm(h, norm_w) @ w_in
    qkv = proj[..., :GDN_CONV_DIM]
    z = proj[..., GDN_CONV_DIM:GDN_CONV_DIM + GDN_VAL_DIM]
    a_raw = proj[..., GDN_CONV_DIM + GDN_VAL_DIM:GDN_CONV_DIM + GDN_VAL_DIM + GDN_HEADS]
    b_raw = proj[..., GDN_CONV_DIM + GDN_VAL_DIM + GDN_HEADS:]
    qkv, new_buf = causal_conv(qkv, conv_buf, conv_w, None)
    qkv = jax.nn.silu(qkv).astype(jnp.float32)
    q = _l2norm(qkv[..., :GDN_KEY_DIM].reshape(Bz, T, GDN_HEADS, GDN_DK)) * (GDN_DK ** -0.5)
    k = _l2norm(qkv[..., GDN_KEY_DIM:2 * GDN_KEY_DIM].reshape(Bz, T, GDN_HEADS, GDN_DK))
    v = qkv[..., 2 * GDN_KEY_DIM:].reshape(Bz, T, GDN_HEADS, GDN_DV)
    g = -jnp.exp(a_log.astype(jnp.float32)) * jax.nn.softplus(a_raw.astype(jnp.float32) + dt_bias.astype(jnp.float32))
    beta = jax.nn.sigmoid(b_raw.astype(jnp.float32))
    st = S0.astype(jnp.float32)
    os_ = []
    for (s, e, L) in _segments(T, with_meta):
        o_s, st = gdn_scan(q[:, s:e], k[:, s:e], v[:, s:e], g[:, s:e], beta[:, s:e], st, L)
        os_.append(o_s)
    o = jnp.concatenate(os_, axis=1)
    o = o * lax.rsqrt(jnp.mean(o * o, axis=-1, keepdims=True) + EPS) * onorm_w.astype(jnp.float32)
    o = o * jax.nn.silu(z.astype(jnp.float32)).reshape(Bz, T, GDN_HEADS, GDN_DV)
    o = o.reshape(Bz, T, GDN_VAL_DIM).astype(h.dtype)
    return h + o @ w_out, new_buf, st


def setup_inputs(seed: int = 0) -> dict:
    key = jax.random.key(seed)
    ks = jax.random.split(key, 32)

    def nrm(k, shape, scale):
        return jax.random.normal(k, shape, jnp.float32) * scale

    def dt_bias_init(k, shape):
        u = jax.random.uniform(k, shape, jnp.float32)
        dt = jnp.exp(math.log(1e-3) + u * (math.log(1e-1) - math.log(1e-3)))
        return dt + jnp.log(-jnp.expm1(-dt))

    return {
        'x_prompt': nrm(ks[0], (BATCH, SEQ, D_MODEL), 1.0),
        'x_sample': nrm(ks[1], (DEC_BATCH, DEC_SEQ, D_MODEL), 1.0),
        'state_ssd': nrm(ks[2], (N_SSD, DEC_BATCH, SSD_HEADS, SSD_HEADDIM, SSD_STATE), 0.1),
        'state_ssd_conv': nrm(ks[3], (N_SSD, DEC_BATCH, CONV_K - 1, SSD_CONV_DIM), 1.0),
        'state_gdn': nrm(ks[4], (N_GDN, DEC_BATCH, GDN_HEADS, GDN_DK, GDN_DV), 0.1),
        'state_gdn_conv': nrm(ks[5], (N_GDN, DEC_BATCH, CONV_K - 1, GDN_CONV_DIM), 1.0),
        'meta_tokens': nrm(ks[6], (N_META, D_MODEL), 1.0),
        'ssd_norm_w': 1.0 + nrm(ks[7], (N_SSD, D_MODEL), 0.02),
        'ssd_w_in': nrm(ks[8], (N_SSD, D_MODEL, SSD_IN_DIM), D_MODEL ** -0.5),
        'ssd_conv_w': nrm(ks[9], (N_SSD, CONV_K, SSD_CONV_DIM), CONV_K ** -0.5),
        'ssd_conv_b': nrm(ks[10], (N_SSD, SSD_CONV_DIM), 0.02),
        'ssd_dt_bias': dt_bias_init(ks[11], (N_SSD, SSD_HEADS)),
        'ssd_a_log': jnp.log(jax.random.uniform(ks[12], (N_SSD, SSD_HEADS), jnp.float32, 1.0, 16.0)),
        'ssd_d': 1.0 + nrm(ks[13], (N_SSD, SSD_HEADS), 0.02),
        'ssd_gnorm_w': 1.0 + nrm(ks[14], (N_SSD, SSD_INNER), 0.02),
        'ssd_w_out': nrm(ks[15], (N_SSD, SSD_INNER, D_MODEL), SSD_INNER ** -0.5),
        'gdn_norm_w': 1.0 + nrm(ks[16], (N_GDN, D_MODEL), 0.02),
        'gdn_w_in': nrm(ks[17], (N_GDN, D_MODEL, GDN_IN_DIM), D_MODEL ** -0.5),
        'gdn_conv_w': nrm(ks[18], (N_GDN, CONV_K, GDN_CONV_DIM), CONV_K ** -0.5),
        'gdn_dt_bias': dt_bias_init(ks[19], (N_GDN, GDN_HEADS)),
        'gdn_a_log': jnp.log(jax.random.uniform(ks[20], (N_GDN, GDN_HEADS), jnp.float32, 1.0, 16.0)),
        'gdn_onorm_w': 1.0 + nrm(ks[21], (N_GDN, GDN_DV), 0.02),
        'gdn_w_out': nrm(ks[22], (N_GDN, GDN_VAL_DIM, D_MODEL), GDN_VAL_DIM ** -0.5),
        'final_norm_w': 1.0 + nrm(ks[23], (D_MODEL,), 0.02),
    }


def reference(x_prompt, x_sample, state_ssd, state_ssd_conv, state_gdn, state_gdn_conv, meta_tokens,
              ssd_norm_w, ssd_w_in, ssd_conv_w, ssd_conv_b, ssd_dt_bias, ssd_a_log, ssd_d, ssd_gnorm_w, ssd_w_out,
              gdn_norm_w, gdn_w_in, gdn_conv_w, gdn_dt_bias, gdn_a_log, gdn_onorm_w, gdn_w_out, final_norm_w):
    Bp = x_prompt.shape[0]
    meta = jnp.broadcast_to(meta_tokens.astype(x_prompt.dtype)[None], (Bp, N_META, D_MODEL))
    hp = jnp.concatenate([meta, x_prompt], axis=1)
    hs = x_sample
    p_ssd, p_ssd_conv, p_gdn, p_gdn_conv = [], [], [], []
    s_ssd, s_ssd_conv, s_gdn, s_gdn_conv = [], [], [], []
    for i in range(DEPTH):
        j = i // 2
        if i % 2 == 0:
            w = (ssd_norm_w[j], ssd_w_in[j], ssd_conv_w[j], ssd_conv_b[j], ssd_dt_bias[j], ssd_a_log[j],
                 ssd_d[j], ssd_gnorm_w[j], ssd_w_out[j])
            zb = jnp.zeros((Bp, CONV_K - 1, SSD_CONV_DIM), hp.dtype)
            zs = jnp.zeros((Bp, SSD_HEADS, SSD_HEADDIM, SSD_STATE), jnp.float32)
            hp, cb, st = ssd_layer(hp, zb, zs, True, *w)
            p_ssd.append(st.astype(state_ssd.dtype))
            p_ssd_conv.append(cb.astype(state_ssd_conv.dtype))
            hs, cb, st = ssd_layer(hs, state_ssd_conv[j], state_ssd[j], False, *w)
            s_ssd.append(st.astype(state_ssd.dtype))
            s_ssd_conv.append(cb.astype(state_ssd_conv.dtype))
        else:
            w = (gdn_norm_w[j], gdn_w_in[j], gdn_conv_w[j], gdn_dt_bias[j], gdn_a_log[j], gdn_onorm_w[j], gdn_w_out[j])
            zb = jnp.zeros((Bp, CONV_K - 1, GDN_CONV_DIM), hp.dtype)
            zs = jnp.zeros((Bp, GDN_HEADS, GDN_DK, GDN_DV), jnp.float32)
            hp, cb, st = gdn_layer(hp, zb, zs, True, *w)
            p_gdn.append(st.astype(state_gdn.dtype))
            p_gdn_conv.append(cb.astype(state_gdn_conv.dtype))
            hs, cb, st = gdn_layer(hs, state_gdn_conv[j], state_gdn[j], False, *w)
            s_gdn.append(st.astype(state_gdn.dtype))
            s_gdn_conv.append(cb.astype(state_gdn_conv.dtype))
    y_prompt = rmsnorm(hp, final_norm_w)[:, N_META:]
    y_sample = rmsnorm(hs, final_norm_w)
    return (y_prompt, y_sample,
            jnp.stack(p_ssd), jnp.stack(p_ssd_conv), jnp.stack(p_gdn), jnp.stack(p_gdn_conv),
            jnp.stack(s_ssd), jnp.stack(s_ssd_conv), jnp.stack(s_gdn), jnp.stack(s_gdn_conv))
```

```python
import os
import numpy as np
from contextlib import ExitStack
import concourse.bass as bass
import concourse.mybir as mybir
from concourse.bass_utils import run_bass_kernel_spmd

F32 = mybir.dt.float32
BF16 = mybir.dt.bfloat16
AF = mybir.ActivationFunctionType
ALU = mybir.AluOpType

D = 1024
NMETA = 16
EPS = 1e-6
SSD_INNER = 2048
SSD_H = 32
SSD_P = 64
SSD_G = 4
SSD_N = 128
SSD_CONV = 3072
SSD_IN = 5152
GDN_H = 8
GDN_DK = 128
GDN_DV = 256
GDN_KEY = 1024
GDN_VAL = 2048
GDN_CONV = 4096
GDN_IN = 6160
NEG = -30000.0


class Sched:
    ROT = int(os.environ.get('KROT', '28000'))

    def __init__(self, nc, es):
        self.nc = nc
        self.es = es
        self.eng = {"pe": nc.tensor, "act": nc.scalar, "dve": nc.vector, "pool": nc.gpsimd, "sp": nc.sync}
        self.nsem = 0
        self.sems = []
        self.cur = {}
        for e in self.eng:
            self.cur[e] = [self._new_sem(), 0]
        self.dma_slots = {"hw": [[self._new_sem(), 0] for _ in range(12)],
                          "sw": [[self._new_sem(), 0] for _ in range(8)]}
        self.dma_rr = {"hw": 0, "sw": 0}
        self.waited = {e: {} for e in self.eng}
        self.last_w = {}
        self.readers = {}
        self.n_inst = 0

    def _new_sem(self):
        s = self.es.enter_context(self.nc.semaphore(f"sm{self.nsem}"))
        self.nsem += 1
        self.sems.append(s)
        return len(self.sems) - 1

    def _wait(self, e, deps):
        best = {}
        for (si, v) in deps:
            if best.get(si, 0) < v:
                best[si] = v
        for si, v in best.items():
            if e == "pe" and si == self.cur["pe"][0]:
                continue
            if NOSELF and si == self.cur[e][0]:
                continue
            if self.waited[e].get(si, 0) >= v:
                continue
            self.eng[e].wait_ge(self.sems[si], v)
            self.waited[e][si] = v

    def _deps(self, reads, writes):
        deps = []
        for k in reads:
            if k in self.last_w:
                deps.append(self.last_w[k])
        for k in writes:
            if k in self.last_w:
                deps.append(self.last_w[k])
            deps.extend(self.readers.get(k, ()))
        return deps

    def _stamp(self, stamp, reads, writes):
        for k in reads:
            r = self.readers.setdefault(k, {})
            if r.get(stamp[0], 0) < stamp[1]:
                r[stamp[0]] = stamp[1]
        for k in writes:
            self.last_w[k] = stamp
            self.readers[k] = {}

    def _deps2(self, reads, writes, e=None):
        deps = []
        for k in reads:
            if k in self.last_w:
                deps.append(self.last_w[k])
            if k.startswith("pb") and e is not None:
                own = self.cur[e][0]
                deps.extend((si, v) for si, v in self.readers.get(k, {}).items() if si != own)
        for k in writes:
            if k in self.last_w:
                deps.append(self.last_w[k])
            deps.extend(self.readers.get(k, {}).items())
        return deps

    stop_at = None

    def op(self, e, fn, reads=(), writes=()):
        if self.stop_at and self.n_inst >= self.stop_at:
            raise _Stop()
        self._wait(e, self._deps2(reads, writes, e))
        c = self.cur[e]
        if c[1] >= self.ROT:
            c[0] = self._new_sem()
            c[1] = 0
        inst = fn(self.eng[e])
        c[1] += 1
        inst.then_inc(self.sems[c[0]], 1)
        self._stamp((c[0], c[1]), reads, writes)
        self.n_inst += 1
        return inst

    def dma(self, e, out, in_, reads=(), writes=(), slow=False):
        self._wait(e, self._deps2(reads, writes))
        kind = "sw" if e == "pool" else "hw"
        slot = self.dma_slots[kind][self.dma_rr[kind]]
        self.dma_rr[kind] = (self.dma_rr[kind] + 1) % len(self.dma_slots[kind])
        if slot[1] >= self.ROT:
            self._wait(e, [(slot[0], slot[1])])
            slot[0] = self._new_sem()
            slot[1] = 0
        if slot[1] > 0:
            self._wait(e, [(slot[0], slot[1])])
        if slow:
            inst = self.eng[e].dma_start(out=out, in_=in_, allow_slow_non_contiguous=True)
        else:
            inst = self.eng[e].dma_start(out=out, in_=in_)
        slot[1] += 16
        inst.then_inc(self.sems[slot[0]], 16)
        self._stamp((slot[0], slot[1]), reads, writes)
        self.n_inst += 1
        return inst

    def barrier(self):
        stamps = [(s[0], s[1]) for kind in self.dma_slots for s in self.dma_slots[kind] if s[1] > 0]
        for k, c in self.cur.items():
            if c[1] > 0:
                stamps.append((c[0], c[1]))
        for e in self.eng:
            self._wait(e, [st for st in stamps if st[0] != self.cur[e][0]])

    def finish(self, e="sp"):
        deps = [(s[0], s[1]) for kind in self.dma_slots for s in self.dma_slots[kind] if s[1] > 0]
        for k, c in self.cur.items():
            if c[1] > 0 and k != e:
                deps.append((c[0], c[1]))
        self._wait(e, deps)


SKIP = os.environ.get('KSKIP', '').split(',')
NOSELF = os.environ.get('KNOSELF', '0') == '1'


class _Stop(Exception):
    pass


def build_program(SEQ, NL=4, dbg=False):
    nc = bass.Bass("TRN2", target_bir_lowering=False)
    TP = NMETA + SEQ
    assert SEQ % 128 == 0
    NCH = SEQ // 128
    TT = TP + 16
    chunks = [("s", TP, 16)] + [("p", 0, 16)] + [("p", 16 + 128 * i, 128) for i in range(NCH)]

    din = {}

    def inp(name, shape):
        din[name] = nc.dram_tensor(name, list(shape), F32, kind="ExternalInput").ap()
        return din[name]

    def outp(name, shape):
        return nc.dram_tensor(name, list(shape), F32, kind="ExternalOutput").ap()

    xp = inp("x_prompt", [SEQ, D])
    xs = inp("x_sample", [16, D])
    meta = inp("meta_tokens", [NMETA, D])
    st_ssd = inp("state_ssd", [2, SSD_H, SSD_P, SSD_N])
    st_ssdc = inp("state_ssd_conv", [2, 3, SSD_CONV])
    st_gdn = inp("state_gdn", [2, GDN_H, GDN_DK, GDN_DV])
    st_gdnc = inp("state_gdn_conv", [2, 3, GDN_CONV])
    W = {}
    for nm, shp in [("ssd_norm_w", [2, D]), ("ssd_w_in", [2, D, SSD_IN]), ("ssd_conv_w", [2, 4, SSD_CONV]),
                    ("ssd_conv_b", [2, SSD_CONV]), ("ssd_dt_bias", [2, SSD_H]), ("ssd_a_log", [2, SSD_H]),
                    ("ssd_d", [2, SSD_H]), ("ssd_gnorm_w", [2, SSD_INNER]), ("ssd_w_out", [2, SSD_INNER, D]),
                    ("gdn_norm_w", [2, D]), ("gdn_w_in", [2, D, GDN_IN]), ("gdn_conv_w", [2, 4, GDN_CONV]),
                    ("gdn_dt_bias", [2, GDN_H]), ("gdn_a_log", [2, GDN_H]), ("gdn_onorm_w", [2, GDN_DV]),
                    ("gdn_w_out", [2, GDN_VAL, D]), ("final_norm_w", [D])]:
        W[nm] = inp(nm, shp)

    y_p = outp("y_prompt", [SEQ, D])
    y_s = outp("y_sample", [16, D])
    o_pssd = outp("p_ssd", [2, SSD_H, SSD_P, SSD_N])
    o_pssdc = outp("p_ssdc", [2, 3, SSD_CONV])
    o_pgdn = outp("p_gdn", [2, GDN_H, GDN_DK, GDN_DV])
    o_pgdnc = outp("p_gdnc", [2, 3, GDN_CONV])
    o_sssd = outp("s_ssd", [2, SSD_H, SSD_P, SSD_N])
    o_sssdc = outp("s_ssdc", [2, 3, SSD_CONV])
    o_sgdn = outp("s_gdn", [2, GDN_H, GDN_DK, GDN_DV])
    o_sgdnc = outp("s_gdnc", [2, 3, GDN_CONV])

    hbuf = [nc.dram_tensor(f"hbuf{i}", [128, 8, TT], F32, kind="Internal").ap() for i in range(2)]
    dbg_o = outp("dbg_o", [128, 1024]) if dbg else None

    with ExitStack() as es0:
        S = Sched(nc, es0)

        uid = [0]

        def mk(es):
            def sb(name, shape, dt=F32):
                uid[0] += 1
                return es.enter_context(nc.sbuf_tensor(f"{name}_u{uid[0]}", list(shape), dt))
            return sb

        sb0 = mk(es0)

        def nfree(ap):
            dims = [list(d) for d in ap.ap][1:]
            dims = [d for d in dims if d[1] != 1]
            merged = []
            for d in dims:
                if merged and merged[-1][0] == d[0] * d[1]:
                    merged[-1] = [d[0], merged[-1][1] * d[1]]
                else:
                    merged.append(d)
            return len(merged)

        def MM(out, lhsT, rhs, start, stop, r, w):
            assert nfree(rhs) <= 1 and nfree(lhsT) <= 1, (nfree(rhs), nfree(lhsT), rhs, lhsT)
            S.op("pe", lambda e: e.matmul(out, lhsT=lhsT, rhs=rhs, start=start, stop=stop), reads=r, writes=w)

        def TR(out, in_, idn, r, w):
            assert nfree(in_) <= 1 and nfree(idn) <= 1, (nfree(in_), in_)
            S.op("pe", lambda e: e.transpose(out, in_, idn), reads=r, writes=w)

        def ACT(out, in_, func, r, w, scale=None, bias=None):
            kw = {}
            if scale is not None:
                kw["scale"] = scale
            if bias is not None:
                kw["bias"] = bias
            S.op("act", lambda e: e.activation(out=out, in_=in_, func=func, **kw), reads=r, writes=w)

        def TT(eng, out, in0, in1, op, r, w):
            S.op(eng, lambda e: e.tensor_tensor(out=out, in0=in0, in1=in1, op=op), reads=r, writes=w)

        def TS(eng, out, in0, s1, op0, r, w, s2=None, op1=None):
            if op1 is None:
                S.op(eng, lambda e: e.tensor_scalar(out=out, in0=in0, scalar1=s1, scalar2=None, op0=op0), reads=r, writes=w)
            else:
                S.op(eng, lambda e: e.tensor_scalar(out=out, in0=in0, scalar1=s1, scalar2=s2, op0=op0, op1=op1),
                     reads=r, writes=w)

        def STT(out, in0, scalar, in1, op0, op1, r, w):
            S.op("dve", lambda e: e.scalar_tensor_tensor(out=out, in0=in0, scalar=scalar, in1=in1, op0=op0, op1=op1),
                 reads=r, writes=w)

        def CP(eng, out, in_, r, w):
            if eng == "act":
                ACT(out, in_, AF.Copy, r, w)
            else:
                S.op(eng, lambda e: e.tensor_copy(out=out, in_=in_), reads=r, writes=w)

        def MEMSET(eng, ap, val, w):
            S.op(eng, lambda e: e.memset(ap, val), writes=w)

        ident = sb0("ident", [128, 128])
        ident_bf = sb0("ident_bf", [128, 128], BF16)
        ones_bf = sb0("ones_bf", [128, 128], BF16)
        ones_f = sb0("ones_f", [128, 128])
        tri = sb0("tri", [128, 128])
        incl = sb0("incl", [128, 128])
        mneg_f = sb0("mneg_f", [128, 512])
        mnegL = {128: sb0("mneg128", [128, 512], BF16), 16: sb0("mneg16", [128, 64], BF16)}
        mnegsL = {128: sb0("mnegs128", [128, 512], BF16), 16: sb0("mnegs16", [128, 64], BF16)}
        epsb = sb0("epsb", [128, 1])
        oneb = sb0("oneb", [128, 1])
        fnw = sb0("fnw", [128, 8])
        CONST = ["const"]
        MEMSET("pool", ident[:], 1.0, CONST)
        S.op("pool", lambda e: e.affine_select(out=ident[:], in_=ident[:], pattern=[[1, 128]], compare_op=ALU.is_equal,
                                               fill=0.0, base=0, channel_multiplier=-1), reads=CONST, writes=CONST)
        CP("pool", ident_bf[:], ident[:], CONST, CONST)
        MEMSET("pool", ones_bf[:], 1.0, CONST)
        MEMSET("pool", ones_f[:], 1.0, CONST)
        MEMSET("pool", tri[:], 1.0, CONST)
        S.op("pool", lambda e: e.affine_select(out=tri[:], in_=tri[:], pattern=[[-1, 128]], compare_op=ALU.is_ge,
                                               fill=0.0, base=-1, channel_multiplier=1), reads=CONST, writes=CONST)
        MEMSET("pool", incl[:], 1.0, CONST)
        S.op("pool", lambda e: e.affine_select(out=incl[:], in_=incl[:], pattern=[[1, 128]], compare_op=ALU.is_ge,
                                               fill=0.0, base=0, channel_multiplier=-1), reads=CONST, writes=CONST)
        for LL in (128, 16):
            for strict, dst in ((False, mnegL[LL]), (True, mnegsL[LL])):
                MEMSET("pool", mneg_f[:, 0:4 * LL], 0.0, CONST)
                S.op("pool", lambda e: e.affine_select(out=mneg_f[:, 0:4 * LL], in_=mneg_f[:, 0:4 * LL],
                                                       pattern=[[0, 4], [1, LL]],
                                                       compare_op=(ALU.is_gt if strict else ALU.is_ge), fill=NEG, base=0,
                                                       channel_multiplier=-1), reads=CONST, writes=CONST)
                CP("pool", dst[:, :], mneg_f[:, 0:4 * LL], CONST, CONST)
        MEMSET("pool", epsb[:], EPS, CONST)
        MEMSET("pool", oneb[:], 1.0, CONST)
        S.dma("sp", fnw[:], W["final_norm_w"].rearrange("(k p) -> p k", p=128), writes=CONST, slow=True)

        PB = [es0.enter_context(nc.psum_tensor(f"pb{i}", [128, 512], F32)) for i in range(8)]

        def bank3(i, n, L):
            return PB[i][:, 0:n * 128].rearrange("p (k t) -> p k t", k=n)[:, :, :L]

        with ExitStack() as es:
            sb = mk(es)
            xtok = [sb(f"xtok{i}", [128, D]) for i in range(2)]
            htile = [sb(f"htile{i}", [128, 8, 128]) for i in range(2)]
            for ci, (stream, off, L) in enumerate(chunks):
                b = ci % 2
                if stream == "s":
                    src = xs[:, :]
                elif off == 0:
                    src = meta[:, :]
                else:
                    src = xp[off - 16: off - 16 + L, :]
                S.dma("sp", xtok[b][:L, :], src, writes=[f"xtok{b}"])
                for kt in range(8):
                    bk = 2 * b + kt // 4
                    TR(PB[bk][:, (kt % 4) * 128:(kt % 4) * 128 + L], xtok[b][:L, kt * 128:(kt + 1) * 128], ident[:L, :L],
                       [f"xtok{b}", "const"], [f"pb{bk}"])
                for half in range(2):
                    bk = 2 * b + half
                    CP("act" if half else "dve", htile[b][:, 4 * half:4 * half + 4, :L], bank3(bk, 4, L),
                       [f"pb{bk}"], [f"htile{b}"])
                S.dma("sp", hbuf[0][:, :, off:off + L], htile[b][:, :, :L], reads=[f"htile{b}"], writes=[f"h0_{ci}"])
        S.barrier()

        def front(sb_t, hsrc, ci, off, L, normw):
            ht, sq, rstd, xn = sb_t
            S.dma("sp", ht[:, :, :L], hbuf[hsrc][:, :, off:off + L], reads=[f"h{hsrc}_{ci}"], writes=["ht"])
            TT("pool", sq[:, :, :L], ht[:, :, :L], ht[:, :, :L], ALU.mult, ["ht"], ["sq"])
            for kt in range(8):
                MM(PB[0][:, :L], ones_bf[:, :], sq[:, kt, :L], kt == 0, kt == 7, ["sq", "const"], ["pb0"])
            ACT(rstd[:, :L], PB[0][:, :L], AF.Ln, ["pb0", "const"], ["rstd"], scale=1.0 / D, bias=epsb[:, 0:1])
            ACT(rstd[:, :L], rstd[:, :L], AF.Exp, ["rstd"], ["rstd"], scale=-0.5)
            for kt in range(8):
                STT(xn[:, kt, :L], ht[:, kt, :L], normw[:, kt:kt + 1], rstd[:, :L], ALU.mult, ALU.mult,
                    ["ht", "rstd", "lw"], ["xn"])

        def conv_tile(bk, slot, L, ct, tmp, halo, convw, convb, xcf_out, eng_silu_key):
            t = tmp[ct % 2]
            tk = f"ctmp{ct % 2}"
            CP("act", t[:, 3:3 + L], PB[bk][:, slot * 128:slot * 128 + L], [f"pb{bk}"], [tk])
            CP("pool", t[:, 0:3], halo[:, :, ct], ["halo"], [tk])
            acc = t[:, 136:136 + L]
            if convb is not None:
                TS("dve", acc, t[:, 0:L], convw[:, ct, 0:1], ALU.mult, [tk, "lw"], [tk], s2=convb[:, ct:ct + 1], op1=ALU.add)
            else:
                TS("dve", acc, t[:, 0:L], convw[:, ct, 0:1], ALU.mult, [tk, "lw"], [tk])
            for k in range(1, 4):
                STT(acc, t[:, k:k + L], convw[:, ct, k:k + 1], acc, ALU.mult, ALU.add, [tk, "lw"], [tk])
            CP("pool", halo[:, :, ct], t[:, L:L + 3], [tk], ["halo"])
            ACT(xcf_out, acc, AF.Silu, [tk], [eng_silu_key])

        def ssd_layer(j, hsrc, hdst, last):
            with ExitStack() as es:
                try:
                    ssd_body(mk(es), j, hsrc, hdst, last)
                except _Stop:
                    S.stop_at = None
                    S.stopped = True
            S.barrier()

        def ssd_body(sb, j, hsrc, hdst, last):
            if True:
                Win = sb("Win", [128, 8, SSD_IN], BF16)
                Wout = sb("Wout", [128, 16, D], BF16)
                normw = sb("normw", [128, 8])
                convw = sb("convw", [128, 24, 4])
                convb = sb("convb", [128, 24])
                dtb = sb("dtb", [32, 1])
                aneg = sb("aneg", [32, 1])
                dsk = sb("dsk", [128, 16])
                gnw = sb("gnw", [128, 16])
                St = sb("St", [128, 2048])
                Sbf = sb("Sbf", [128, 2048], BF16)
                halo = sb("halo", [128, 3, 24])
                ht = sb("ht", [128, 8, 128])
                sq = sb("sq", [128, 16, 128], BF16)
                rstd = sb("rstd", [128, 4, 128])
                xn = sb("xn", [128, 8, 128], BF16)
                zs = sb("zs", [128, 16, 128])
                tmp = [sb(f"ctmp{i}", [128, 272]) for i in range(2)]
                xcf = sb("xcf", [128, 24, 128])
                bcbf = sb("bcbf", [128, 2, 128], BF16)
                dtT = sb("dtT", [32, 5, 128])
                onesr = sb("onesr", [32, 128])
                tokm = sb("tokm", [128, 5, 32])
                edec = sb("edec", [128, 32])
                xdt = sb("xdt", [128, 512], BF16)
                xde = sb("xde", [128, 512], BF16)
                btok = sb("btok", [128, 128], BF16)
                cbT = sb("cbT", [128, 128])
                Dq = sb("Dq", [128, 1024])
                dec = sb("dec", [128, 1024])
                WT = sb("WT", [128, 1024], BF16)
                ytok = sb("ytok", [128, 512])
                yT = sb("yT", [128, 16, 128])
                ygn = sb("ygn", [128, 16, 128], BF16)
                stg = sb("stg", [128, 16, 128])
                hstg = sb("hstg", [72, 128])

                for kt in range(8):
                    S.dma("pool", Win[:, kt, :], W["ssd_w_in"][j, kt * 128:(kt + 1) * 128, :], writes=[f"Win{kt}"])
                for kt in range(16):
                    S.dma("pool", Wout[:, kt, :], W["ssd_w_out"][j, kt * 128:(kt + 1) * 128, :], writes=[f"Wout{kt}"])
                LW = ["lw"]
                S.dma("sp", normw[:], W["ssd_norm_w"][j].rearrange("(k p) -> p k", p=128), writes=LW, slow=True)
                for k in range(4):
                    S.dma("sp", convw[:, :, k], W["ssd_conv_w"][j, k].rearrange("(c p) -> p c", p=128), writes=LW, slow=True)
                S.dma("sp", convb[:], W["ssd_conv_b"][j].rearrange("(c p) -> p c", p=128), writes=LW, slow=True)
                S.dma("sp", dtb[:], W["ssd_dt_bias"][j].rearrange("(h o) -> h o", o=1), writes=LW, slow=True)
                S.dma("sp", aneg[:], W["ssd_a_log"][j].rearrange("(h o) -> h o", o=1), writes=LW, slow=True)
                S.dma("sp", gnw[:], W["ssd_gnorm_w"][j].rearrange("(c p) -> p c", p=128), writes=LW, slow=True)
                d2 = W["ssd_d"][j].rearrange("(c two) -> two c", two=2)
                for two in range(2):
                    S.dma("sp", dsk[two * 64:(two + 1) * 64, :], d2[two].partition_broadcast(64), writes=LW, slow=True)
                ACT(aneg[:], aneg[:], AF.Exp, LW, LW)
                TS("dve", aneg[:], aneg[:], -1.0, ALU.mult, LW, LW)
                MEMSET("pool", onesr[:], 1.0, LW)
                WinK = [f"Win{kt}" for kt in range(8)]
                WoutK = [f"Wout{kt}" for kt in range(16)]

                def store_state(o_state, o_conv):
                    for c in range(16):
                        bk = 4 + (c // 4) % 2
                        TR(PB[bk][:, (c % 4) * 128:(c % 4 + 1) * 128], St[:, c * 128:(c + 1) * 128], ident[:, :],
                           ["St", "const"], [f"pb{bk}"])
                        if c % 4 == 3:
                            CP("act" if (c // 4) % 2 else "dve", stg[:, c - 3:c + 1, :], bank3(bk, 4, 128), [f"pb{bk}"], ["stg"])
                    S.dma("sp", o_state[j].rearrange("(c two) p n -> (two p) c n", two=2), stg[:, :, :], reads=["stg"],
                          writes=["ostate"])
                    TR(PB[6][:72, :128], halo[:, :, :].rearrange("p k c -> p (k c)"), ident[:, :], ["halo", "const"], ["pb6"])
                    CP("dve", hstg[:, :], PB[6][:72, :128], ["pb6"], ["hstg"])
                    for k in range(3):
                        S.dma("sp", o_conv[j, k].rearrange("(c p) -> c p", p=128), hstg[k * 24:(k + 1) * 24, :],
                              reads=["hstg"], writes=["oconv"])

                prev_stream = None
                for ci, (stream, off, L) in enumerate(chunks):
                    if stream != prev_stream:
                        if prev_stream == "s":
                            store_state(o_sssd, o_sssdc)
                        if stream == "s":
                            S.dma("sp", stg[:, :, :], st_ssd[j].rearrange("(c two) p n -> (two p) c n", two=2),
                                  reads=[], writes=["stg"])
                            for c in range(16):
                                bk = 4 + (c // 4) % 2
                                TR(PB[bk][:, (c % 4) * 128:(c % 4 + 1) * 128], stg[:, c, :], ident[:, :],
                                   ["stg", "const"], [f"pb{bk}"])
                                if c % 4 == 3:
                                    CP("act" if (c // 4) % 2 else "dve", St[:, (c - 3) * 128:(c + 1) * 128], PB[bk][:, :],
                                       [f"pb{bk}"], ["St"])
                            CP("pool", Sbf[:, :], St[:, :], ["St"], ["Sbf"])
                            S.dma("sp", hstg[:, :], st_ssdc[j].rearrange("k (c p) -> (k c) p", p=128), writes=["hstg"])
                            TR(PB[6][:, :72], hstg[:, :], ident[:72, :72], ["hstg", "const"], ["pb6"])
                            CP("dve", halo[:, :, :], PB[6][:, :72].rearrange("p (k c) -> p k c", k=3), ["pb6"], ["halo"])
                        else:
                            MEMSET("pool", St[:, :], 0.0, ["St"])
                            MEMSET("pool", Sbf[:, :], 0.0, ["Sbf"])
                            MEMSET("pool", halo[:, :, :], 0.0, ["halo"])
                        prev_stream = stream

                    front((ht, sq[:, 0:8, :], rstd[:, 0, :], xn), hsrc, ci, off, L, normw)

                    for kt in range(8):
                        MM(PB[3][:32, :L], Win[:, kt, 5120:5152], xn[:, kt, :L], kt == 0, kt == 7, ["xn", f"Win{kt}"], ["pb3"])
                    ACT(dtT[:, 0, :L], PB[3][:32, :L], AF.Exp, ["pb3", "lw"], ["dtT"], bias=dtb[:, 0:1])
                    ACT(dtT[:, 0, :L], dtT[:, 0, :L], AF.Ln, ["dtT", "const"], ["dtT"], bias=oneb[:32, 0:1])
                    TS("dve", dtT[:, 1, :L], dtT[:, 0, :L], aneg[:, 0:1], ALU.mult, ["dtT", "lw"], ["dtT"])
                    S.op("dve", lambda e: e.tensor_tensor_scan(out=dtT[:, 2, :L], data0=onesr[:, :L], data1=dtT[:, 1, :L],
                                                               initial=0.0, op0=ALU.mult, op1=ALU.add),
                         reads=["dtT", "lw"], writes=["dtT"])
                    ACT(dtT[:, 3, :L], dtT[:, 2, :L], AF.Exp, ["dtT"], ["dtT"], scale=-1.0, bias=dtT[:, 2, L - 1:L])
                    TT("dve", dtT[:, 3, :L], dtT[:, 3, :L], dtT[:, 0, :L], ALU.mult, ["dtT"], ["dtT"])
                    ACT(dtT[:, 4, :L], dtT[:, 2, :L], AF.Exp, ["dtT"], ["dtT"])
                    for q in range(5):
                        TR(PB[3][:L, 64 + q * 32:64 + (q + 1) * 32], dtT[:, q, :L], ident[:32, :32], ["dtT", "const"], ["pb3"])
                    CP("dve", tokm[:L, :, :], PB[3][:L, 64:224].rearrange("p (q h) -> p q h", q=5), ["pb3"], ["tokm"])
                    MM(PB[3][:, 256:288], ones_f[:L, :], tokm[:L, 1, :], True, True, ["tokm", "const"], ["pb3"])
                    ACT(edec[:, :], PB[3][:, 256:288], AF.Exp, ["pb3"], ["edec"])

                    order = list(range(16, 24)) + list(range(0, 16))
                    for bi in range(6):
                        bk = 1 + bi % 2
                        for sl in range(4):
                            ct = order[bi * 4 + sl]
                            c0 = 2048 + ct * 128
                            for kt in range(8):
                                MM(PB[bk][:, sl * 128:sl * 128 + L], Win[:, kt, c0:c0 + 128], xn[:, kt, :L], kt == 0, kt == 7,
                                   ["xn", f"Win{kt}"], [f"pb{bk}"])
                        for sl in range(4):
                            ct = order[bi * 4 + sl]
                            conv_tile(bk, sl, L, ct, tmp, halo, convw, convb, xcf[:, ct, :L], f"xcf{ct}")
                    for bi in range(4):
                        bk = 1 + bi % 2
                        for sl in range(4):
                            zt = bi * 4 + sl
                            for kt in range(8):
                                MM(PB[bk][:, sl * 128:sl * 128 + L], Win[:, kt, zt * 128:(zt + 1) * 128], xn[:, kt, :L],
                                   kt == 0, kt == 7, ["xn", f"Win{kt}"], [f"pb{bk}"])
                        ACT(zs[:, bi * 4:bi * 4 + 4, :L], bank3(bk, 4, L), AF.Silu, [f"pb{bk}"], [f"zs{bi}"])

                    for g in range(4):
                        hs = slice(8 * g, 8 * g + 8)
                        xk = [f"xcf{4 * g + i}" for i in range(4)]
                        CP("pool", bcbf[:, 0, :L], xcf[:, 16 + g, :L], [f"xcf{16 + g}"], ["bcbf"])
                        CP("pool", bcbf[:, 1, :L], xcf[:, 20 + g, :L], [f"xcf{20 + g}"], ["bcbf"])
                        for i in range(4):
                            TR(PB[4][:L, i * 128:(i + 1) * 128], xcf[:, 4 * g + i, :L], ident[:, :], [xk[i], "const"], ["pb4"])
                        TR(PB[5][:L, 0:128], xcf[:, 16 + g, :L], ident[:, :], [f"xcf{16 + g}", "const"], ["pb5"])
                        x3 = PB[4][:L, :].rearrange("p (h q) -> p h q", h=8)
                        TT("dve", xdt[:L, :].rearrange("p (h q) -> p h q", h=8), x3,
                           tokm[:L, 0, hs].unsqueeze(2).broadcast_to([L, 8, 64]), ALU.mult, ["pb4", "tokm"], ["xdt"])
                        TT("dve", xde[:L, :].rearrange("p (h q) -> p h q", h=8), x3,
                           tokm[:L, 3, hs].unsqueeze(2).broadcast_to([L, 8, 64]), ALU.mult, ["pb4", "tokm"], ["xde"])
                        CP("act", btok[:L, :], PB[5][:L, 0:128], ["pb5"], ["btok"])
                        MM(PB[5][:L, 128:128 + L], bcbf[:, 0, :L], bcbf[:, 1, :L], True, True, ["bcbf"], ["pb5"])
                        CP("act", cbT[:L, :L], PB[5][:L, 128:128 + L], ["pb5"], ["cbT"])
                        Dq3 = Dq[:L, 0:8 * L].rearrange("p (h t) -> p h t", h=8)
                        dec3 = dec[:L, 0:8 * L].rearrange("p (h t) -> p h t", h=8)
                        WT3 = WT[:L, 0:8 * L].rearrange("p (h t) -> p h t", h=8)
                        TT("pool", Dq3, incl[:L, :L].unsqueeze(1).broadcast_to([L, 8, L]),
                           tokm[:L, 1, hs].unsqueeze(2).broadcast_to([L, 8, L]), ALU.mult, ["const", "tokm"], ["Dq"])
                        for hf in range(2):
                            bk = 6 + hf
                            MM(PB[bk][:L, 0:4 * L], tri[:L, :L], Dq[:L, 4 * hf * L:(4 * hf + 4) * L], True, False,
                               ["Dq", "const"], [f"pb{bk}"])
                            MM(PB[bk][:L, 0:4 * L], ident_bf[:L, :L], mnegL[L][:L, :], False, True, ["const"], [f"pb{bk}"])
                            ACT(dec[:L, 4 * hf * L:(4 * hf + 4) * L], PB[bk][:L, 0:4 * L], AF.Exp, [f"pb{bk}"], ["dec"])
                        TT("dve", WT3, dec3, cbT[:L, :L].unsqueeze(1).broadcast_to([L, 8, L]), ALU.mult,
                           ["dec", "cbT"], ["WT"])
                        MM(PB[1][:L, :], bcbf[:, 1, :L], Sbf[:, g * 512:(g + 1) * 512], True, True, ["bcbf", f"Sbf{g}"], ["pb1"])
                        for h in range(8):
                            MM(PB[2][:L, h * 64:(h + 1) * 64], WT[:L, h * L:(h + 1) * L], xdt[:L, h * 64:(h + 1) * 64], True, True,
                               ["WT", "xdt"], ["pb2"])
                        TT("dve", ytok[:L, :].rearrange("p (h q) -> p h q", h=8), PB[1][:L, :].rearrange("p (h q) -> p h q", h=8),
                           tokm[:L, 4, hs].unsqueeze(2).broadcast_to([L, 8, 64]), ALU.mult, ["pb1", "tokm"], ["ytok"])
                        TT("dve", ytok[:L, :], ytok[:L, :], PB[2][:L, :], ALU.add, ["pb2", "ytok"], ["ytok"])
                        MM(PB[3][:, :], btok[:L, :], xde[:L, :], True, True, ["btok", "xde"], ["pb3"])
                        Sg = St[:, g * 512:(g + 1) * 512]
                        TT("pool", Sg.rearrange("p (h q) -> p h q", h=8), Sg.rearrange("p (h q) -> p h q", h=8),
                           edec[:, hs].unsqueeze(2).broadcast_to([128, 8, 64]), ALU.mult, [f"St{g}", "edec"], [f"St{g}"])
                        TT("dve", Sg, Sg, PB[3][:, :], ALU.add, [f"St{g}", "pb3"], [f"St{g}"])
                        CP("act", Sbf[:, g * 512:(g + 1) * 512], Sg, [f"St{g}"], [f"Sbf{g}"])
                        for i in range(4):
                            TR(PB[4][:, i * 128:i * 128 + L], ytok[:L, i * 128:(i + 1) * 128], ident[:L, :L], ["ytok", "const"], ["pb4"])
                        for i in range(4):
                            ct = 4 * g + i
                            STT(yT[:, ct, :L], xcf[:, ct, :L], dsk[:, ct:ct + 1], PB[4][:, i * 128:i * 128 + L], ALU.mult, ALU.add,
                                [xk[i], "lw", "pb4"], [f"yT{g}"])
                        TT("pool", yT[:, 4 * g:4 * g + 4, :L], yT[:, 4 * g:4 * g + 4, :L], zs[:, 4 * g:4 * g + 4, :L], ALU.mult,
                           [f"yT{g}", f"zs{g}"], [f"yT{g}"])
                        TT("pool", sq[:, 4 * g:4 * g + 4, :L], yT[:, 4 * g:4 * g + 4, :L], yT[:, 4 * g:4 * g + 4, :L], ALU.mult,
                           [f"yT{g}"], ["sq"])
                        for i in range(4):
                            MM(PB[0][:, g * 128:g * 128 + L], ones_bf[:, :], sq[:, 4 * g + i, :L], i == 0, i == 3,
                               ["sq", "const"], ["pb0"])
                    ACT(rstd[:, :, :L], bank3(0, 4, L), AF.Ln, ["pb0", "const"], ["rstd"], scale=1.0 / 512, bias=epsb[:, 0:1])
                    ACT(rstd[:, :, :L], rstd[:, :, :L], AF.Exp, ["rstd"], ["rstd"], scale=-0.5)
                    for ct in range(16):
                        STT(ygn[:, ct, :L], yT[:, ct, :L], gnw[:, ct:ct + 1], rstd[:, ct // 4, :L], ALU.mult, ALU.mult,
                            [f"yT{ct // 4}", "rstd", "lw"], ["ygn"])
                    for hf in range(2):
                        bk = 1 + hf
                        for sl in range(4):
                            dt_ = hf * 4 + sl
                            for kt in range(16):
                                MM(PB[bk][:, sl * 128:sl * 128 + L], Wout[:, kt, dt_ * 128:(dt_ + 1) * 128], ygn[:, kt, :L],
                                   kt == 0, kt == 15, ["ygn", f"Wout{kt}"], [f"pb{bk}"])
                        TT("dve", ht[:, 4 * hf:4 * hf + 4, :L], ht[:, 4 * hf:4 * hf + 4, :L], bank3(bk, 4, L), ALU.add,
                           ["ht", f"pb{bk}"], ["ht"])
                    S.dma("sp", hbuf[hdst][:, :, off:off + L], ht[:, :, :L], reads=["ht"], writes=[f"h{hdst}_{ci}"])
                store_state(o_pssd, o_pssdc)

        def gdn_layer(j, hsrc, hdst, last):
            with ExitStack() as es:
                try:
                    gdn_body(mk(es), j, hsrc, hdst, last)
                except _Stop:
                    S.stop_at = None
                    S.stopped = True
                    S.barrier()
                    dbt = mk(es)("dbt", [128, 256])
                    for _ in range(int(os.environ.get("KDUMMY", "0"))):
                        MEMSET(os.environ.get("KDUMMYENG", "dve"), dbt[:, 0:8], 0.0, ["dbt"])
                    for qq in range(4):
                        bkq = 7 if qq < 2 else 6
                        CP("dve", dbt[:, :], PB[bkq][:, (qq % 2) * 256:(qq % 2) * 256 + 256], [f"pb{bkq}"], ["dbt"])
                        S.dma("sp", dbg_o[:, qq * 256:(qq + 1) * 256], dbt[:, :], reads=["dbt"], writes=["dbgo"])
            S.barrier()

        def gdn_body(sb, j, hsrc, hdst, last):
            if True:
                normw = sb("gnormw", [128, 8])
                convw = sb("gconvw", [128, 32, 4])
                dtb = sb("gdtb", [8, 1])
                aneg = sb("ganeg", [8, 1])
                onw = sb("onw", [128, 2])
                Sg = sb("Sg", [128, 8, 256])
                halo = sb("ghalo", [128, 3, 32])
                ht = sb("ght", [128, 8, 128])
                sq = sb("gsq", [128, 8, 128], BF16)
                rstd = sb("grstd", [128, 8, 128])
                xn = sb("gxn", [128, 8, 128], BF16)
                tmp = [sb(f"gctmp{i}", [128, 272]) for i in range(2)]
                cf = sb("cf", [128, 16, 128])
                qn = sb("qn", [128, 8, 128], BF16)
                kn = sb("kn", [128, 8, 128], BF16)
                qg = sb("qg", [128, 8, 128], BF16)
                kb = sb("kb", [128, 8, 128], BF16)
                gq = sb("gq", [8, 6, 128])
                onesr = sb("gonesr", [8, 128])
                BD = sb("BD", [8, 1024])
                tokg = sb("tokg", [128, 4, 8])
                egl = sb("egl", [128, 8])
                kbg = [sb(f"kbg{i}", [128, 128], BF16) for i in range(4)]
                kgt = [sb(f"kgt{i}", [128, 128], BF16) for i in range(4)]
                bv = [sb(f"bv{i}", [128, 256], BF16) for i in range(4)]
                QKd = [sb(f"QKd{i}", [128, 128], BF16) for i in range(4)]
                Sbf = [sb(f"gSbf{i}", [128, 256], BF16) for i in range(4)]
                Dqh = [sb(f"Dqh{i}", [128, 128]) for i in range(2)]
                decI = [sb(f"decI{i}", [128, 128]) for i in range(2)]
                decS = [sb(f"decS{i}", [128, 128]) for i in range(2)]
                negw = [sb(f"negw{i}", [128, 128], BF16) for i in range(2)]
                vnew = [sb(f"vnew{i}", [128, 256], BF16) for i in range(2)]
                Pm = [sb(f"Pm{i}", [128, 512]) for i in range(2)]
                Qm = [sb(f"Qm{i}", [128, 512]) for i in range(2)]
                Rm = sb("Rm", [128, 512])
                RTm = sb("RTm", [128, 512])
                TTb = sb("TTb", [128, 512], BF16)
                Win = sb("gWin", [128, 8, GDN_IN], BF16)
                Wout = sb("gWout", [128, 16, D], BF16)
                zsb = [Pm[i][:, :].rearrange("p (k t) -> p k t", k=4) for i in range(2)]
                hstg = Rm[:96, 0:128]
                ogn = qn
                ogn2 = kn

                for kt in range(8):
                    S.dma("pool", Win[:, kt, :], W["gdn_w_in"][j, kt * 128:(kt + 1) * 128, :], writes=[f"Win{kt}"])
                for kt in range(16):
                    S.dma("pool", Wout[:, kt, :], W["gdn_w_out"][j, kt * 128:(kt + 1) * 128, :], writes=[f"Wout{kt}"])
                LW = ["lw"]
                S.dma("sp", normw[:], W["gdn_norm_w"][j].rearrange("(k p) -> p k", p=128), writes=LW, slow=True)
                for k in range(4):
                    S.dma("sp", convw[:, :, k], W["gdn_conv_w"][j, k].rearrange("(c p) -> p c", p=128), writes=LW, slow=True)
                S.dma("sp", dtb[:], W["gdn_dt_bias"][j].rearrange("(h o) -> h o", o=1), writes=LW, slow=True)
                S.dma("sp", aneg[:], W["gdn_a_log"][j].rearrange("(h o) -> h o", o=1), writes=LW, slow=True)
                S.dma("sp", onw[:], W["gdn_onorm_w"][j].rearrange("(c p) -> p c", p=128), writes=LW, slow=True)
                ACT(aneg[:], aneg[:], AF.Exp, LW, LW)
                TS("dve", aneg[:], aneg[:], -1.0, ALU.mult, LW, LW)
                MEMSET("pool", onesr[:], 1.0, LW)

                def store_state(o_state, o_conv):
                    S.dma("sp", o_state[j].rearrange("h k v -> k h v"), Sg[:, :, :], reads=[f"Sg{h}" for h in range(8)],
                          writes=["ostate"])
                    TR(PB[6][:96, :128], halo[:, :, :].rearrange("p k c -> p (k c)"), ident[:, :], ["halo", "const"], ["pb6"])
                    CP("dve", hstg[:, :], PB[6][:96, :128], ["pb6"], ["Rm"])
                    for k in range(3):
                        S.dma("sp", o_conv[j, k].rearrange("(c p) -> c p", p=128), hstg[k * 32:(k + 1) * 32, :],
                              reads=["Rm"], writes=["oconv"])

                def proj_conv(bk_list, tiles, col0, L, dst_of):
                    for bi in range(len(tiles) // 4):
                        bk = bk_list[bi % 2]
                        for sl in range(4):
                            ct = tiles[bi * 4 + sl]
                            c0 = col0 + ct * 128
                            for kt in range(8):
                                MM(PB[bk][:, sl * 128:sl * 128 + L], Win[:, kt, c0:c0 + 128], xn[:, kt, :L], kt == 0, kt == 7,
                                   ["xn", f"Win{kt}"], [f"pb{bk}"])
                        for sl in range(4):
                            ct = tiles[bi * 4 + sl]
                            out_ap, key = dst_of(ct)
                            conv_tile(bk, sl, L, ct, tmp, halo, convw, None, out_ap, key)

                prev_stream = None
                for ci, (stream, off, L) in enumerate(chunks):
                    NLEV = {128: 6, 16: 3}[L]
                    if stream != prev_stream:
                        if prev_stream == "s":
                            store_state(o_sgdn, o_sgdnc)
                        if stream == "s":
                            S.dma("sp", Sg[:, :, :], st_gdn[j].rearrange("h k v -> k h v"), writes=[f"Sg{h}" for h in range(8)])
                            S.dma("sp", hstg[:, :], st_gdnc[j].rearrange("k (c p) -> (k c) p", p=128), writes=["Rm"])
                            TR(PB[6][:, :96], hstg[:, :], ident[:96, :96], ["Rm", "const"], ["pb6"])
                            CP("dve", halo[:, :, :], PB[6][:, :96].rearrange("p (k c) -> p k c", k=3), ["pb6"], ["halo"])
                        else:
                            MEMSET("pool", Sg[:, :, :], 0.0, [f"Sg{h}" for h in range(8)])
                            MEMSET("pool", halo[:, :, :], 0.0, ["halo"])
                        prev_stream = stream

                    front((ht, sq, rstd[:, 0, :], xn), hsrc, ci, off, L, normw)
                    if dbg and dbg <= 1:
                        break

                    for q, c0 in ((0, 6144), (1, 6152)):
                        for kt in range(8):
                            MM(PB[3][:8, q * 128:q * 128 + L], Win[:, kt, c0:c0 + 8], xn[:, kt, :L], kt == 0, kt == 7,
                               ["xn", f"Win{kt}"], ["pb3"])
                    ACT(gq[:, 0, :L], PB[3][:8, 0:L], AF.Exp, ["pb3", "lw"], ["gq"], bias=dtb[:, 0:1])
                    ACT(gq[:, 0, :L], gq[:, 0, :L], AF.Ln, ["gq", "const"], ["gq"], bias=oneb[:8, 0:1])
                    TS("dve", gq[:, 0, :L], gq[:, 0, :L], aneg[:, 0:1], ALU.mult, ["gq", "lw"], ["gq"])
                    S.op("dve", lambda e: e.tensor_tensor_scan(out=gq[:, 1, :L], data0=onesr[:, :L], data1=gq[:, 0, :L],
                                                               initial=0.0, op0=ALU.mult, op1=ALU.add),
                         reads=["gq", "lw"], writes=["gq"])
                    ACT(gq[:, 2, :L], PB[3][:8, 128:128 + L], AF.Exp, ["pb3"], ["gq"], scale=-1.0)
                    TS("dve", gq[:, 2, :L], gq[:, 2, :L], 1.0, ALU.add, ["gq"], ["gq"])
                    S.op("dve", lambda e: e.reciprocal(out=gq[:, 2, :L], in_=gq[:, 2, :L]), reads=["gq"], writes=["gq"])
                    ACT(gq[:, 3, :L], gq[:, 1, :L], AF.Exp, ["gq"], ["gq"])
                    TT("dve", gq[:, 4, :L], gq[:, 2, :L], gq[:, 3, :L], ALU.mult, ["gq"], ["gq"])
                    ACT(gq[:, 5, :L], gq[:, 1, :L], AF.Exp, ["gq"], ["gq"], scale=-1.0, bias=gq[:, 1, L - 1:L])
                    for qi, q in enumerate((0, 2, 4, 5)):
                        TR(PB[3][:L, 256 + qi * 8:256 + (qi + 1) * 8], gq[:, q, :L], ident[:8, :8], ["gq", "const"], ["pb3"])
                    CP("dve", tokg[:L, :, :], PB[3][:L, 256:288].rearrange("p (q h) -> p q h", q=4), ["pb3"], ["tokg"])
                    MM(PB[3][:, 320:328], ones_f[:L, :], tokg[:L, 0, :], True, True, ["tokg", "const"], ["pb3"])
                    ACT(egl[:, :], PB[3][:, 320:328], AF.Exp, ["pb3"], ["egl"])

                    if dbg and dbg <= 2:
                        break
                    proj_conv([1, 2], list(range(16)), 0, L, lambda ct: (cf[:, ct, :L], f"cf{ct}"))
                    for qd in range(4):
                        t0 = qd * 4
                        cfk = [f"cf{t0 + i}" for i in range(4)]
                        TT("pool", sq[:, 0:4, :L], cf[:, t0:t0 + 4, :L], cf[:, t0:t0 + 4, :L], ALU.mult, cfk, ["sq"])
                        for i in range(4):
                            MM(PB[0][:, i * 128:i * 128 + L], ones_bf[:, :], sq[:, i, :L], True, True, ["sq", "const"], ["pb0"])
                        ACT(rstd[:, 0:4, :L], bank3(0, 4, L), AF.Ln, ["pb0", "const"], ["rstd"], bias=epsb[:, 0:1])
                        ACT(rstd[:, 0:4, :L], rstd[:, 0:4, :L], AF.Exp, ["rstd"], ["rstd"], scale=-0.5)
                        if qd < 2:
                            STT(qn[:, t0:t0 + 4, :L], cf[:, t0:t0 + 4, :L], float(GDN_DK ** -0.5), rstd[:, 0:4, :L], ALU.mult, ALU.mult,
                                cfk + ["rstd"], ["qn"])
                        else:
                            TT("dve", kn[:, t0 - 8:t0 - 4, :L], cf[:, t0:t0 + 4, :L], rstd[:, 0:4, :L], ALU.mult, cfk + ["rstd"], ["kn"])
                    if dbg and dbg <= 3:
                        break
                    for q, src, dst, dkey in ((2, kn, kb, "kb"), (3, qn, qg, "qg")):
                        TT("pool", BD[:8, 0:8 * L].rearrange("p (h t) -> p h t", h=8),
                           gq[:8, q, :L].unsqueeze(1).broadcast_to([8, 8, L]),
                           ident[:8, :8].unsqueeze(2).broadcast_to([8, 8, L]), ALU.mult, ["gq", "const"], ["BD"])
                        for hf in range(2):
                            MM(PB[4 + hf][:, 0:4 * L], ones_f[:8, :], BD[:8, 4 * hf * L:(4 * hf + 4) * L], True, True,
                               ["BD", "const"], [f"pb{4 + hf}"])
                            TT("dve", dst[:, 4 * hf:4 * hf + 4, :L], src[:, 4 * hf:4 * hf + 4, :L],
                               PB[4 + hf][:, 0:4 * L].rearrange("p (h t) -> p h t", h=4), ALU.mult,
                               [f"pb{4 + hf}", "qn" if q == 3 else "kn"], [dkey])
                    if dbg and dbg <= 4:
                        break
                    proj_conv([1, 2], list(range(16, 32)), 0, L, lambda ct: (cf[:, ct - 16, :L], f"cf{ct - 16}"))

                    if dbg and dbg <= 5:
                        break
                    for hb in range(2):
                        for hh in range(4):
                            h = hb * 4 + hh
                            p = hh % 2
                            bA, bB = 4 + p, 6 + p
                            kA, kB = f"pb{bA}", f"pb{bB}"
                            if 'a' not in SKIP:
                                CP("pool", Sbf[hh][:, :], Sg[:, h, :], [f"Sg{h}"], [f"Sbf{hh}"])
                            if 'b' not in SKIP:
                                MM(PB[bA][:L, 0:128], kn[:, h, :L], ident_bf[:, :], True, True, ["kn", "const"], [kA])
                            for vt in range(2):
                                TR(PB[bA][:L, 128 + vt * 128:256 + vt * 128], cf[:, 2 * h + vt, :L], ident[:, :],
                                   [f"cf{2 * h + vt}", "const"], [kA])
                            if 'c' not in SKIP:
                                TS("dve", kbg[hh][:L, :], PB[bA][:L, 0:128], tokg[:L, 2, h:h + 1], ALU.mult, [kA, "tokg"], [f"kbg{hh}"])
                            if 'd' not in SKIP:
                                TS("dve", kgt[hh][:L, :], PB[bA][:L, 0:128], tokg[:L, 3, h:h + 1], ALU.mult, [kA, "tokg"], [f"kgt{hh}"])
                            if 'e' not in SKIP:
                                TS("dve", bv[hh][:L, :], PB[bA][:L, 128:384], tokg[:L, 1, h:h + 1], ALU.mult, [kA, "tokg"], [f"bv{hh}"])
                            MM(PB[bB][:L, 0:L], kn[:, h, :L], kb[:, h, :L], True, True, ["kn", "kb"], [kB])
                            MM(PB[bB][:L, 128:128 + L], kn[:, h, :L], qn[:, h, :L], True, True, ["kn", "qn"], [kB])
                            if 'f' not in SKIP:
                                TS("pool", Dqh[p][:L, :L], incl[:L, :L], tokg[:L, 0, h:h + 1], ALU.mult, ["const", "tokg"], [f"Dqh{p}"])
                            MM(PB[bB][:L, 256:256 + L], tri[:L, :L], Dqh[p][:L, :L], True, False, [f"Dqh{p}", "const"], [kB])
                            MM(PB[bB][:L, 256:256 + L], ident_bf[:L, :L], mnegL[L][:L, 0:L], False, True, ["const"], [kB])
                            MM(PB[bB][:L, 384:384 + L], tri[:L, :L], Dqh[p][:L, :L], True, False, [f"Dqh{p}", "const"], [kB])
                            MM(PB[bB][:L, 384:384 + L], ident_bf[:L, :L], mnegsL[L][:L, 0:L], False, True, ["const"], [kB])
                            ACT(decI[p][:L, :L], PB[bB][:L, 256:256 + L], AF.Exp, [kB], [f"decI{p}"])
                            ACT(decS[p][:L, :L], PB[bB][:L, 384:384 + L], AF.Exp, [kB], [f"decS{p}"])
                            TT("dve", QKd[hh][:L, :L], PB[bB][:L, 128:128 + L], decI[p][:L, :L], ALU.mult, [kB, f"decI{p}"], [f"QKd{hh}"])
                            STT(Pm[0][:L, hh * L:(hh + 1) * L], PB[bB][:L, 0:L], -1.0, decS[p][:L, :L], ALU.mult, ALU.mult,
                                [kB, f"decS{p}"], ["Pm0"])
                        if dbg and dbg <= 6:
                            break
                        for hh in range(4):
                            TR(PB[1][:L, hh * L:(hh + 1) * L], Pm[0][:L, hh * L:(hh + 1) * L], ident[:L, :L], ["Pm0", "const"], ["pb1"])
                        CP("act", Qm[0][:L, 0:4 * L], PB[1][:L, 0:4 * L], ["pb1"], ["Qm0"])
                        i4 = ident[:L, :L].unsqueeze(1).broadcast_to([L, 4, L])
                        TT("pool", Rm[:L, 0:4 * L].rearrange("p (h t) -> p h t", h=4),
                           Pm[0][:L, 0:4 * L].rearrange("p (h t) -> p h t", h=4), i4, ALU.add, ["Pm0", "const"], ["Rm"])
                        TT("pool", RTm[:L, 0:4 * L].rearrange("p (h t) -> p h t", h=4),
                           Qm[0][:L, 0:4 * L].rearrange("p (h t) -> p h t", h=4), i4, ALU.add, ["Qm0", "const"], ["RTm"])
                        for lev in range(1, NLEV + 1):
                            a, b = (lev - 1) % 2, lev % 2
                            lastlev = lev == NLEV
                            for hh in range(4):
                                sl = slice(hh * L, (hh + 1) * L)
                                MM(PB[0][:L, sl], Qm[a][:L, sl], Pm[a][:L, sl], True, True, [f"Qm{a}", f"Pm{a}"], ["pb0"])
                            CP("act", Pm[b][:L, 0:4 * L], PB[0][:L, 0:4 * L], ["pb0"], [f"Pm{b}"])
                            if not lastlev:
                                for hh in range(4):
                                    sl = slice(hh * L, (hh + 1) * L)
                                    MM(PB[1][:L, sl], Pm[a][:L, sl], Qm[a][:L, sl], True, True, [f"Qm{a}", f"Pm{a}"], ["pb1"])
                                CP("act", Qm[b][:L, 0:4 * L], PB[1][:L, 0:4 * L], ["pb1"], [f"Qm{b}"])
                            for hh in range(4):
                                sl = slice(hh * L, (hh + 1) * L)
                                MM(PB[2][:L, sl], RTm[:L, sl], Pm[b][:L, sl], True, True, ["RTm", f"Pm{b}"], ["pb2"])
                            if not lastlev:
                                for hh in range(4):
                                    sl = slice(hh * L, (hh + 1) * L)
                                    MM(PB[3][:L, sl], Pm[b][:L, sl], RTm[:L, sl], True, True, ["RTm", f"Pm{b}"], ["pb3"])
                            TT("dve", Rm[:L, 0:4 * L], Rm[:L, 0:4 * L], PB[2][:L, 0:4 * L], ALU.add, ["Rm", "pb2"], ["Rm"])
                            if not lastlev:
                                TT("dve", RTm[:L, 0:4 * L], RTm[:L, 0:4 * L], PB[3][:L, 0:4 * L], ALU.add, ["RTm", "pb3"], ["RTm"])
                        CP("pool", TTb[:L, 0:4 * L], Rm[:L, 0:4 * L], ["Rm"], ["TTb"])
                        if dbg and dbg <= 7:
                            break
                        for hh in range(4):
                            h = hb * 4 + hh
                            p = hh % 2
                            bA, bB = 4 + p, 6 + p
                            kA, kB = f"pb{bA}", f"pb{bB}"
                            sl = slice(hh * L, (hh + 1) * L)
                            MM(PB[bA][:, 0:L], kbg[hh][:L, :], TTb[:L, sl], True, True, [f"kbg{hh}", "TTb"], [kA])
                            ACT(negw[p][:, :L], PB[bA][:, 0:L], AF.Copy, [kA], [f"negw{p}"], scale=-1.0)
                            MM(PB[bA][:L, 128:384], TTb[:L, sl], bv[hh][:L, :], True, False, [f"bv{hh}", "TTb"], [kA])
                            MM(PB[bA][:L, 128:384], negw[p][:, :L], Sbf[hh][:, :], False, True, [f"negw{p}", f"Sbf{hh}"], [kA])
                            CP("dve", vnew[p][:L, :], PB[bA][:L, 128:384], [kA], [f"vnew{p}"])
                            for vt in range(2):
                                MM(PB[bB][:, vt * 128:vt * 128 + L], Sbf[hh][:, vt * 128:(vt + 1) * 128], qg[:, h, :L], True, False,
                                   [f"Sbf{hh}", "qg"], [kB])
                                MM(PB[bB][:, vt * 128:vt * 128 + L], vnew[p][:L, vt * 128:(vt + 1) * 128], QKd[hh][:L, :L], False, True,
                                   [f"vnew{p}", f"QKd{hh}"], [kB])
                            CP("act", cf[:, 2 * h:2 * h + 2, :L], bank3(bB, 2, L), [kB], [f"cf{2 * h}", f"cf{2 * h + 1}"])
                            MM(PB[bB][:, 256:512], kgt[hh][:L, :], vnew[p][:L, :], True, True, [f"kgt{hh}", f"vnew{p}"], [kB])
                            STT(Sg[:, h, :], Sg[:, h, :], egl[:, h:h + 1], PB[bB][:, 256:512], ALU.mult, ALU.add,
                                [f"Sg{h}", "egl", kB], [f"Sg{h}"])

                    if dbg and dbg <= 8:
                        break
                    cfall = [f"cf{t}" for t in range(16)]
                    for hf in range(2):
                        TT("pool", sq[:, :, :L], cf[:, 8 * hf:8 * hf + 8, :L], cf[:, 8 * hf:8 * hf + 8, :L], ALU.mult, cfall, ["sq"])
                        bk = 0 if hf == 0 else 3
                        for hh in range(4):
                            for vt in range(2):
                                MM(PB[bk][:, hh * 128:hh * 128 + L], ones_bf[:, :], sq[:, 2 * hh + vt, :L], vt == 0, vt == 1,
                                   ["sq", "const"], [f"pb{bk}"])
                        ACT(rstd[:, 4 * hf:4 * hf + 4, :L], bank3(bk, 4, L), AF.Ln, [f"pb{bk}", "const"], ["rstd"],
                            scale=1.0 / GDN_DV, bias=epsb[:, 0:1])
                    ACT(rstd[:, :, :L], rstd[:, :, :L], AF.Exp, ["rstd"], ["rstd"], scale=-0.5)
                    for bi in range(4):
                        bk = 1 + bi % 2
                        zb = zsb[bi % 2]
                        for sl_ in range(4):
                            zt = bi * 4 + sl_
                            c0 = 4096 + zt * 128
                            for kt in range(8):
                                MM(PB[bk][:, sl_ * 128:sl_ * 128 + L], Win[:, kt, c0:c0 + 128], xn[:, kt, :L], kt == 0, kt == 7,
                                   ["xn", f"Win{kt}"], [f"pb{bk}"])
                        ACT(zb[:, :, :L], bank3(bk, 4, L), AF.Silu, [f"pb{bk}"], [f"Pm{bi % 2}"])
                        for sl_ in range(4):
                            zt = bi * 4 + sl_
                            STT(cf[:, zt, :L], cf[:, zt, :L], onw[:, zt % 2:zt % 2 + 1], rstd[:, zt // 2, :L], ALU.mult, ALU.mult,
                                [f"cf{zt}", "rstd", "lw"], [f"cf{zt}"])
                        dsto = (ogn if bi < 2 else ogn2)[:, (bi % 2) * 4:(bi % 2) * 4 + 4, :L]
                        TT("dve", dsto, cf[:, bi * 4:bi * 4 + 4, :L], zb[:, :, :L], ALU.mult,
                           [f"cf{bi * 4 + i}" for i in range(4)] + [f"Pm{bi % 2}"], ["qn" if bi < 2 else "kn"])
                    for hf in range(2):
                        bk = 1 + hf
                        for sl_ in range(4):
                            dt_ = hf * 4 + sl_
                            for kt in range(16):
                                src_o = (ogn if kt < 8 else ogn2)[:, kt % 8, :L]
                                MM(PB[bk][:, sl_ * 128:sl_ * 128 + L], Wout[:, kt, dt_ * 128:(dt_ + 1) * 128], src_o,
                                   kt == 0, kt == 15, ["qn", "kn", f"Wout{kt}"], [f"pb{bk}"])
                        TT("dve", ht[:, 4 * hf:4 * hf + 4, :L], ht[:, 4 * hf:4 * hf + 4, :L], bank3(bk, 4, L), ALU.add,
                           ["ht", f"pb{bk}"], ["ht"])
                    S.dma("sp", hbuf[hdst][:, :, off:off + L], ht[:, :, :L], reads=["ht"], writes=[f"h{hdst}_{ci}"])
                store_state(o_pgdn, o_pgdnc)

        cur = 0
        if dbg and dbg > 100:
            S.stop_at = dbg
        S.stopped = False
        for li in range(NL):
            if li % 2 == 0:
                ssd_layer(li // 2, cur, 1 - cur, li == NL - 1)
            else:
                gdn_layer(li // 2, cur, 1 - cur, li == NL - 1)
            cur = 1 - cur
            if S.stopped:
                print("stopped at", S.n_inst)
                break

        with ExitStack() as es:
            sb = mk(es)
            htile = [sb(f"htile{i}", [128, 8, 128]) for i in range(2)]
            sq = [sb(f"sq{i}", [128, 8, 128], BF16) for i in range(2)]
            rstd = [sb(f"rstd{i}", [128, 128]) for i in range(2)]
            ytile = [sb(f"ytile{i}", [128, 8, 128]) for i in range(2)]
            ytok = [sb(f"ytok{i}", [128, D]) for i in range(2)]
            for ci, (stream, off, L) in enumerate(chunks):
                if stream == "p" and off == 0:
                    continue
                b = ci % 2
                S.dma("sp", htile[b][:, :, :L], hbuf[cur][:, :, off:off + L], reads=[f"h{cur}_{ci}"], writes=[f"htile{b}"])
                TT("pool", sq[b][:, :, :L], htile[b][:, :, :L], htile[b][:, :, :L], ALU.mult, [f"htile{b}"], [f"sq{b}"])
                bk = 4 + b
                for kt in range(8):
                    MM(PB[bk][:, :L], ones_bf[:, :], sq[b][:, kt, :L], kt == 0, kt == 7, [f"sq{b}", "const"], [f"pb{bk}"])
                ACT(rstd[b][:, :L], PB[bk][:, :L], AF.Ln, [f"pb{bk}", "const"], [f"rstd{b}"], scale=1.0 / D, bias=epsb[:, 0:1])
                ACT(rstd[b][:, :L], rstd[b][:, :L], AF.Exp, [f"rstd{b}"], [f"rstd{b}"], scale=-0.5)
                for kt in range(8):
                    STT(ytile[b][:, kt, :L], htile[b][:, kt, :L], fnw[:, kt:kt + 1], rstd[b][:, :L], ALU.mult, ALU.mult,
                        [f"htile{b}", "const", f"rstd{b}"], [f"ytile{b}"])
                for kt in range(8):
                    bk2 = 2 * b + kt // 4
                    TR(PB[bk2][:L, (kt % 4) * 128:(kt % 4 + 1) * 128], ytile[b][:, kt, :L], ident[:, :],
                       [f"ytile{b}", "const"], [f"pb{bk2}"])
                for half in range(2):
                    bk2 = 2 * b + half
                    CP("act" if half else "dve", ytok[b][:L, 512 * half:512 * half + 512], PB[bk2][:L, :], [f"pb{bk2}"], [f"ytok{b}"])
                dst = y_s[:, :] if stream == "s" else y_p[off - 16: off - 16 + L, :]
                S.dma("sp", dst, ytok[b][:L, :], reads=[f"ytok{b}"], writes=[f"yout{ci}"])
        S.finish("sp")
        print("instructions:", S.n_inst, "sems:", S.nsem)
    return nc


_PROG_CACHE = {}

_WNAMES = ["ssd_norm_w", "ssd_w_in", "ssd_conv_w", "ssd_conv_b", "ssd_dt_bias", "ssd_a_log", "ssd_d", "ssd_gnorm_w",
           "ssd_w_out", "gdn_norm_w", "gdn_w_in", "gdn_conv_w", "gdn_dt_bias", "gdn_a_log", "gdn_onorm_w", "gdn_w_out",
           "final_norm_w"]


def kernel(**inputs):
    f32 = lambda a: np.ascontiguousarray(np.asarray(a), dtype=np.float32)
    x_prompt = f32(inputs["x_prompt"])
    x_sample = f32(inputs["x_sample"])
    B, SEQ, _ = x_prompt.shape
    NS = x_sample.shape[0]
    n = 8
    assert NS == n and B <= n
    if SEQ not in _PROG_CACHE:
        _PROG_CACHE[SEQ] = build_program(SEQ)
    nc = _PROG_CACHE[SEQ]
    wts = {k: f32(inputs[k]) for k in _WNAMES}
    st = {k: f32(inputs[k]) for k in ["state_ssd", "state_ssd_conv", "state_gdn", "state_gdn_conv"]}
    meta = f32(inputs["meta_tokens"])
    in_maps = []
    for c in range(n):
        m = {"x_prompt": x_prompt[c % B], "x_sample": x_sample[c], "meta_tokens": meta}
        for k, v in st.items():
            m[k] = np.ascontiguousarray(v[:, c])
        m.update(wts)
        in_maps.append(m)
    res = run_bass_kernel_spmd(nc, in_maps, core_ids=list(range(n)))
    R = res.results
    y_prompt = np.stack([R[b]["y_prompt"] for b in range(B)])
    y_sample = np.stack([R[c]["y_sample"] for c in range(n)])
    outs = [y_prompt, y_sample]
    for nm in ["p_ssd", "p_ssdc", "p_gdn", "p_gdnc"]:
        outs.append(np.stack([R[b][nm] for b in range(B)], axis=1))
    for nm in ["s_ssd", "s_ssdc", "s_gdn", "s_gdnc"]:
        outs.append(np.stack([R[c][nm] for c in range(n)], axis=1))
    return tuple(np.ascontiguousarray(o, dtype=np.float32) for o in outs)
```

```python
import os
import numpy as np
from contextlib import ExitStack
import concourse.bass as bass
import concourse.mybir as mybir
from concourse.bass_utils import run_bass_kernel_spmd

F32 = mybir.dt.float32
BF16 = mybir.dt.bfloat16
AF = mybir.ActivationFunctionType
ALU = mybir.AluOpType

D = 1024
NMETA = 16
EPS = 1e-6
SSD_INNER = 2048
SSD_H = 32
SSD_P = 64
SSD_G = 4
SSD_N = 128
SSD_CONV = 3072
SSD_IN = 5152
GDN_H = 8
GDN_DK = 128
GDN_DV = 256
GDN_KEY = 1024
GDN_VAL = 2048
GDN_CONV = 4096
GDN_IN = 6160
NEG = -30000.0
NTMP = 4


class Sched:
    ROT = int(os.environ.get('KROT', '28000'))

    def __init__(self, nc, es):
        self.nc = nc
        self.es = es
        self.eng = {"pe": nc.tensor, "act": nc.scalar, "dve": nc.vector, "pool": nc.gpsimd, "sp": nc.sync}
        self.nsem = 0
        self.sems = []
        self.cur = {}
        for e in self.eng:
            self.cur[e] = [self._new_sem(), 0]
        self.dma_slots = {"hw": [[self._new_sem(), 0] for _ in range(12)],
                          "sw": [[self._new_sem(), 0] for _ in range(8)]}
        self.dma_rr = {"hw": 0, "sw": 0}
        self.waited = {e: {} for e in self.eng}
        self.last_w = {}
        self.readers = {}
        self.n_inst = 0

    def _new_sem(self):
        s = self.es.enter_context(self.nc.semaphore(f"sm{self.nsem}"))
        self.nsem += 1
        self.sems.append(s)
        return len(self.sems) - 1

    def _wait(self, e, deps):
        best = {}
        for (si, v) in deps:
            if best.get(si, 0) < v:
                best[si] = v
        for si, v in best.items():
            if e == "pe" and si == self.cur["pe"][0]:
                continue
            if NOSELF and si == self.cur[e][0]:
                continue
            if self.waited[e].get(si, 0) >= v:
                continue
            self.eng[e].wait_ge(self.sems[si], v)
            self.waited[e][si] = v

    def _deps(self, reads, writes):
        deps = []
        for k in reads:
            if k in self.last_w:
                deps.append(self.last_w[k])
        for k in writes:
            if k in self.last_w:
                deps.append(self.last_w[k])
            deps.extend(self.readers.get(k, ()))
        return deps

    def _stamp(self, stamp, reads, writes):
        for k in reads:
            r = self.readers.setdefault(k, {})
            if r.get(stamp[0], 0) < stamp[1]:
                r[stamp[0]] = stamp[1]
        for k in writes:
            self.last_w[k] = stamp
            self.readers[k] = {}

    def _deps2(self, reads, writes, e=None):
        deps = []
        for k in reads:
            if k in self.last_w:
                deps.append(self.last_w[k])
            if k.startswith("pb") and e is not None:
                own = self.cur[e][0]
                deps.extend((si, v) for si, v in self.readers.get(k, {}).items() if si != own)
        for k in writes:
            if k in self.last_w:
                deps.append(self.last_w[k])
            deps.extend(self.readers.get(k, {}).items())
        return deps

    stop_at = None
    names = None
    phase = "pre"

    def op(self, e, fn, reads=(), writes=()):
        if self.stop_at and self.n_inst >= self.stop_at:
            raise _Stop()
        self._wait(e, self._deps2(reads, writes, e))
        c = self.cur[e]
        if c[1] >= self.ROT:
            c[0] = self._new_sem()
            c[1] = 0
        inst = fn(self.eng[e])
        c[1] += 1
        if self.names is not None:
            try:
                self.names[str(inst.ins.name)] = self.phase
            except Exception:
                pass
        inst.then_inc(self.sems[c[0]], 1)
        self._stamp((c[0], c[1]), reads, writes)
        self.n_inst += 1
        return inst

    def dma(self, e, out, in_, reads=(), writes=(), slow=False):
        self._wait(e, self._deps2(reads, writes))
        kind = "sw" if e == "pool" else "hw"
        slot = self.dma_slots[kind][self.dma_rr[kind]]
        self.dma_rr[kind] = (self.dma_rr[kind] + 1) % len(self.dma_slots[kind])
        if slot[1] >= self.ROT:
            self._wait(e, [(slot[0], slot[1])])
            slot[0] = self._new_sem()
            slot[1] = 0
        if slot[1] > 0:
            self._wait(e, [(slot[0], slot[1])])
        if slow:
            inst = self.eng[e].dma_start(out=out, in_=in_, allow_slow_non_contiguous=True)
        else:
            inst = self.eng[e].dma_start(out=out, in_=in_)
        slot[1] += 16
        inst.then_inc(self.sems[slot[0]], 16)
        self._stamp((slot[0], slot[1]), reads, writes)
        self.n_inst += 1
        return inst

    def barrier(self):
        stamps = [(s[0], s[1]) for kind in self.dma_slots for s in self.dma_slots[kind] if s[1] > 0]
        for k, c in self.cur.items():
            if c[1] > 0:
                stamps.append((c[0], c[1]))
        for e in self.eng:
            self._wait(e, [st for st in stamps if st[0] != self.cur[e][0]])

    def finish(self, e="sp"):
        deps = [(s[0], s[1]) for kind in self.dma_slots for s in self.dma_slots[kind] if s[1] > 0]
        for k, c in self.cur.items():
            if c[1] > 0 and k != e:
                deps.append((c[0], c[1]))
        self._wait(e, deps)


SKIP = os.environ.get('KSKIP', '').split(',')
NOSELF = os.environ.get('KNOSELF', '0') == '1'


class _Stop(Exception):
    pass


def build_program(SEQ, NL=4, dbg=False):
    nc = bass.Bass("TRN2", target_bir_lowering=False)
    TP = NMETA + SEQ
    assert SEQ % 128 == 0
    NCH = SEQ // 128
    TT = TP + 16
    chunks = [("s", TP, 16)] + [("p", 0, 16)] + [("p", 16 + 128 * i, 128) for i in range(NCH)]

    din = {}

    def inp(name, shape):
        din[name] = nc.dram_tensor(name, list(shape), F32, kind="ExternalInput").ap()
        return din[name]

    def outp(name, shape):
        return nc.dram_tensor(name, list(shape), F32, kind="ExternalOutput").ap()

    xp = inp("x_prompt", [SEQ, D])
    xs = inp("x_sample", [16, D])
    meta = inp("meta_tokens", [NMETA, D])
    st_ssd = inp("state_ssd", [2, SSD_H, SSD_P, SSD_N])
    st_ssdc = inp("state_ssd_conv", [2, 3, SSD_CONV])
    st_gdn = inp("state_gdn", [2, GDN_H, GDN_DK, GDN_DV])
    st_gdnc = inp("state_gdn_conv", [2, 3, GDN_CONV])
    W = {}
    for nm, shp in [("ssd_norm_w", [2, D]), ("ssd_w_in", [2, D, SSD_IN]), ("ssd_conv_w", [2, 4, SSD_CONV]),
                    ("ssd_conv_b", [2, SSD_CONV]), ("ssd_dt_bias", [2, SSD_H]), ("ssd_a_log", [2, SSD_H]),
                    ("ssd_d", [2, SSD_H]), ("ssd_gnorm_w", [2, SSD_INNER]), ("ssd_w_out", [2, SSD_INNER, D]),
                    ("gdn_norm_w", [2, D]), ("gdn_w_in", [2, D, GDN_IN]), ("gdn_conv_w", [2, 4, GDN_CONV]),
                    ("gdn_dt_bias", [2, GDN_H]), ("gdn_a_log", [2, GDN_H]), ("gdn_onorm_w", [2, GDN_DV]),
                    ("gdn_w_out", [2, GDN_VAL, D]), ("final_norm_w", [D])]:
        W[nm] = inp(nm, shp)

    y_p = outp("y_prompt", [SEQ, D])
    y_s = outp("y_sample", [16, D])
    o_pssd = outp("p_ssd", [2, SSD_H, SSD_P, SSD_N])
    o_pssdc = outp("p_ssdc", [2, 3, SSD_CONV])
    o_pgdn = outp("p_gdn", [2, GDN_H, GDN_DK, GDN_DV])
    o_pgdnc = outp("p_gdnc", [2, 3, GDN_CONV])
    o_sssd = outp("s_ssd", [2, SSD_H, SSD_P, SSD_N])
    o_sssdc = outp("s_ssdc", [2, 3, SSD_CONV])
    o_sgdn = outp("s_gdn", [2, GDN_H, GDN_DK, GDN_DV])
    o_sgdnc = outp("s_gdnc", [2, 3, GDN_CONV])

    hbuf = [nc.dram_tensor(f"hbuf{i}", [128, 8, TT], F32, kind="Internal").ap() for i in range(2)]
    dbg_o = outp("dbg_o", [128, 1024]) if dbg else None

    with ExitStack() as es0:
        S = Sched(nc, es0)

        uid = [0]

        def mk(es):
            def sb(name, shape, dt=F32):
                uid[0] += 1
                return es.enter_context(nc.sbuf_tensor(f"{name}_u{uid[0]}", list(shape), dt))
            return sb

        sb0 = mk(es0)

        def nfree(ap):
            dims = [list(d) for d in ap.ap][1:]
            dims = [d for d in dims if d[1] != 1]
            merged = []
            for d in dims:
                if merged and merged[-1][0] == d[0] * d[1]:
                    merged[-1] = [d[0], merged[-1][1] * d[1]]
                else:
                    merged.append(d)
            return len(merged)

        def MM(out, lhsT, rhs, start, stop, r, w):
            assert nfree(rhs) <= 1 and nfree(lhsT) <= 1, (nfree(rhs), nfree(lhsT), rhs, lhsT)
            S.op("pe", lambda e: e.matmul(out, lhsT=lhsT, rhs=rhs, start=start, stop=stop), reads=r, writes=w)

        def TR(out, in_, idn, r, w):
            assert nfree(in_) <= 1 and nfree(idn) <= 1, (nfree(in_), in_)
            S.op("pe", lambda e: e.transpose(out, in_, idn), reads=r, writes=w)

        def ACT(out, in_, func, r, w, scale=None, bias=None):
            kw = {}
            if scale is not None:
                kw["scale"] = scale
            if bias is not None:
                kw["bias"] = bias
            S.op("act", lambda e: e.activation(out=out, in_=in_, func=func, **kw), reads=r, writes=w)

        def TT(eng, out, in0, in1, op, r, w):
            S.op(eng, lambda e: e.tensor_tensor(out=out, in0=in0, in1=in1, op=op), reads=r, writes=w)

        def TS(eng, out, in0, s1, op0, r, w, s2=None, op1=None):
            if op1 is None:
                S.op(eng, lambda e: e.tensor_scalar(out=out, in0=in0, scalar1=s1, scalar2=None, op0=op0), reads=r, writes=w)
            else:
                S.op(eng, lambda e: e.tensor_scalar(out=out, in0=in0, scalar1=s1, scalar2=s2, op0=op0, op1=op1),
                     reads=r, writes=w)

        def STT(out, in0, scalar, in1, op0, op1, r, w):
            S.op("dve", lambda e: e.scalar_tensor_tensor(out=out, in0=in0, scalar=scalar, in1=in1, op0=op0, op1=op1),
                 reads=r, writes=w)

        def CP(eng, out, in_, r, w):
            if eng == "act":
                ACT(out, in_, AF.Copy, r, w)
            else:
                S.op(eng, lambda e: e.tensor_copy(out=out, in_=in_), reads=r, writes=w)

        def MEMSET(eng, ap, val, w):
            S.op(eng, lambda e: e.memset(ap, val), writes=w)

        ident = sb0("ident", [128, 128])
        ident_bf = sb0("ident_bf", [128, 128], BF16)
        ones_bf = sb0("ones_bf", [128, 128], BF16)
        ones_f = sb0("ones_f", [128, 128])
        tri = sb0("tri", [128, 128])
        incl = sb0("incl", [128, 128])
        mneg_f = sb0("mneg_f", [128, 512])
        mnegL = {128: sb0("mneg128", [128, 512], BF16), 16: sb0("mneg16", [128, 64], BF16)}
        mnegsL = {128: sb0("mnegs128", [128, 512], BF16), 16: sb0("mnegs16", [128, 64], BF16)}
        epsb = sb0("epsb", [128, 1])
        oneb = sb0("oneb", [128, 1])
        fnw = sb0("fnw", [128, 8])
        CONST = ["const"]
        MEMSET("pool", ident[:], 1.0, CONST)
        S.op("pool", lambda e: e.affine_select(out=ident[:], in_=ident[:], pattern=[[1, 128]], compare_op=ALU.is_equal,
                                               fill=0.0, base=0, channel_multiplier=-1), reads=CONST, writes=CONST)
        CP("pool", ident_bf[:], ident[:], CONST, CONST)
        MEMSET("pool", ones_bf[:], 1.0, CONST)
        MEMSET("pool", ones_f[:], 1.0, CONST)
        MEMSET("pool", tri[:], 1.0, CONST)
        S.op("pool", lambda e: e.affine_select(out=tri[:], in_=tri[:], pattern=[[-1, 128]], compare_op=ALU.is_ge,
                                               fill=0.0, base=-1, channel_multiplier=1), reads=CONST, writes=CONST)
        MEMSET("pool", incl[:], 1.0, CONST)
        S.op("pool", lambda e: e.affine_select(out=incl[:], in_=incl[:], pattern=[[1, 128]], compare_op=ALU.is_ge,
                                               fill=0.0, base=0, channel_multiplier=-1), reads=CONST, writes=CONST)
        for LL in (128, 16):
            for strict, dst in ((False, mnegL[LL]), (True, mnegsL[LL])):
                MEMSET("pool", mneg_f[:, 0:4 * LL], 0.0, CONST)
                S.op("pool", lambda e: e.affine_select(out=mneg_f[:, 0:4 * LL], in_=mneg_f[:, 0:4 * LL],
                                                       pattern=[[0, 4], [1, LL]],
                                                       compare_op=(ALU.is_gt if strict else ALU.is_ge), fill=NEG, base=0,
                                                       channel_multiplier=-1), reads=CONST, writes=CONST)
                CP("pool", dst[:, :], mneg_f[:, 0:4 * LL], CONST, CONST)
        MEMSET("pool", epsb[:], EPS, CONST)
        MEMSET("pool", oneb[:], 1.0, CONST)
        S.dma("sp", fnw[:], W["final_norm_w"].rearrange("(k p) -> p k", p=128), writes=CONST, slow=True)

        PB = [es0.enter_context(nc.psum_tensor(f"pb{i}", [128, 512], F32)) for i in range(8)]

        def bank3(i, n, L):
            return PB[i][:, 0:n * 128].rearrange("p (k t) -> p k t", k=n)[:, :, :L]

        S.phase = "prologue"
        with ExitStack() as es:
            sb = mk(es)
            xtok = [sb(f"xtok{i}", [128, D]) for i in range(2)]
            htile = [sb(f"htile{i}", [128, 8, 128]) for i in range(2)]
            for ci, (stream, off, L) in enumerate(chunks):
                b = ci % 2
                if stream == "s":
                    src = xs[:, :]
                elif off == 0:
                    src = meta[:, :]
                else:
                    src = xp[off - 16: off - 16 + L, :]
                S.dma("sp", xtok[b][:L, :], src, writes=[f"xtok{b}"])
                for kt in range(8):
                    bk = 2 * b + kt // 4
                    TR(PB[bk][:, (kt % 4) * 128:(kt % 4) * 128 + L], xtok[b][:L, kt * 128:(kt + 1) * 128], ident[:L, :L],
                       [f"xtok{b}", "const"], [f"pb{bk}"])
                for half in range(2):
                    bk = 2 * b + half
                    CP("act" if half else "dve", htile[b][:, 4 * half:4 * half + 4, :L], bank3(bk, 4, L),
                       [f"pb{bk}"], [f"htile{b}"])
                S.dma("sp", hbuf[0][:, :, off:off + L], htile[b][:, :, :L], reads=[f"htile{b}"], writes=[f"h0_{ci}"])
        S.barrier()

        def front(sb_t, hsrc, ci, off, L, normw):
            S.phase = "front"
            ht, sq, rstd, xn = sb_t
            S.dma("sp", ht[:, :, :L], hbuf[hsrc][:, :, off:off + L], reads=[f"h{hsrc}_{ci}"], writes=["ht"])
            TT("pool", sq[:, :, :L], ht[:, :, :L], ht[:, :, :L], ALU.mult, ["ht"], ["sq"])
            for kt in range(8):
                MM(PB[0][:, :L], ones_bf[:, :], sq[:, kt, :L], kt == 0, kt == 7, ["sq", "const"], ["pb0"])
            ACT(rstd[:, :L], PB[0][:, :L], AF.Ln, ["pb0", "const"], ["rstd"], scale=1.0 / D, bias=epsb[:, 0:1])
            ACT(rstd[:, :L], rstd[:, :L], AF.Exp, ["rstd"], ["rstd"], scale=-0.5)
            for kt in range(8):
                STT(xn[:, kt, :L], ht[:, kt, :L], normw[:, kt:kt + 1], rstd[:, :L], ALU.mult, ALU.mult,
                    ["ht", "rstd", "lw"], ["xn"])

        def proj_conv_b(Win, xn, tiles, col0, L, tmp, halo, convw, convb, dst_of, banks=(1, 2)):
            nb = len(tiles) // 4
            skew = len(tmp) >= 8

            def tslot(bi, sl):
                i = ((bi % 2) * 4 + sl) if skew else sl
                return tmp[i], f"ctmp{i}"

            def stA(bi):
                bk = banks[bi % 2]
                for sl in range(4):
                    ct = tiles[bi * 4 + sl]
                    c0 = col0 + ct * 128
                    for kt in range(8):
                        MM(PB[bk][:, sl * 128:sl * 128 + L], Win[:, kt, c0:c0 + 128], xn[:, kt, :L], kt == 0, kt == 7,
                           ["xn", f"Win{kt}"], [f"pb{bk}"])
                for sl in range(4):
                    t, tk = tslot(bi, sl)
                    CP("act", t[:, 3:3 + L], PB[bk][:, sl * 128:sl * 128 + L], [f"pb{bk}"], [tk])
                for sl in range(4):
                    ct = tiles[bi * 4 + sl]
                    t, tk = tslot(bi, sl)
                    CP("pool", t[:, 0:3], halo[:, :, ct], ["halo"], [tk])

            def stB(bi):
                for k in range(4):
                    for sl in range(4):
                        ct = tiles[bi * 4 + sl]
                        t, tk = tslot(bi, sl)
                        acc = t[:, 136:136 + L]
                        if k == 0:
                            if convb is not None:
                                TS("dve", acc, t[:, 0:L], convw[:, ct, 0:1], ALU.mult, [tk, "lw"], [tk],
                                   s2=convb[:, ct:ct + 1], op1=ALU.add)
                            else:
                                TS("dve", acc, t[:, 0:L], convw[:, ct, 0:1], ALU.mult, [tk, "lw"], [tk])
                        else:
                            STT(acc, t[:, k:k + L], convw[:, ct, k:k + 1], acc, ALU.mult, ALU.add, [tk, "lw"], [tk])
                for sl in range(4):
                    ct = tiles[bi * 4 + sl]
                    t, tk = tslot(bi, sl)
                    CP("pool", halo[:, :, ct], t[:, L:L + 3], [tk], ["halo"])

            def stC(bi):
                for sl in range(4):
                    ct = tiles[bi * 4 + sl]
                    t, tk = tslot(bi, sl)
                    out_ap, key = dst_of(ct)
                    ACT(out_ap, t[:, 136:136 + L], AF.Silu, [tk], [key])

            if skew:
                stA(0)
                for bi in range(nb):
                    if bi + 1 < nb:
                        stA(bi + 1)
                    stB(bi)
                    stC(bi)
            else:
                for bi in range(nb):
                    stA(bi)
                    stB(bi)
                    stC(bi)

        def ssd_layer(j, hsrc, hdst, last):
            with ExitStack() as es:
                try:
                    ssd_body(mk(es), j, hsrc, hdst, last)
                except _Stop:
                    S.stop_at = None
                    S.stopped = True
            S.barrier()

        def ssd_body(sb, j, hsrc, hdst, last):
            if True:
                Win = sb("Win", [128, 8, SSD_IN], BF16)
                Wout = sb("Wout", [128, 16, D], BF16)
                normw = sb("normw", [128, 8])
                convw = sb("convw", [128, 24, 4])
                convb = sb("convb", [128, 24])
                dtb = sb("dtb", [32, 1])
                aneg = sb("aneg", [32, 1])
                dsk = sb("dsk", [128, 16])
                gnw = sb("gnw", [128, 16])
                St = sb("St", [128, 2048])
                Sbf = sb("Sbf", [128, 2048], BF16)
                halo = sb("halo", [128, 3, 24])
                ht = sb("ht", [128, 8, 128])
                sq = sb("sq", [128, 16, 128], BF16)
                rstd = sb("rstd", [128, 4, 128])
                xn = sb("xn", [128, 8, 128], BF16)
                zs = sb("zs", [128, 16, 128])
                ctall = sb("ctall", [128, 8 * 272])
                tmp = [ctall[:, i * 272:(i + 1) * 272] for i in range(8)]
                xcf = sb("xcf", [128, 24, 128])
                bcbf = sb("bcbf", [128, 2, 128], BF16)
                dtT = sb("dtT", [32, 5, 128])
                onesr = sb("onesr", [32, 128])
                tokm = sb("tokm", [128, 5, 32])
                edec = sb("edec", [128, 32])
                xdt = sb("xdt", [128, 512], BF16)
                xde = sb("xde", [128, 512], BF16)
                btok = sb("btok", [128, 128], BF16)
                cbT = sb("cbT", [128, 128])
                Dq = sb("Dq", [128, 1024])
                dec = sb("dec", [128, 1024])
                WT = sb("WT", [128, 1024], BF16)
                ytok = sb("ytok", [128, 512])
                yT = sb("yT", [128, 16, 128])
                ygn = sq
                stg = yT
                hstg = sb("hstg", [72, 128])

                for kt in range(8):
                    S.dma("pool", Win[:, kt, :], W["ssd_w_in"][j, kt * 128:(kt + 1) * 128, :], writes=[f"Win{kt}"])
                for kt in range(16):
                    S.dma("pool", Wout[:, kt, :], W["ssd_w_out"][j, kt * 128:(kt + 1) * 128, :], writes=[f"Wout{kt}"])
                LW = ["lw"]
                S.dma("sp", normw[:], W["ssd_norm_w"][j].rearrange("(k p) -> p k", p=128), writes=LW, slow=True)
                for k in range(4):
                    S.dma("sp", convw[:, :, k], W["ssd_conv_w"][j, k].rearrange("(c p) -> p c", p=128), writes=LW, slow=True)
                S.dma("sp", convb[:], W["ssd_conv_b"][j].rearrange("(c p) -> p c", p=128), writes=LW, slow=True)
                S.dma("sp", dtb[:], W["ssd_dt_bias"][j].rearrange("(h o) -> h o", o=1), writes=LW, slow=True)
                S.dma("sp", aneg[:], W["ssd_a_log"][j].rearrange("(h o) -> h o", o=1), writes=LW, slow=True)
                S.dma("sp", gnw[:], W["ssd_gnorm_w"][j].rearrange("(c p) -> p c", p=128), writes=LW, slow=True)
                d2 = W["ssd_d"][j].rearrange("(c two) -> two c", two=2)
                for two in range(2):
                    S.dma("sp", dsk[two * 64:(two + 1) * 64, :], d2[two].partition_broadcast(64), writes=LW, slow=True)
                ACT(aneg[:], aneg[:], AF.Exp, LW, LW)
                TS("dve", aneg[:], aneg[:], -1.0, ALU.mult, LW, LW)
                MEMSET("pool", onesr[:], 1.0, LW)
                WinK = [f"Win{kt}" for kt in range(8)]
                WoutK = [f"Wout{kt}" for kt in range(16)]

                def store_state(o_state, o_conv):
                    S.barrier()
                    for c in range(16):
                        bk = 4 + (c // 4) % 2
                        TR(PB[bk][:, (c % 4) * 128:(c % 4 + 1) * 128], St[:, c * 128:(c + 1) * 128], ident[:, :],
                           ["St", "const"], [f"pb{bk}"])
                        if c % 4 == 3:
                            CP("act" if (c // 4) % 2 else "dve", stg[:, c - 3:c + 1, :], bank3(bk, 4, 128), [f"pb{bk}"], ["stg"])
                    S.dma("sp", o_state[j].rearrange("(c two) p n -> (two p) c n", two=2), stg[:, :, :], reads=["stg"],
                          writes=["ostate"])
                    TR(PB[6][:72, :128], halo[:, :, :].rearrange("p k c -> p (k c)"), ident[:, :], ["halo", "const"], ["pb6"])
                    CP("dve", hstg[:, :], PB[6][:72, :128], ["pb6"], ["hstg"])
                    for k in range(3):
                        S.dma("sp", o_conv[j, k].rearrange("(c p) -> c p", p=128), hstg[k * 24:(k + 1) * 24, :],
                              reads=["hstg"], writes=["oconv"])
                    S.barrier()

                prev_stream = None
                for ci, (stream, off, L) in enumerate(chunks):
                    if stream != prev_stream:
                        if prev_stream == "s":
                            store_state(o_sssd, o_sssdc)
                        if stream == "s":
                            S.dma("sp", stg[:, :, :], st_ssd[j].rearrange("(c two) p n -> (two p) c n", two=2),
                                  reads=[], writes=["stg"])
                            for c in range(16):
                                bk = 4 + (c // 4) % 2
                                TR(PB[bk][:, (c % 4) * 128:(c % 4 + 1) * 128], stg[:, c, :], ident[:, :],
                                   ["stg", "const"], [f"pb{bk}"])
                                if c % 4 == 3:
                                    CP("act" if (c // 4) % 2 else "dve", St[:, (c - 3) * 128:(c + 1) * 128], PB[bk][:, :],
                                       [f"pb{bk}"], ["St"])
                            CP("pool", Sbf[:, :], St[:, :], ["St"], ["Sbf"])
                            S.dma("sp", hstg[:, :], st_ssdc[j].rearrange("k (c p) -> (k c) p", p=128), writes=["hstg"])
                            TR(PB[6][:, :72], hstg[:, :], ident[:72, :72], ["hstg", "const"], ["pb6"])
                            CP("dve", halo[:, :, :], PB[6][:, :72].rearrange("p (k c) -> p k c", k=3), ["pb6"], ["halo"])
                            S.barrier()
                        else:
                            MEMSET("pool", St[:, :], 0.0, ["St"])
                            MEMSET("pool", Sbf[:, :], 0.0, ["Sbf"])
                            MEMSET("pool", halo[:, :, :], 0.0, ["halo"])
                        prev_stream = stream

                    front((ht, sq[:, 0:8, :], rstd[:, 0, :], xn), hsrc, ci, off, L, normw)

                    S.phase = "ssd_inproj"
                    for kt in range(8):
                        MM(PB[3][:32, :L], Win[:, kt, 5120:5152], xn[:, kt, :L], kt == 0, kt == 7, ["xn", f"Win{kt}"], ["pb3"])
                    ACT(dtT[:, 0, :L], PB[3][:32, :L], AF.Exp, ["pb3", "lw"], ["dtT"], bias=dtb[:, 0:1])
                    ACT(dtT[:, 0, :L], dtT[:, 0, :L], AF.Ln, ["dtT", "const"], ["dtT"], bias=oneb[:32, 0:1])
                    TS("dve", dtT[:, 1, :L], dtT[:, 0, :L], aneg[:, 0:1], ALU.mult, ["dtT", "lw"], ["dtT"])
                    S.op("dve", lambda e: e.tensor_tensor_scan(out=dtT[:, 2, :L], data0=onesr[:, :L], data1=dtT[:, 1, :L],
                                                               initial=0.0, op0=ALU.mult, op1=ALU.add),
                         reads=["dtT", "lw"], writes=["dtT"])
                    ACT(dtT[:, 3, :L], dtT[:, 2, :L], AF.Exp, ["dtT"], ["dtT"], scale=-1.0, bias=dtT[:, 2, L - 1:L])
                    TT("dve", dtT[:, 3, :L], dtT[:, 3, :L], dtT[:, 0, :L], ALU.mult, ["dtT"], ["dtT"])
                    ACT(dtT[:, 4, :L], dtT[:, 2, :L], AF.Exp, ["dtT"], ["dtT"])
                    for q in range(5):
                        TR(PB[3][:L, 64 + q * 32:64 + (q + 1) * 32], dtT[:, q, :L], ident[:32, :32], ["dtT", "const"], ["pb3"])
                    CP("dve", tokm[:L, :, :], PB[3][:L, 64:224].rearrange("p (q h) -> p q h", q=5), ["pb3"], ["tokm"])
                    MM(PB[3][:, 256:288], ones_f[:L, :], tokm[:L, 1, :], True, True, ["tokm", "const"], ["pb3"])
                    ACT(edec[:, :], PB[3][:, 256:288], AF.Exp, ["pb3"], ["edec"])

                    order = list(range(16, 24)) + list(range(0, 16))
                    proj_conv_b(Win, xn, order, 2048, L, tmp, halo, convw, convb, lambda ct: (xcf[:, ct, :L], f"xcf{ct}"))
                    for bi in range(4):
                        bk = 1 + bi % 2
                        for sl in range(4):
                            zt = bi * 4 + sl
                            for kt in range(8):
                                MM(PB[bk][:, sl * 128:sl * 128 + L], Win[:, kt, zt * 128:(zt + 1) * 128], xn[:, kt, :L],
                                   kt == 0, kt == 7, ["xn", f"Win{kt}"], [f"pb{bk}"])
                        ACT(zs[:, bi * 4:bi * 4 + 4, :L], bank3(bk, 4, L), AF.Silu, [f"pb{bk}"], [f"zs{bi}"])

                    S.phase = "ssd_scan"
                    for g in range(4):
                        hs = slice(8 * g, 8 * g + 8)
                        xk = [f"xcf{4 * g + i}" for i in range(4)]
                        CP("act", bcbf[:, 0, :L], xcf[:, 16 + g, :L], [f"xcf{16 + g}"], ["bcbf"])
                        CP("act", bcbf[:, 1, :L], xcf[:, 20 + g, :L], [f"xcf{20 + g}"], ["bcbf"])
                        for i in range(4):
                            TR(PB[4][:L, i * 128:(i + 1) * 128], xcf[:, 4 * g + i, :L], ident[:, :], [xk[i], "const"], ["pb4"])
                        TR(PB[5][:L, 0:128], xcf[:, 16 + g, :L], ident[:, :], [f"xcf{16 + g}", "const"], ["pb5"])
                        x3 = PB[4][:L, :].rearrange("p (h q) -> p h q", h=8)
                        TT("dve", xdt[:L, :].rearrange("p (h q) -> p h q", h=8), x3,
                           tokm[:L, 0, hs].unsqueeze(2).broadcast_to([L, 8, 64]), ALU.mult, ["pb4", "tokm"], ["xdt"])
                        TT("dve", xde[:L, :].rearrange("p (h q) -> p h q", h=8), x3,
                           tokm[:L, 3, hs].unsqueeze(2).broadcast_to([L, 8, 64]), ALU.mult, ["pb4", "tokm"], ["xde"])
                        CP("act", btok[:L, :], PB[5][:L, 0:128], ["pb5"], ["btok"])
                        MM(PB[5][:L, 128:128 + L], bcbf[:, 0, :L], bcbf[:, 1, :L], True, True, ["bcbf"], ["pb5"])
                        CP("act", cbT[:L, :L], PB[5][:L, 128:128 + L], ["pb5"], ["cbT"])
                        Dq3 = Dq[:L, 0:8 * L].rearrange("p (h t) -> p h t", h=8)
                        dec3 = dec[:L, 0:8 * L].rearrange("p (h t) -> p h t", h=8)
                        WT3 = WT[:L, 0:8 * L].rearrange("p (h t) -> p h t", h=8)
                        TT("pool", Dq3, incl[:L, :L].unsqueeze(1).broadcast_to([L, 8, L]),
                           tokm[:L, 1, hs].unsqueeze(2).broadcast_to([L, 8, L]), ALU.mult, ["const", "tokm"], ["Dq"])
                        for hf in range(2):
                            bk = 6 + hf
                            MM(PB[bk][:L, 0:4 * L], tri[:L, :L], Dq[:L, 4 * hf * L:(4 * hf + 4) * L], True, False,
                               ["Dq", "const"], [f"pb{bk}"])
                            MM(PB[bk][:L, 0:4 * L], ident_bf[:L, :L], mnegL[L][:L, :], False, True, ["const"], [f"pb{bk}"])
                            ACT(dec[:L, 4 * hf * L:(4 * hf + 4) * L], PB[bk][:L, 0:4 * L], AF.Exp, [f"pb{bk}"], ["dec"])
                        TT("dve", WT3, dec3, cbT[:L, :L].unsqueeze(1).broadcast_to([L, 8, L]), ALU.mult,
                           ["dec", "cbT"], ["WT"])
                        MM(PB[1][:L, :], bcbf[:, 1, :L], Sbf[:, g * 512:(g + 1) * 512], True, True, ["bcbf", f"Sbf{g}"], ["pb1"])
                        for h in range(8):
                            MM(PB[2][:L, h * 64:(h + 1) * 64], WT[:L, h * L:(h + 1) * L], xdt[:L, h * 64:(h + 1) * 64], True, True,
                               ["WT", "xdt"], ["pb2"])
                        TT("dve", ytok[:L, :].rearrange("p (h q) -> p h q", h=8), PB[1][:L, :].rearrange("p (h q) -> p h q", h=8),
                           tokm[:L, 4, hs].unsqueeze(2).broadcast_to([L, 8, 64]), ALU.mult, ["pb1", "tokm"], ["ytok"])
                        TT("dve", ytok[:L, :], ytok[:L, :], PB[2][:L, :], ALU.add, ["pb2", "ytok"], ["ytok"])
                        MM(PB[3][:, :], btok[:L, :], xde[:L, :], True, True, ["btok", "xde"], ["pb3"])
                        Sg = St[:, g * 512:(g + 1) * 512]
                        TT("pool", Sg.rearrange("p (h q) -> p h q", h=8), Sg.rearrange("p (h q) -> p h q", h=8),
                           edec[:, hs].unsqueeze(2).broadcast_to([128, 8, 64]), ALU.mult, [f"St{g}", "edec"], [f"St{g}"])
                        TT("dve", Sg, Sg, PB[3][:, :], ALU.add, [f"St{g}", "pb3"], [f"St{g}"])
                        CP("act", Sbf[:, g * 512:(g + 1) * 512], Sg, [f"St{g}"], [f"Sbf{g}"])
                        for i in range(4):
                            TR(PB[4][:, i * 128:i * 128 + L], ytok[:L, i * 128:(i + 1) * 128], ident[:L, :L], ["ytok", "const"], ["pb4"])
                        for i in range(4):
                            ct = 4 * g + i
                            STT(yT[:, ct, :L], xcf[:, ct, :L], dsk[:, ct:ct + 1], PB[4][:, i * 128:i * 128 + L], ALU.mult, ALU.add,
                                [xk[i], "lw", "pb4"], [f"yT{g}"])
                        TT("pool", yT[:, 4 * g:4 * g + 4, :L], yT[:, 4 * g:4 * g + 4, :L], zs[:, 4 * g:4 * g + 4, :L], ALU.mult,
                           [f"yT{g}", f"zs{g}"], [f"yT{g}"])
                        TT("pool", sq[:, 4 * g:4 * g + 4, :L], yT[:, 4 * g:4 * g + 4, :L], yT[:, 4 * g:4 * g + 4, :L], ALU.mult,
                           [f"yT{g}"], ["sq"])
                        for i in range(4):
                            MM(PB[0][:, g * 128:g * 128 + L], ones_bf[:, :], sq[:, 4 * g + i, :L], i == 0, i == 3,
                               ["sq", "const"], ["pb0"])
                    S.phase = "ssd_out"
                    ACT(rstd[:, :, :L], bank3(0, 4, L), AF.Ln, ["pb0", "const"], ["rstd"], scale=1.0 / 512, bias=epsb[:, 0:1])
                    ACT(rstd[:, :, :L], rstd[:, :, :L], AF.Exp, ["rstd"], ["rstd"], scale=-0.5)
                    for ct in range(16):
                        STT(ygn[:, ct, :L], yT[:, ct, :L], gnw[:, ct:ct + 1], rstd[:, ct // 4, :L], ALU.mult, ALU.mult,
                            [f"yT{ct // 4}", "rstd", "lw"], ["sq"])
                    for hf in range(2):
                        bk = 1 + hf
                        for sl in range(4):
                            dt_ = hf * 4 + sl
                            for kt in range(16):
                                MM(PB[bk][:, sl * 128:sl * 128 + L], Wout[:, kt, dt_ * 128:(dt_ + 1) * 128], ygn[:, kt, :L],
                                   kt == 0, kt == 15, ["sq", f"Wout{kt}"], [f"pb{bk}"])
                        TT("dve", ht[:, 4 * hf:4 * hf + 4, :L], ht[:, 4 * hf:4 * hf + 4, :L], bank3(bk, 4, L), ALU.add,
                           ["ht", f"pb{bk}"], ["ht"])
                    S.dma("sp", hbuf[hdst][:, :, off:off + L], ht[:, :, :L], reads=["ht"], writes=[f"h{hdst}_{ci}"])
                store_state(o_pssd, o_pssdc)

        def gdn_layer(j, hsrc, hdst, last):
            with ExitStack() as es:
                try:
                    gdn_body(mk(es), j, hsrc, hdst, last)
                except _Stop:
                    S.stop_at = None
                    S.stopped = True
                    S.barrier()
                    dbt = mk(es)("dbt", [128, 256])
                    for _ in range(int(os.environ.get("KDUMMY", "0"))):
                        MEMSET(os.environ.get("KDUMMYENG", "dve"), dbt[:, 0:8], 0.0, ["dbt"])
                    for qq in range(4):
                        bkq = 7 if qq < 2 else 6
                        CP("dve", dbt[:, :], PB[bkq][:, (qq % 2) * 256:(qq % 2) * 256 + 256], [f"pb{bkq}"], ["dbt"])
                        S.dma("sp", dbg_o[:, qq * 256:(qq + 1) * 256], dbt[:, :], reads=["dbt"], writes=["dbgo"])
            S.barrier()

        def gdn_body(sb, j, hsrc, hdst, last):
            if True:
                normw = sb("gnormw", [128, 8])
                convw = sb("gconvw", [128, 32, 4])
                dtb = sb("gdtb", [8, 1])
                aneg = sb("ganeg", [8, 1])
                onw = sb("onw", [128, 2])
                Sg = sb("Sg", [128, 8, 256])
                halo = sb("ghalo", [128, 3, 32])
                ht = sb("ght", [128, 8, 128])
                sq = sb("gsq", [128, 8, 128], BF16)
                rstd = sb("grstd", [128, 8, 128])
                xn = sb("gxn", [128, 8, 128], BF16)
                ctall = sb("gctall", [128, 4 * 272])
                tmp = [ctall[:, i * 272:(i + 1) * 272] for i in range(4)]
                cf = sb("cf", [128, 16, 128])
                qn = sb("qn", [128, 8, 128], BF16)
                kn = sb("kn", [128, 8, 128], BF16)
                qg = sb("qg", [128, 8, 128], BF16)
                kb = sb("kb", [128, 8, 128], BF16)
                gq = sb("gq", [8, 6, 128])
                onesr = sb("gonesr", [8, 128])
                BD = ctall
                tokg = sb("tokg", [128, 4, 8])
                egl = sb("egl", [128, 8])
                kbg = [sb(f"kbg{i}", [128, 128], BF16) for i in range(4)]
                kgt = [sb(f"kgt{i}", [128, 128], BF16) for i in range(4)]
                bv = [sb(f"bv{i}", [128, 256], BF16) for i in range(4)]
                QKd = [sb(f"QKd{i}", [128, 128], BF16) for i in range(4)]
                Sbf = [sb(f"gSbf{i}", [128, 256], BF16) for i in range(4)]
                Dqh = [sb(f"Dqh{i}", [128, 128]) for i in range(2)]
                decI = [sb(f"decI{i}", [128, 128]) for i in range(2)]
                decS = [sb(f"decS{i}", [128, 128]) for i in range(2)]
                negw = [sb(f"negw{i}", [128, 128], BF16) for i in range(2)]
                vnew = [sb(f"vnew{i}", [128, 256], BF16) for i in range(2)]
                Pm = [sb(f"Pm{i}", [128, 512]) for i in range(2)]
                Qm = [sb(f"Qm{i}", [128, 512]) for i in range(2)]
                Rm = sb("Rm", [128, 512])
                TTb = sb("TTb", [128, 512], BF16)
                Win = sb("gWin", [128, 8, GDN_IN], BF16)
                Wout = sb("gWout", [128, 16, D], BF16)
                zsb = [Pm[i][:, :].rearrange("p (k t) -> p k t", k=4) for i in range(2)]
                hstg = Rm[:96, 0:128]
                CTK = [f"ctmp{i}" for i in range(4)]
                ogn = qn
                ogn2 = kn

                for kt in range(8):
                    S.dma("pool", Win[:, kt, :], W["gdn_w_in"][j, kt * 128:(kt + 1) * 128, :], writes=[f"Win{kt}"])
                for kt in range(16):
                    S.dma("pool", Wout[:, kt, :], W["gdn_w_out"][j, kt * 128:(kt + 1) * 128, :], writes=[f"Wout{kt}"])
                LW = ["lw"]
                S.dma("sp", normw[:], W["gdn_norm_w"][j].rearrange("(k p) -> p k", p=128), writes=LW, slow=True)
                for k in range(4):
                    S.dma("sp", convw[:, :, k], W["gdn_conv_w"][j, k].rearrange("(c p) -> p c", p=128), writes=LW, slow=True)
                S.dma("sp", dtb[:], W["gdn_dt_bias"][j].rearrange("(h o) -> h o", o=1), writes=LW, slow=True)
                S.dma("sp", aneg[:], W["gdn_a_log"][j].rearrange("(h o) -> h o", o=1), writes=LW, slow=True)
                S.dma("sp", onw[:], W["gdn_onorm_w"][j].rearrange("(c p) -> p c", p=128), writes=LW, slow=True)
                ACT(aneg[:], aneg[:], AF.Exp, LW, LW)
                TS("dve", aneg[:], aneg[:], -1.0, ALU.mult, LW, LW)
                MEMSET("pool", onesr[:], 1.0, LW)

                def store_state(o_state, o_conv):
                    S.dma("sp", o_state[j].rearrange("h k v -> k h v"), Sg[:, :, :], reads=[f"Sg{h}" for h in range(8)],
                          writes=["ostate"])
                    TR(PB[6][:96, :128], halo[:, :, :].rearrange("p k c -> p (k c)"), ident[:, :], ["halo", "const"], ["pb6"])
                    CP("dve", hstg[:, :], PB[6][:96, :128], ["pb6"], ["Rm"])
                    for k in range(3):
                        S.dma("sp", o_conv[j, k].rearrange("(c p) -> c p", p=128), hstg[k * 32:(k + 1) * 32, :],
                              reads=["Rm"], writes=["oconv"])

                def proj_conv(bk_list, tiles, col0, L, dst_of):
                    for bi in range(len(tiles) // 4):
                        bk = bk_list[bi % 2]
                        for sl in range(4):
                            ct = tiles[bi * 4 + sl]
                            c0 = col0 + ct * 128
                            for kt in range(8):
                                MM(PB[bk][:, sl * 128:sl * 128 + L], Win[:, kt, c0:c0 + 128], xn[:, kt, :L], kt == 0, kt == 7,
                                   ["xn", f"Win{kt}"], [f"pb{bk}"])
                        for sl in range(4):
                            ct = tiles[bi * 4 + sl]
                            out_ap, key = dst_of(ct)
                            conv_tile(bk, sl, L, ct, tmp, halo, convw, None, out_ap, key)

                prev_stream = None
                for ci, (stream, off, L) in enumerate(chunks):
                    NLEV = {128: 6, 16: 3}[L]
                    if stream != prev_stream:
                        if prev_stream == "s":
                            store_state(o_sgdn, o_sgdnc)
                        if stream == "s":
                            S.dma("sp", Sg[:, :, :], st_gdn[j].rearrange("h k v -> k h v"), writes=[f"Sg{h}" for h in range(8)])
                            S.dma("sp", hstg[:, :], st_gdnc[j].rearrange("k (c p) -> (k c) p", p=128), writes=["Rm"])
                            TR(PB[6][:, :96], hstg[:, :], ident[:96, :96], ["Rm", "const"], ["pb6"])
                            CP("dve", halo[:, :, :], PB[6][:, :96].rearrange("p (k c) -> p k c", k=3), ["pb6"], ["halo"])
                        else:
                            MEMSET("pool", Sg[:, :, :], 0.0, [f"Sg{h}" for h in range(8)])
                            MEMSET("pool", halo[:, :, :], 0.0, ["halo"])
                        prev_stream = stream

                    front((ht, sq, rstd[:, 0, :], xn), hsrc, ci, off, L, normw)
                    if dbg and dbg <= 1:
                        break

                    S.phase = "gdn_gates"
                    for q, c0 in ((0, 6144), (1, 6152)):
                        for kt in range(8):
                            MM(PB[3][:8, q * 128:q * 128 + L], Win[:, kt, c0:c0 + 8], xn[:, kt, :L], kt == 0, kt == 7,
                               ["xn", f"Win{kt}"], ["pb3"])
                    ACT(gq[:, 0, :L], PB[3][:8, 0:L], AF.Exp, ["pb3", "lw"], ["gq"], bias=dtb[:, 0:1])
                    ACT(gq[:, 0, :L], gq[:, 0, :L], AF.Ln, ["gq", "const"], ["gq"], bias=oneb[:8, 0:1])
                    TS("dve", gq[:, 0, :L], gq[:, 0, :L], aneg[:, 0:1], ALU.mult, ["gq", "lw"], ["gq"])
                    S.op("dve", lambda e: e.tensor_tensor_scan(out=gq[:, 1, :L], data0=onesr[:, :L], data1=gq[:, 0, :L],
                                                               initial=0.0, op0=ALU.mult, op1=ALU.add),
                         reads=["gq", "lw"], writes=["gq"])
                    ACT(gq[:, 2, :L], PB[3][:8, 128:128 + L], AF.Exp, ["pb3"], ["gq"], scale=-1.0)
                    TS("dve", gq[:, 2, :L], gq[:, 2, :L], 1.0, ALU.add, ["gq"], ["gq"])
                    S.op("dve", lambda e: e.reciprocal(out=gq[:, 2, :L], in_=gq[:, 2, :L]), reads=["gq"], writes=["gq"])
                    ACT(gq[:, 3, :L], gq[:, 1, :L], AF.Exp, ["gq"], ["gq"])
                    TT("dve", gq[:, 4, :L], gq[:, 2, :L], gq[:, 3, :L], ALU.mult, ["gq"], ["gq"])
                    ACT(gq[:, 5, :L], gq[:, 1, :L], AF.Exp, ["gq"], ["gq"], scale=-1.0, bias=gq[:, 1, L - 1:L])
                    for qi, q in enumerate((0, 2, 4, 5)):
                        TR(PB[3][:L, 256 + qi * 8:256 + (qi + 1) * 8], gq[:, q, :L], ident[:8, :8], ["gq", "const"], ["pb3"])
                    CP("dve", tokg[:L, :, :], PB[3][:L, 256:288].rearrange("p (q h) -> p q h", q=4), ["pb3"], ["tokg"])
                    MM(PB[3][:, 320:328], ones_f[:L, :], tokg[:L, 0, :], True, True, ["tokg", "const"], ["pb3"])
                    ACT(egl[:, :], PB[3][:, 320:328], AF.Exp, ["pb3"], ["egl"])

                    if dbg and dbg <= 2:
                        break
                    S.phase = "gdn_qk"
                    proj_conv_b(Win, xn, list(range(16)), 0, L, tmp, halo, convw, None, lambda ct: (cf[:, ct, :L], f"cf{ct}"))
                    for qd in range(4):
                        t0 = qd * 4
                        cfk = [f"cf{t0 + i}" for i in range(4)]
                        TT("pool", sq[:, 0:4, :L], cf[:, t0:t0 + 4, :L], cf[:, t0:t0 + 4, :L], ALU.mult, cfk, ["sq"])
                        for i in range(4):
                            MM(PB[0][:, i * 128:i * 128 + L], ones_bf[:, :], sq[:, i, :L], True, True, ["sq", "const"], ["pb0"])
                        ACT(rstd[:, 0:4, :L], bank3(0, 4, L), AF.Ln, ["pb0", "const"], ["rstd"], bias=epsb[:, 0:1])
                        ACT(rstd[:, 0:4, :L], rstd[:, 0:4, :L], AF.Exp, ["rstd"], ["rstd"], scale=-0.5)
                        if qd < 2:
                            STT(qn[:, t0:t0 + 4, :L], cf[:, t0:t0 + 4, :L], float(GDN_DK ** -0.5), rstd[:, 0:4, :L], ALU.mult, ALU.mult,
                                cfk + ["rstd"], ["qn"])
                        else:
                            TT("dve", kn[:, t0 - 8:t0 - 4, :L], cf[:, t0:t0 + 4, :L], rstd[:, 0:4, :L], ALU.mult, cfk + ["rstd"], ["kn"])
                    if dbg and dbg <= 3:
                        break
                    S.phase = "gdn_rep"
                    for q, src, dst, dkey in ((2, kn, kb, "kb"), (3, qn, qg, "qg")):
                        TT("pool", BD[:8, 0:8 * L].rearrange("p (h t) -> p h t", h=8),
                           gq[:8, q, :L].unsqueeze(1).broadcast_to([8, 8, L]),
                           ident[:8, :8].unsqueeze(2).broadcast_to([8, 8, L]), ALU.mult, ["gq", "const"], CTK)
                        for hf in range(2):
                            MM(PB[4 + hf][:, 0:4 * L], ones_f[:8, :], BD[:8, 4 * hf * L:(4 * hf + 4) * L], True, True,
                               CTK + ["const"], [f"pb{4 + hf}"])
                            TT("dve", dst[:, 4 * hf:4 * hf + 4, :L], src[:, 4 * hf:4 * hf + 4, :L],
                               PB[4 + hf][:, 0:4 * L].rearrange("p (h t) -> p h t", h=4), ALU.mult,
                               [f"pb{4 + hf}", "qn" if q == 3 else "kn"], [dkey])
                    if dbg and dbg <= 4:
                        break
                    S.phase = "gdn_v"
                    proj_conv_b(Win, xn, list(range(16, 32)), 0, L, tmp, halo, convw, None, lambda ct: (cf[:, ct - 16, :L], f"cf{ct - 16}"))

                    if dbg and dbg <= 5:
                        break
                    S.phase = "gdn_phI"
                    for hb in range(2):
                        S.phase = "gdn_phI"
                        for hh in range(4):
                            h = hb * 4 + hh
                            p = hh % 2
                            bA, bB = 4 + p, 6 + p
                            kA, kB = f"pb{bA}", f"pb{bB}"
                            if 'a' not in SKIP:
                                CP("act", Sbf[hh][:, :], Sg[:, h, :], [f"Sg{h}"], [f"Sbf{hh}"])
                            if 'b' not in SKIP:
                                MM(PB[bA][:L, 0:128], kn[:, h, :L], ident_bf[:, :], True, True, ["kn", "const"], [kA])
                            for vt in range(2):
                                TR(PB[bA][:L, 128 + vt * 128:256 + vt * 128], cf[:, 2 * h + vt, :L], ident[:, :],
                                   [f"cf{2 * h + vt}", "const"], [kA])
                            if 'c' not in SKIP:
                                TS("dve", kbg[hh][:L, :], PB[bA][:L, 0:128], tokg[:L, 2, h:h + 1], ALU.mult, [kA, "tokg"], [f"kbg{hh}"])
                            if 'd' not in SKIP:
                                TS("dve", kgt[hh][:L, :], PB[bA][:L, 0:128], tokg[:L, 3, h:h + 1], ALU.mult, [kA, "tokg"], [f"kgt{hh}"])
                            if 'e' not in SKIP:
                                TS("dve", bv[hh][:L, :], PB[bA][:L, 128:384], tokg[:L, 1, h:h + 1], ALU.mult, [kA, "tokg"], [f"bv{hh}"])
                            MM(PB[bB][:L, 0:L], kn[:, h, :L], kb[:, h, :L], True, True, ["kn", "kb"], [kB])
                            MM(PB[bB][:L, 128:128 + L], kn[:, h, :L], qn[:, h, :L], True, True, ["kn", "qn"], [kB])
                            if 'f' not in SKIP:
                                TS("dve", Dqh[p][:L, :L], incl[:L, :L], tokg[:L, 0, h:h + 1], ALU.mult, ["const", "tokg"], [f"Dqh{p}"])
                            MM(PB[bB][:L, 256:256 + L], tri[:L, :L], Dqh[p][:L, :L], True, False, [f"Dqh{p}", "const"], [kB])
                            MM(PB[bB][:L, 256:256 + L], ident_bf[:L, :L], mnegL[L][:L, 0:L], False, True, ["const"], [kB])
                            MM(PB[bB][:L, 384:384 + L], tri[:L, :L], Dqh[p][:L, :L], True, False, [f"Dqh{p}", "const"], [kB])
                            MM(PB[bB][:L, 384:384 + L], ident_bf[:L, :L], mnegsL[L][:L, 0:L], False, True, ["const"], [kB])
                            ACT(decI[p][:L, :L], PB[bB][:L, 256:256 + L], AF.Exp, [kB], [f"decI{p}"])
                            ACT(decS[p][:L, :L], PB[bB][:L, 384:384 + L], AF.Exp, [kB], [f"decS{p}"])
                            TT("dve", QKd[hh][:L, :L], PB[bB][:L, 128:128 + L], decI[p][:L, :L], ALU.mult, [kB, f"decI{p}"], [f"QKd{hh}"])
                            STT(Pm[0][:L, hh * L:(hh + 1) * L], PB[bB][:L, 0:L], -1.0, decS[p][:L, :L], ALU.mult, ALU.mult,
                                [kB, f"decS{p}"], ["Pm0"])
                        if dbg and dbg <= 6:
                            break
                        S.phase = "gdn_neu"
                        for hh in range(4):
                            sl = slice(hh * L, (hh + 1) * L)
                            TR(PB[1][:L, sl], Pm[0][:L, sl], ident[:L, :L], ["Pm0", "const"], ["pb1"])
                        CP("act", Qm[0][:L, 0:4 * L], PB[1][:L, 0:4 * L], ["pb1"], ["Qm0"])
                        i4 = ident[:L, :L].unsqueeze(1).broadcast_to([L, 4, L])
                        TT("pool", Rm[:L, 0:4 * L].rearrange("p (h t) -> p h t", h=4),
                           Pm[0][:L, 0:4 * L].rearrange("p (h t) -> p h t", h=4), i4, ALU.add, ["Pm0", "const"], ["Rm"])
                        for lev in range(1, NLEV + 1):
                            a, b = (lev - 1) % 2, lev % 2
                            lastlev = lev == NLEV
                            for hh in range(4):
                                sl = slice(hh * L, (hh + 1) * L)
                                MM(PB[1][:L, sl], Pm[a][:L, sl], Qm[a][:L, sl], True, True, [f"Qm{a}", f"Pm{a}"], ["pb1"])
                            CP("act", Qm[b][:L, 0:4 * L], PB[1][:L, 0:4 * L], ["pb1"], [f"Qm{b}"])
                            if not lastlev:
                                for hh in range(4):
                                    sl = slice(hh * L, (hh + 1) * L)
                                    MM(PB[0][:L, sl], Qm[a][:L, sl], Pm[a][:L, sl], True, True, [f"Qm{a}", f"Pm{a}"], ["pb0"])
                                CP("act", Pm[b][:L, 0:4 * L], PB[0][:L, 0:4 * L], ["pb0"], [f"Pm{b}"])
                            for hh in range(4):
                                sl = slice(hh * L, (hh + 1) * L)
                                MM(PB[2 + lev % 2][:L, sl], Qm[b][:L, sl], Rm[:L, sl], True, True, ["Rm", f"Qm{b}"], [f"pb{2 + lev % 2}"])
                            TT("dve", Rm[:L, 0:4 * L], Rm[:L, 0:4 * L], PB[2 + lev % 2][:L, 0:4 * L], ALU.add,
                               ["Rm", f"pb{2 + lev % 2}"], ["Rm"])
                        CP("act", TTb[:L, 0:4 * L], Rm[:L, 0:4 * L], ["Rm"], ["TTb"])
                        if dbg and dbg <= 7:
                            break
                        S.phase = "gdn_phIII"
                        for hh in range(4):
                            h = hb * 4 + hh
                            p = hh % 2
                            bA, bB = 4 + p, 6 + p
                            kA, kB = f"pb{bA}", f"pb{bB}"
                            sl = slice(hh * L, (hh + 1) * L)
                            MM(PB[bA][:, 0:L], kbg[hh][:L, :], TTb[:L, sl], True, True, [f"kbg{hh}", "TTb"], [kA])
                            ACT(negw[p][:, :L], PB[bA][:, 0:L], AF.Copy, [kA], [f"negw{p}"], scale=-1.0)
                            MM(PB[bA][:L, 128:384], TTb[:L, sl], bv[hh][:L, :], True, False, [f"bv{hh}", "TTb"], [kA])
                            MM(PB[bA][:L, 128:384], negw[p][:, :L], Sbf[hh][:, :], False, True, [f"negw{p}", f"Sbf{hh}"], [kA])
                            CP("dve", vnew[p][:L, :], PB[bA][:L, 128:384], [kA], [f"vnew{p}"])
                            for vt in range(2):
                                MM(PB[bB][:, vt * 128:vt * 128 + L], Sbf[hh][:, vt * 128:(vt + 1) * 128], qg[:, h, :L], True, False,
                                   [f"Sbf{hh}", "qg"], [kB])
                                MM(PB[bB][:, vt * 128:vt * 128 + L], vnew[p][:L, vt * 128:(vt + 1) * 128], QKd[hh][:L, :L], False, True,
                                   [f"vnew{p}", f"QKd{hh}"], [kB])
                            CP("act", cf[:, 2 * h:2 * h + 2, :L], bank3(bB, 2, L), [kB], [f"cf{2 * h}", f"cf{2 * h + 1}"])
                            MM(PB[bB][:, 256:512], kgt[hh][:L, :], vnew[p][:L, :], True, True, [f"kgt{hh}", f"vnew{p}"], [kB])
                            STT(Sg[:, h, :], Sg[:, h, :], egl[:, h:h + 1], PB[bB][:, 256:512], ALU.mult, ALU.add,
                                [f"Sg{h}", "egl", kB], [f"Sg{h}"])

                    if dbg and dbg <= 8:
                        break
                    S.phase = "gdn_out"
                    cfall = [f"cf{t}" for t in range(16)]
                    for hf in range(2):
                        TT("pool", sq[:, :, :L], cf[:, 8 * hf:8 * hf + 8, :L], cf[:, 8 * hf:8 * hf + 8, :L], ALU.mult, cfall, ["sq"])
                        bk = 0 if hf == 0 else 3
                        for hh in range(4):
                            for vt in range(2):
                                MM(PB[bk][:, hh * 128:hh * 128 + L], ones_bf[:, :], sq[:, 2 * hh + vt, :L], vt == 0, vt == 1,
                                   ["sq", "const"], [f"pb{bk}"])
                        ACT(rstd[:, 4 * hf:4 * hf + 4, :L], bank3(bk, 4, L), AF.Ln, [f"pb{bk}", "const"], ["rstd"],
                            scale=1.0 / GDN_DV, bias=epsb[:, 0:1])
                    ACT(rstd[:, :, :L], rstd[:, :, :L], AF.Exp, ["rstd"], ["rstd"], scale=-0.5)
                    for bi in range(4):
                        bk = 1 + bi % 2
                        zb = zsb[bi % 2]
                        for sl_ in range(4):
                            zt = bi * 4 + sl_
                            c0 = 4096 + zt * 128
                            for kt in range(8):
                                MM(PB[bk][:, sl_ * 128:sl_ * 128 + L], Win[:, kt, c0:c0 + 128], xn[:, kt, :L], kt == 0, kt == 7,
                                   ["xn", f"Win{kt}"], [f"pb{bk}"])
                        ACT(zb[:, :, :L], bank3(bk, 4, L), AF.Silu, [f"pb{bk}"], [f"Pm{bi % 2}"])
                        for sl_ in range(4):
                            zt = bi * 4 + sl_
                            STT(cf[:, zt, :L], cf[:, zt, :L], onw[:, zt % 2:zt % 2 + 1], rstd[:, zt // 2, :L], ALU.mult, ALU.mult,
                                [f"cf{zt}", "rstd", "lw"], [f"cf{zt}"])
                        dsto = (ogn if bi < 2 else ogn2)[:, (bi % 2) * 4:(bi % 2) * 4 + 4, :L]
                        TT("dve", dsto, cf[:, bi * 4:bi * 4 + 4, :L], zb[:, :, :L], ALU.mult,
                           [f"cf{bi * 4 + i}" for i in range(4)] + [f"Pm{bi % 2}"], ["qn" if bi < 2 else "kn"])
                    for hf in range(2):
                        bk = 1 + hf
                        for sl_ in range(4):
                            dt_ = hf * 4 + sl_
                            for kt in range(16):
                                src_o = (ogn if kt < 8 else ogn2)[:, kt % 8, :L]
                                MM(PB[bk][:, sl_ * 128:sl_ * 128 + L], Wout[:, kt, dt_ * 128:(dt_ + 1) * 128], src_o,
                                   kt == 0, kt == 15, ["qn", "kn", f"Wout{kt}"], [f"pb{bk}"])
                        TT("dve", ht[:, 4 * hf:4 * hf + 4, :L], ht[:, 4 * hf:4 * hf + 4, :L], bank3(bk, 4, L), ALU.add,
                           ["ht", f"pb{bk}"], ["ht"])
                    S.dma("sp", hbuf[hdst][:, :, off:off + L], ht[:, :, :L], reads=["ht"], writes=[f"h{hdst}_{ci}"])
                store_state(o_pgdn, o_pgdnc)

        cur = 0
        if dbg and dbg > 100:
            S.stop_at = dbg
        S.stopped = False
        for li in range(NL):
            if li % 2 == 0:
                ssd_layer(li // 2, cur, 1 - cur, li == NL - 1)
            else:
                gdn_layer(li // 2, cur, 1 - cur, li == NL - 1)
            cur = 1 - cur
            if S.stopped:
                print("stopped at", S.n_inst)
                break

        S.phase = "epilogue"
        with ExitStack() as es:
            sb = mk(es)
            htile = [sb(f"htile{i}", [128, 8, 128]) for i in range(2)]
            sq = [sb(f"sq{i}", [128, 8, 128], BF16) for i in range(2)]
            rstd = [sb(f"rstd{i}", [128, 128]) for i in range(2)]
            ytile = [sb(f"ytile{i}", [128, 8, 128]) for i in range(2)]
            ytok = [sb(f"ytok{i}", [128, D]) for i in range(2)]
            for ci, (stream, off, L) in enumerate(chunks):
                if stream == "p" and off == 0:
                    continue
                b = ci % 2
                S.dma("sp", htile[b][:, :, :L], hbuf[cur][:, :, off:off + L], reads=[f"h{cur}_{ci}"], writes=[f"htile{b}"])
                TT("pool", sq[b][:, :, :L], htile[b][:, :, :L], htile[b][:, :, :L], ALU.mult, [f"htile{b}"], [f"sq{b}"])
                bk = 4 + b
                for kt in range(8):
                    MM(PB[bk][:, :L], ones_bf[:, :], sq[b][:, kt, :L], kt == 0, kt == 7, [f"sq{b}", "const"], [f"pb{bk}"])
                ACT(rstd[b][:, :L], PB[bk][:, :L], AF.Ln, [f"pb{bk}", "const"], [f"rstd{b}"], scale=1.0 / D, bias=epsb[:, 0:1])
                ACT(rstd[b][:, :L], rstd[b][:, :L], AF.Exp, [f"rstd{b}"], [f"rstd{b}"], scale=-0.5)
                for kt in range(8):
                    STT(ytile[b][:, kt, :L], htile[b][:, kt, :L], fnw[:, kt:kt + 1], rstd[b][:, :L], ALU.mult, ALU.mult,
                        [f"htile{b}", "const", f"rstd{b}"], [f"ytile{b}"])
                for kt in range(8):
                    bk2 = 2 * b + kt // 4
                    TR(PB[bk2][:L, (kt % 4) * 128:(kt % 4 + 1) * 128], ytile[b][:, kt, :L], ident[:, :],
                       [f"ytile{b}", "const"], [f"pb{bk2}"])
                for half in range(2):
                    bk2 = 2 * b + half
                    CP("act" if half else "dve", ytok[b][:L, 512 * half:512 * half + 512], PB[bk2][:L, :], [f"pb{bk2}"], [f"ytok{b}"])
                dst = y_s[:, :] if stream == "s" else y_p[off - 16: off - 16 + L, :]
                S.dma("sp", dst, ytok[b][:L, :], reads=[f"ytok{b}"], writes=[f"yout{ci}"])
        S.finish("sp")
        print("instructions:", S.n_inst, "sems:", S.nsem)
    return nc


_PROG_CACHE = {}

_WNAMES = ["ssd_norm_w", "ssd_w_in", "ssd_conv_w", "ssd_conv_b", "ssd_dt_bias", "ssd_a_log", "ssd_d", "ssd_gnorm_w",
           "ssd_w_out", "gdn_norm_w", "gdn_w_in", "gdn_conv_w", "gdn_dt_bias", "gdn_a_log", "gdn_onorm_w", "gdn_w_out",
           "final_norm_w"]


def kernel(**inputs):
    f32 = lambda a: np.ascontiguousarray(np.asarray(a), dtype=np.float32)
    x_prompt = f32(inputs["x_prompt"])
    x_sample = f32(inputs["x_sample"])
    B, SEQ, _ = x_prompt.shape
    NS = x_sample.shape[0]
    n = 8
    assert NS == n and B <= n
    if SEQ not in _PROG_CACHE:
        _PROG_CACHE[SEQ] = build_program(SEQ)
    nc = _PROG_CACHE[SEQ]
    wts = {k: f32(inputs[k]) for k in _WNAMES}
    st = {k: f32(inputs[k]) for k in ["state_ssd", "state_ssd_conv", "state_gdn", "state_gdn_conv"]}
    meta = f32(inputs["meta_tokens"])
    in_maps = []
    for c in range(n):
        m = {"x_prompt": x_prompt[c % B], "x_sample": x_sample[c], "meta_tokens": meta}
        for k, v in st.items():
            m[k] = np.ascontiguousarray(v[:, c])
        m.update(wts)
        in_maps.append(m)
    res = run_bass_kernel_spmd(nc, in_maps, core_ids=list(range(n)))
    R = res.results
    y_prompt = np.stack([R[b]["y_prompt"] for b in range(B)])
    y_sample = np.stack([R[c]["y_sample"] for c in range(n)])
    outs = [y_prompt, y_sample]
    for nm in ["p_ssd", "p_ssdc", "p_gdn", "p_gdnc"]:
        outs.append(np.stack([R[b][nm] for b in range(B)], axis=1))
    for nm in ["s_ssd", "s_ssdc", "s_gdn", "s_gdnc"]:
        outs.append(np.stack([R[c][nm] for c in range(n)], axis=1))
    return tuple(np.ascontiguousarray(o, dtype=np.float32) for o in outs)
```

```python
import os
import numpy as np
from contextlib import ExitStack
import concourse.bass as bass
import concourse.mybir as mybir
from concourse.bass_utils import run_bass_kernel_spmd

F32 = mybir.dt.float32
BF16 = mybir.dt.bfloat16
AF = mybir.ActivationFunctionType
ALU = mybir.AluOpType

D = 1024
NMETA = 16
EPS = 1e-6
SSD_INNER = 2048
SSD_H = 32
SSD_P = 64
SSD_G = 4
SSD_N = 128
SSD_CONV = 3072
SSD_IN = 5152
GDN_H = 8
GDN_DK = 128
GDN_DV = 256
GDN_KEY = 1024
GDN_VAL = 2048
GDN_CONV = 4096
GDN_IN = 6160
NEG = -30000.0
NTMP = 4


class Sched:
    ROT = int(os.environ.get('KROT', '28000'))

    def __init__(self, nc, es):
        self.nc = nc
        self.es = es
        self.eng = {"pe": nc.tensor, "act": nc.scalar, "dve": nc.vector, "pool": nc.gpsimd, "sp": nc.sync}
        self.nsem = 0
        self.sems = []
        self.cur = {}
        for e in self.eng:
            self.cur[e] = [self._new_sem(), 0]
        self.dma_slots = {"hw": [[self._new_sem(), 0] for _ in range(12)],
                          "sw": [[self._new_sem(), 0] for _ in range(8)]}
        self.dma_rr = {"hw": 0, "sw": 0}
        self.waited = {e: {} for e in self.eng}
        self.last_w = {}
        self.readers = {}
        self.n_inst = 0

    def _new_sem(self):
        s = self.es.enter_context(self.nc.semaphore(f"sm{self.nsem}"))
        self.nsem += 1
        self.sems.append(s)
        return len(self.sems) - 1

    def _wait(self, e, deps):
        best = {}
        for (si, v) in deps:
            if best.get(si, 0) < v:
                best[si] = v
        for si, v in best.items():
            if e == "pe" and si == self.cur["pe"][0]:
                continue
            if NOSELF and si == self.cur[e][0]:
                continue
            if self.waited[e].get(si, 0) >= v:
                continue
            self.eng[e].wait_ge(self.sems[si], v)
            self.waited[e][si] = v

    def _deps(self, reads, writes):
        deps = []
        for k in reads:
            if k in self.last_w:
                deps.append(self.last_w[k])
        for k in writes:
            if k in self.last_w:
                deps.append(self.last_w[k])
            deps.extend(self.readers.get(k, ()))
        return deps

    def _stamp(self, stamp, reads, writes):
        for k in reads:
            r = self.readers.setdefault(k, {})
            if r.get(stamp[0], 0) < stamp[1]:
                r[stamp[0]] = stamp[1]
        for k in writes:
            self.last_w[k] = stamp
            self.readers[k] = {}

    def _deps2(self, reads, writes, e=None):
        deps = []
        for k in reads:
            if k in self.last_w:
                deps.append(self.last_w[k])
            if k.startswith("pb") and e is not None:
                own = self.cur[e][0]
                deps.extend((si, v) for si, v in self.readers.get(k, {}).items() if si != own)
        for k in writes:
            if k in self.last_w:
                deps.append(self.last_w[k])
            deps.extend(self.readers.get(k, {}).items())
        return deps

    stop_at = None
    names = None
    phase = "pre"

    def op(self, e, fn, reads=(), writes=()):
        if self.stop_at and self.n_inst >= self.stop_at:
            raise _Stop()
        self._wait(e, self._deps2(reads, writes, e))
        c = self.cur[e]
        if c[1] >= self.ROT:
            c[0] = self._new_sem()
            c[1] = 0
        inst = fn(self.eng[e])
        c[1] += 1
        if self.names is not None:
            try:
                self.names[str(inst.ins.name)] = self.phase
            except Exception:
                pass
        inst.then_inc(self.sems[c[0]], 1)
        self._stamp((c[0], c[1]), reads, writes)
        self.n_inst += 1
        return inst

    def dma(self, e, out, in_, reads=(), writes=(), slow=False):
        self._wait(e, self._deps2(reads, writes))
        kind = "sw" if e == "pool" else "hw"
        slot = self.dma_slots[kind][self.dma_rr[kind]]
        self.dma_rr[kind] = (self.dma_rr[kind] + 1) % len(self.dma_slots[kind])
        if slot[1] >= self.ROT:
            self._wait(e, [(slot[0], slot[1])])
            slot[0] = self._new_sem()
            slot[1] = 0
        if slot[1] > 0:
            self._wait(e, [(slot[0], slot[1])])
        if slow:
            inst = self.eng[e].dma_start(out=out, in_=in_, allow_slow_non_contiguous=True)
        else:
            inst = self.eng[e].dma_start(out=out, in_=in_)
        slot[1] += 16
        inst.then_inc(self.sems[slot[0]], 16)
        self._stamp((slot[0], slot[1]), reads, writes)
        self.n_inst += 1
        return inst

    def barrier(self):
        stamps = [(s[0], s[1]) for kind in self.dma_slots for s in self.dma_slots[kind] if s[1] > 0]
        for k, c in self.cur.items():
            if c[1] > 0:
                stamps.append((c[0], c[1]))
        for e in self.eng:
            self._wait(e, [st for st in stamps if st[0] != self.cur[e][0]])

    def finish(self, e="sp"):
        deps = [(s[0], s[1]) for kind in self.dma_slots for s in self.dma_slots[kind] if s[1] > 0]
        for k, c in self.cur.items():
            if c[1] > 0 and k != e:
                deps.append((c[0], c[1]))
        self._wait(e, deps)


SKIP = os.environ.get('KSKIP', '').split(',')
NOSELF = os.environ.get('KNOSELF', '0') == '1'


class _Stop(Exception):
    pass


def build_program(SEQ, NL=4, dbg=False):
    nc = bass.Bass("TRN2", target_bir_lowering=False)
    TP = NMETA + SEQ
    assert SEQ % 128 == 0
    NCH = SEQ // 128
    TT = TP + 16
    chunks = [("s", TP, 16)] + [("p", 0, 16)] + [("p", 16 + 128 * i, 128) for i in range(NCH)]

    din = {}

    def inp(name, shape):
        din[name] = nc.dram_tensor(name, list(shape), F32, kind="ExternalInput").ap()
        return din[name]

    def outp(name, shape):
        return nc.dram_tensor(name, list(shape), F32, kind="ExternalOutput").ap()

    xp = inp("x_prompt", [SEQ, D])
    xs = inp("x_sample", [16, D])
    meta = inp("meta_tokens", [NMETA, D])
    st_ssd = inp("state_ssd", [2, SSD_H, SSD_P, SSD_N])
    st_ssdc = inp("state_ssd_conv", [2, 3, SSD_CONV])
    st_gdn = inp("state_gdn", [2, GDN_H, GDN_DK, GDN_DV])
    st_gdnc = inp("state_gdn_conv", [2, 3, GDN_CONV])
    W = {}
    for nm, shp in [("ssd_norm_w", [2, D]), ("ssd_w_in", [2, D, SSD_IN]), ("ssd_conv_w", [2, 4, SSD_CONV]),
                    ("ssd_conv_b", [2, SSD_CONV]), ("ssd_dt_bias", [2, SSD_H]), ("ssd_a_log", [2, SSD_H]),
                    ("ssd_d", [2, SSD_H]), ("ssd_gnorm_w", [2, SSD_INNER]), ("ssd_w_out", [2, SSD_INNER, D]),
                    ("gdn_norm_w", [2, D]), ("gdn_w_in", [2, D, GDN_IN]), ("gdn_conv_w", [2, 4, GDN_CONV]),
                    ("gdn_dt_bias", [2, GDN_H]), ("gdn_a_log", [2, GDN_H]), ("gdn_onorm_w", [2, GDN_DV]),
                    ("gdn_w_out", [2, GDN_VAL, D]), ("final_norm_w", [D])]:
        W[nm] = inp(nm, shp)

    y_p = outp("y_prompt", [SEQ, D])
    y_s = outp("y_sample", [16, D])
    o_pssd = outp("p_ssd", [2, SSD_H, SSD_P, SSD_N])
    o_pssdc = outp("p_ssdc", [2, 3, SSD_CONV])
    o_pgdn = outp("p_gdn", [2, GDN_H, GDN_DK, GDN_DV])
    o_pgdnc = outp("p_gdnc", [2, 3, GDN_CONV])
    o_sssd = outp("s_ssd", [2, SSD_H, SSD_P, SSD_N])
    o_sssdc = outp("s_ssdc", [2, 3, SSD_CONV])
    o_sgdn = outp("s_gdn", [2, GDN_H, GDN_DK, GDN_DV])
    o_sgdnc = outp("s_gdnc", [2, 3, GDN_CONV])

    hbuf = [nc.dram_tensor(f"hbuf{i}", [128, 8, TT], F32, kind="Internal").ap() for i in range(2)]
    dbg_o = outp("dbg_o", [128, 1024]) if dbg else None

    with ExitStack() as es0:
        S = Sched(nc, es0)

        uid = [0]

        def mk(es):
            def sb(name, shape, dt=F32):
                uid[0] += 1
                return es.enter_context(nc.sbuf_tensor(f"{name}_u{uid[0]}", list(shape), dt))
            return sb

        sb0 = mk(es0)

        def nfree(ap):
            dims = [list(d) for d in ap.ap][1:]
            dims = [d for d in dims if d[1] != 1]
            merged = []
            for d in dims:
                if merged and merged[-1][0] == d[0] * d[1]:
                    merged[-1] = [d[0], merged[-1][1] * d[1]]
                else:
                    merged.append(d)
            return len(merged)

        def MM(out, lhsT, rhs, start, stop, r, w):
            assert nfree(rhs) <= 1 and nfree(lhsT) <= 1, (nfree(rhs), nfree(lhsT), rhs, lhsT)
            S.op("pe", lambda e: e.matmul(out, lhsT=lhsT, rhs=rhs, start=start, stop=stop), reads=r, writes=w)

        def TR(out, in_, idn, r, w):
            assert nfree(in_) <= 1 and nfree(idn) <= 1, (nfree(in_), in_)
            S.op("pe", lambda e: e.transpose(out, in_, idn), reads=r, writes=w)

        def ACT(out, in_, func, r, w, scale=None, bias=None):
            kw = {}
            if scale is not None:
                kw["scale"] = scale
            if bias is not None:
                kw["bias"] = bias
            S.op("act", lambda e: e.activation(out=out, in_=in_, func=func, **kw), reads=r, writes=w)

        def TT(eng, out, in0, in1, op, r, w):
            S.op(eng, lambda e: e.tensor_tensor(out=out, in0=in0, in1=in1, op=op), reads=r, writes=w)

        def TS(eng, out, in0, s1, op0, r, w, s2=None, op1=None):
            if op1 is None:
                S.op(eng, lambda e: e.tensor_scalar(out=out, in0=in0, scalar1=s1, scalar2=None, op0=op0), reads=r, writes=w)
            else:
                S.op(eng, lambda e: e.tensor_scalar(out=out, in0=in0, scalar1=s1, scalar2=s2, op0=op0, op1=op1),
                     reads=r, writes=w)

        def STT(out, in0, scalar, in1, op0, op1, r, w):
            S.op("dve", lambda e: e.scalar_tensor_tensor(out=out, in0=in0, scalar=scalar, in1=in1, op0=op0, op1=op1),
                 reads=r, writes=w)

        def CP(eng, out, in_, r, w):
            if eng == "act":
                ACT(out, in_, AF.Copy, r, w)
            else:
                S.op(eng, lambda e: e.tensor_copy(out=out, in_=in_), reads=r, writes=w)

        def MEMSET(eng, ap, val, w):
            S.op(eng, lambda e: e.memset(ap, val), writes=w)

        ident = sb0("ident", [128, 128])
        ident_bf = sb0("ident_bf", [128, 128], BF16)
        ones_bf = sb0("ones_bf", [128, 128], BF16)
        ones_f = sb0("ones_f", [128, 128])
        tri = sb0("tri", [128, 128])
        incl = sb0("incl", [128, 128])
        mneg_f = sb0("mneg_f", [128, 512])
        mnegL = {128: sb0("mneg128", [128, 512], BF16), 16: sb0("mneg16", [128, 64], BF16)}
        mnegsL = {128: sb0("mnegs128", [128, 512], BF16), 16: sb0("mnegs16", [128, 64], BF16)}
        epsb = sb0("epsb", [128, 1])
        oneb = sb0("oneb", [128, 1])
        fnw = sb0("fnw", [128, 8])
        CONST = ["const"]
        MEMSET("pool", ident[:], 1.0, CONST)
        S.op("pool", lambda e: e.affine_select(out=ident[:], in_=ident[:], pattern=[[1, 128]], compare_op=ALU.is_equal,
                                               fill=0.0, base=0, channel_multiplier=-1), reads=CONST, writes=CONST)
        CP("pool", ident_bf[:], ident[:], CONST, CONST)
        MEMSET("pool", ones_bf[:], 1.0, CONST)
        MEMSET("pool", ones_f[:], 1.0, CONST)
        MEMSET("pool", tri[:], 1.0, CONST)
        S.op("pool", lambda e: e.affine_select(out=tri[:], in_=tri[:], pattern=[[-1, 128]], compare_op=ALU.is_ge,
                                               fill=0.0, base=-1, channel_multiplier=1), reads=CONST, writes=CONST)
        MEMSET("pool", incl[:], 1.0, CONST)
        S.op("pool", lambda e: e.affine_select(out=incl[:], in_=incl[:], pattern=[[1, 128]], compare_op=ALU.is_ge,
                                               fill=0.0, base=0, channel_multiplier=-1), reads=CONST, writes=CONST)
        for LL in (128, 16):
            for strict, dst in ((False, mnegL[LL]), (True, mnegsL[LL])):
                MEMSET("pool", mneg_f[:, 0:4 * LL], 0.0, CONST)
                S.op("pool", lambda e: e.affine_select(out=mneg_f[:, 0:4 * LL], in_=mneg_f[:, 0:4 * LL],
                                                       pattern=[[0, 4], [1, LL]],
                                                       compare_op=(ALU.is_gt if strict else ALU.is_ge), fill=NEG, base=0,
                                                       channel_multiplier=-1), reads=CONST, writes=CONST)
                CP("pool", dst[:, :], mneg_f[:, 0:4 * LL], CONST, CONST)
        MEMSET("pool", epsb[:], EPS, CONST)
        MEMSET("pool", oneb[:], 1.0, CONST)
        S.dma("sp", fnw[:], W["final_norm_w"].rearrange("(k p) -> p k", p=128), writes=CONST, slow=True)

        PB = [es0.enter_context(nc.psum_tensor(f"pb{i}", [128, 512], F32)) for i in range(8)]

        def bank3(i, n, L):
            return PB[i][:, 0:n * 128].rearrange("p (k t) -> p k t", k=n)[:, :, :L]

        S.phase = "prologue"
        with ExitStack() as es:
            sb = mk(es)
            xtok = [sb(f"xtok{i}", [128, D]) for i in range(2)]
            htile = [sb(f"htile{i}", [128, 8, 128]) for i in range(2)]
            for ci, (stream, off, L) in enumerate(chunks):
                b = ci % 2
                if stream == "s":
                    src = xs[:, :]
                elif off == 0:
                    src = meta[:, :]
                else:
                    src = xp[off - 16: off - 16 + L, :]
                S.dma("sp", xtok[b][:L, :], src, writes=[f"xtok{b}"])
                for kt in range(8):
                    bk = 2 * b + kt // 4
                    TR(PB[bk][:, (kt % 4) * 128:(kt % 4) * 128 + L], xtok[b][:L, kt * 128:(kt + 1) * 128], ident[:L, :L],
                       [f"xtok{b}", "const"], [f"pb{bk}"])
                for half in range(2):
                    bk = 2 * b + half
                    CP("act" if half else "dve", htile[b][:, 4 * half:4 * half + 4, :L], bank3(bk, 4, L),
                       [f"pb{bk}"], [f"htile{b}"])
                S.dma("sp", hbuf[0][:, :, off:off + L], htile[b][:, :, :L], reads=[f"htile{b}"], writes=[f"h0_{ci}"])
        S.barrier()

        def front(sb_t, hsrc, ci, off, L, normw):
            S.phase = "front"
            ht, sq, rstd, xn = sb_t
            S.dma("sp", ht[:, :, :L], hbuf[hsrc][:, :, off:off + L], reads=[f"h{hsrc}_{ci}"], writes=["ht"])
            TT("pool", sq[:, :, :L], ht[:, :, :L], ht[:, :, :L], ALU.mult, ["ht"], ["sq"])
            for kt in range(8):
                MM(PB[0][:, :L], ones_bf[:, :], sq[:, kt, :L], kt == 0, kt == 7, ["sq", "const"], ["pb0"])
            ACT(rstd[:, :L], PB[0][:, :L], AF.Ln, ["pb0", "const"], ["rstd"], scale=1.0 / D, bias=epsb[:, 0:1])
            ACT(rstd[:, :L], rstd[:, :L], AF.Exp, ["rstd"], ["rstd"], scale=-0.5)
            for kt in range(8):
                STT(xn[:, kt, :L], ht[:, kt, :L], normw[:, kt:kt + 1], rstd[:, :L], ALU.mult, ALU.mult,
                    ["ht", "rstd", "lw"], ["xn"])

        def proj_conv_b(Win, xn, tiles, col0, L, tmp, halo, convw, convb, dst_of, banks=(1, 2)):
            nb = len(tiles) // 4
            skew = len(tmp) >= 8

            def tslot(bi, sl):
                i = ((bi % 2) * 4 + sl) if skew else sl
                return tmp[i], f"ctmp{i}"

            def stA(bi):
                bk = banks[bi % 2]
                for sl in range(4):
                    ct = tiles[bi * 4 + sl]
                    c0 = col0 + ct * 128
                    for kt in range(8):
                        MM(PB[bk][:, sl * 128:sl * 128 + L], Win[:, kt, c0:c0 + 128], xn[:, kt, :L], kt == 0, kt == 7,
                           ["xn", f"Win{kt}"], [f"pb{bk}"])
                for sl in range(4):
                    t, tk = tslot(bi, sl)
                    CP("act", t[:, 3:3 + L], PB[bk][:, sl * 128:sl * 128 + L], [f"pb{bk}"], [tk])
                for sl in range(4):
                    ct = tiles[bi * 4 + sl]
                    t, tk = tslot(bi, sl)
                    CP("pool", t[:, 0:3], halo[:, :, ct], ["halo"], [tk])

            def stB(bi):
                for k in range(4):
                    for sl in range(4):
                        ct = tiles[bi * 4 + sl]
                        t, tk = tslot(bi, sl)
                        acc = t[:, 136:136 + L]
                        if k == 0:
                            if convb is not None:
                                TS("dve", acc, t[:, 0:L], convw[:, ct, 0:1], ALU.mult, [tk, "lw"], [tk],
                                   s2=convb[:, ct:ct + 1], op1=ALU.add)
                            else:
                                TS("dve", acc, t[:, 0:L], convw[:, ct, 0:1], ALU.mult, [tk, "lw"], [tk])
                        else:
                            STT(acc, t[:, k:k + L], convw[:, ct, k:k + 1], acc, ALU.mult, ALU.add, [tk, "lw"], [tk])
                for sl in range(4):
                    ct = tiles[bi * 4 + sl]
                    t, tk = tslot(bi, sl)
                    CP("pool", halo[:, :, ct], t[:, L:L + 3], [tk], ["halo"])

            def stC(bi):
                for sl in range(4):
                    ct = tiles[bi * 4 + sl]
                    t, tk = tslot(bi, sl)
                    out_ap, key = dst_of(ct)
                    ACT(out_ap, t[:, 136:136 + L], AF.Silu, [tk], [key])

            if skew:
                stA(0)
                for bi in range(nb):
                    if bi + 1 < nb:
                        stA(bi + 1)
                    stB(bi)
                    stC(bi)
            else:
                for bi in range(nb):
                    stA(bi)
                    stB(bi)
                    stC(bi)

        def ssd_layer(j, hsrc, hdst, last):
            with ExitStack() as es:
                try:
                    ssd_body(mk(es), j, hsrc, hdst, last)
                except _Stop:
                    S.stop_at = None
                    S.stopped = True
            S.barrier()

        def ssd_body(sb, j, hsrc, hdst, last):
            if True:
                Win = sb("Win", [128, 8, SSD_IN], BF16)
                Wout = sb("Wout", [128, 16, D], BF16)
                normw = sb("normw", [128, 8])
                convw = sb("convw", [128, 24, 4])
                convb = sb("convb", [128, 24])
                dtb = sb("dtb", [32, 1])
                aneg = sb("aneg", [32, 1])
                dsk = sb("dsk", [128, 16])
                gnw = sb("gnw", [128, 16])
                St = sb("St", [128, 2048])
                Sbf = sb("Sbf", [128, 2048], BF16)
                halo = sb("halo", [128, 3, 24])
                ht = sb("ht", [128, 8, 128])
                sq = sb("sq", [128, 16, 128], BF16)
                rstd = sb("rstd", [128, 4, 128])
                xn = sb("xn", [128, 8, 128], BF16)
                zs = sb("zs", [128, 16, 128])
                ctall = sb("ctall", [128, 8 * 272])
                tmp = [ctall[:, i * 272:(i + 1) * 272] for i in range(8)]
                xcf = sb("xcf", [128, 24, 128])
                bcbf2 = [sb(f"bcbf{i}", [128, 2, 128], BF16) for i in range(2)]
                dtT = sb("dtT", [32, 5, 128])
                onesr = sb("onesr", [32, 128])
                tokm = sb("tokm", [128, 5, 32])
                edec = sb("edec", [128, 32])
                xdt2 = [sb(f"xdt{i}", [128, 512], BF16) for i in range(2)]
                xde2 = [sb(f"xde{i}", [128, 512], BF16) for i in range(2)]
                btok2 = [sb(f"btok{i}", [128, 128], BF16) for i in range(2)]
                cbT = sb("cbT", [128, 128])
                Dq = sb("Dq", [128, 1024])
                dec = sb("dec", [128, 1024])
                WT2 = [sb(f"WT{i}", [128, 1024], BF16) for i in range(2)]
                ytok = sb("ytok", [128, 512])
                yT = sb("yT", [128, 16, 128])
                ygn = sq
                stg = yT
                hstg = sb("hstg", [72, 128])

                for kt in range(8):
                    S.dma("pool", Win[:, kt, :], W["ssd_w_in"][j, kt * 128:(kt + 1) * 128, :], writes=[f"Win{kt}"])
                for kt in range(16):
                    S.dma("pool", Wout[:, kt, :], W["ssd_w_out"][j, kt * 128:(kt + 1) * 128, :], writes=[f"Wout{kt}"])
                LW = ["lw"]
                S.dma("sp", normw[:], W["ssd_norm_w"][j].rearrange("(k p) -> p k", p=128), writes=LW, slow=True)
                for k in range(4):
                    S.dma("sp", convw[:, :, k], W["ssd_conv_w"][j, k].rearrange("(c p) -> p c", p=128), writes=LW, slow=True)
                S.dma("sp", convb[:], W["ssd_conv_b"][j].rearrange("(c p) -> p c", p=128), writes=LW, slow=True)
                S.dma("sp", dtb[:], W["ssd_dt_bias"][j].rearrange("(h o) -> h o", o=1), writes=LW, slow=True)
                S.dma("sp", aneg[:], W["ssd_a_log"][j].rearrange("(h o) -> h o", o=1), writes=LW, slow=True)
                S.dma("sp", gnw[:], W["ssd_gnorm_w"][j].rearrange("(c p) -> p c", p=128), writes=LW, slow=True)
                d2 = W["ssd_d"][j].rearrange("(c two) -> two c", two=2)
                for two in range(2):
                    S.dma("sp", dsk[two * 64:(two + 1) * 64, :], d2[two].partition_broadcast(64), writes=LW, slow=True)
                ACT(aneg[:], aneg[:], AF.Exp, LW, LW)
                TS("dve", aneg[:], aneg[:], -1.0, ALU.mult, LW, LW)
                MEMSET("pool", onesr[:], 1.0, LW)
                WinK = [f"Win{kt}" for kt in range(8)]
                WoutK = [f"Wout{kt}" for kt in range(16)]

                def store_state(o_state, o_conv):
                    S.barrier()
                    for c in range(16):
                        bk = 4 + (c // 4) % 2
                        TR(PB[bk][:, (c % 4) * 128:(c % 4 + 1) * 128], St[:, c * 128:(c + 1) * 128], ident[:, :],
                           ["St", "const"], [f"pb{bk}"])
                        if c % 4 == 3:
                            CP("act" if (c // 4) % 2 else "dve", stg[:, c - 3:c + 1, :], bank3(bk, 4, 128), [f"pb{bk}"], ["stg"])
                    S.dma("sp", o_state[j].rearrange("(c two) p n -> (two p) c n", two=2), stg[:, :, :], reads=["stg"],
                          writes=["ostate"])
                    TR(PB[6][:72, :128], halo[:, :, :].rearrange("p k c -> p (k c)"), ident[:, :], ["halo", "const"], ["pb6"])
                    CP("dve", hstg[:, :], PB[6][:72, :128], ["pb6"], ["hstg"])
                    for k in range(3):
                        S.dma("sp", o_conv[j, k].rearrange("(c p) -> c p", p=128), hstg[k * 24:(k + 1) * 24, :],
                              reads=["hstg"], writes=["oconv"])
                    S.barrier()

                prev_stream = None
                for ci, (stream, off, L) in enumerate(chunks):
                    if stream != prev_stream:
                        if prev_stream == "s":
                            store_state(o_sssd, o_sssdc)
                        if stream == "s":
                            S.dma("sp", stg[:, :, :], st_ssd[j].rearrange("(c two) p n -> (two p) c n", two=2),
                                  reads=[], writes=["stg"])
                            for c in range(16):
                                bk = 4 + (c // 4) % 2
                                TR(PB[bk][:, (c % 4) * 128:(c % 4 + 1) * 128], stg[:, c, :], ident[:, :],
                                   ["stg", "const"], [f"pb{bk}"])
                                if c % 4 == 3:
                                    CP("act" if (c // 4) % 2 else "dve", St[:, (c - 3) * 128:(c + 1) * 128], PB[bk][:, :],
                                       [f"pb{bk}"], ["St"])
                            CP("pool", Sbf[:, :], St[:, :], ["St"], ["Sbf"])
                            S.dma("sp", hstg[:, :], st_ssdc[j].rearrange("k (c p) -> (k c) p", p=128), writes=["hstg"])
                            TR(PB[6][:, :72], hstg[:, :], ident[:72, :72], ["hstg", "const"], ["pb6"])
                            CP("dve", halo[:, :, :], PB[6][:, :72].rearrange("p (k c) -> p k c", k=3), ["pb6"], ["halo"])
                            S.barrier()
                        else:
                            MEMSET("pool", St[:, :], 0.0, ["St"])
                            MEMSET("pool", Sbf[:, :], 0.0, ["Sbf"])
                            MEMSET("pool", halo[:, :, :], 0.0, ["halo"])
                        prev_stream = stream

                    front((ht, sq[:, 0:8, :], rstd[:, 0, :], xn), hsrc, ci, off, L, normw)

                    S.phase = "ssd_inproj"
                    for kt in range(8):
                        MM(PB[3][:32, :L], Win[:, kt, 5120:5152], xn[:, kt, :L], kt == 0, kt == 7, ["xn", f"Win{kt}"], ["pb3"])
                    ACT(dtT[:, 0, :L], PB[3][:32, :L], AF.Exp, ["pb3", "lw"], ["dtT"], bias=dtb[:, 0:1])
                    ACT(dtT[:, 0, :L], dtT[:, 0, :L], AF.Ln, ["dtT", "const"], ["dtT"], bias=oneb[:32, 0:1])
                    TS("dve", dtT[:, 1, :L], dtT[:, 0, :L], aneg[:, 0:1], ALU.mult, ["dtT", "lw"], ["dtT"])
                    S.op("dve", lambda e: e.tensor_tensor_scan(out=dtT[:, 2, :L], data0=onesr[:, :L], data1=dtT[:, 1, :L],
                                                               initial=0.0, op0=ALU.mult, op1=ALU.add),
                         reads=["dtT", "lw"], writes=["dtT"])
                    ACT(dtT[:, 3, :L], dtT[:, 2, :L], AF.Exp, ["dtT"], ["dtT"], scale=-1.0, bias=dtT[:, 2, L - 1:L])
                    TT("dve", dtT[:, 3, :L], dtT[:, 3, :L], dtT[:, 0, :L], ALU.mult, ["dtT"], ["dtT"])
                    ACT(dtT[:, 4, :L], dtT[:, 2, :L], AF.Exp, ["dtT"], ["dtT"])
                    for q in range(5):
                        TR(PB[3][:L, 64 + q * 32:64 + (q + 1) * 32], dtT[:, q, :L], ident[:32, :32], ["dtT", "const"], ["pb3"])
                    CP("dve", tokm[:L, :, :], PB[3][:L, 64:224].rearrange("p (q h) -> p q h", q=5), ["pb3"], ["tokm"])
                    MM(PB[3][:, 256:288], ones_f[:L, :], tokm[:L, 1, :], True, True, ["tokm", "const"], ["pb3"])
                    ACT(edec[:, :], PB[3][:, 256:288], AF.Exp, ["pb3"], ["edec"])

                    order = list(range(16, 24)) + list(range(0, 16))
                    proj_conv_b(Win, xn, order, 2048, L, tmp, halo, convw, convb, lambda ct: (xcf[:, ct, :L], f"xcf{ct}"))
                    for bi in range(4):
                        bk = 1 + bi % 2
                        for sl in range(4):
                            zt = bi * 4 + sl
                            for kt in range(8):
                                MM(PB[bk][:, sl * 128:sl * 128 + L], Win[:, kt, zt * 128:(zt + 1) * 128], xn[:, kt, :L],
                                   kt == 0, kt == 7, ["xn", f"Win{kt}"], [f"pb{bk}"])
                        ACT(zs[:, bi * 4:bi * 4 + 4, :L], bank3(bk, 4, L), AF.Silu, [f"pb{bk}"], [f"zs{bi}"])

                    S.phase = "ssd_scan"
                    def scanA(g):
                        d = g % 2
                        hs = slice(8 * g, 8 * g + 8)
                        xk = [f"xcf{4 * g + i}" for i in range(4)]
                        CP("act", bcbf2[d][:, 0, :L], xcf[:, 16 + g, :L], [f"xcf{16 + g}"], [f"bcbf{d}"])
                        CP("act", bcbf2[d][:, 1, :L], xcf[:, 20 + g, :L], [f"xcf{20 + g}"], [f"bcbf{d}"])
                        for i in range(4):
                            TR(PB[4][:L, i * 128:(i + 1) * 128], xcf[:, 4 * g + i, :L], ident[:, :], [xk[i], "const"], ["pb4"])
                        TR(PB[5][:L, 0:128], xcf[:, 16 + g, :L], ident[:, :], [f"xcf{16 + g}", "const"], ["pb5"])
                        MM(PB[5][:L, 128:128 + L], bcbf2[d][:, 0, :L], bcbf2[d][:, 1, :L], True, True, [f"bcbf{d}"], ["pb5"])
                        TT("pool", Dq[:L, 0:8 * L].rearrange("p (h t) -> p h t", h=8), incl[:L, :L].unsqueeze(1).broadcast_to([L, 8, L]),
                           tokm[:L, 1, hs].unsqueeze(2).broadcast_to([L, 8, L]), ALU.mult, ["const", "tokm"], ["Dq"])
                        x3 = PB[4][:L, :].rearrange("p (h q) -> p h q", h=8)
                        TT("dve", xdt2[d][:L, :].rearrange("p (h q) -> p h q", h=8), x3,
                           tokm[:L, 0, hs].unsqueeze(2).broadcast_to([L, 8, 64]), ALU.mult, ["pb4", "tokm"], [f"xdt{d}"])
                        TT("dve", xde2[d][:L, :].rearrange("p (h q) -> p h q", h=8), x3,
                           tokm[:L, 3, hs].unsqueeze(2).broadcast_to([L, 8, 64]), ALU.mult, ["pb4", "tokm"], [f"xde{d}"])
                        CP("act", btok2[d][:L, :], PB[5][:L, 0:128], ["pb5"], [f"btok{d}"])
                        CP("act", cbT[:L, :L], PB[5][:L, 128:128 + L], ["pb5"], ["cbT"])
                        for hf in range(2):
                            bk = 6 + hf
                            MM(PB[bk][:L, 0:4 * L], tri[:L, :L], Dq[:L, 4 * hf * L:(4 * hf + 4) * L], True, False,
                               ["Dq", "const"], [f"pb{bk}"])
                            MM(PB[bk][:L, 0:4 * L], ident_bf[:L, :L], mnegL[L][:L, :], False, True, ["const"], [f"pb{bk}"])
                        for hf in range(2):
                            bk = 6 + hf
                            ACT(dec[:L, 4 * hf * L:(4 * hf + 4) * L], PB[bk][:L, 0:4 * L], AF.Exp, [f"pb{bk}"], ["dec"])
                        TT("dve", WT2[d][:L, 0:8 * L].rearrange("p (h t) -> p h t", h=8),
                           dec[:L, 0:8 * L].rearrange("p (h t) -> p h t", h=8),
                           cbT[:L, :L].unsqueeze(1).broadcast_to([L, 8, L]), ALU.mult, ["dec", "cbT"], [f"WT{d}"])

                    def scanB(g):
                        d = g % 2
                        hs = slice(8 * g, 8 * g + 8)
                        xk = [f"xcf{4 * g + i}" for i in range(4)]
                        MM(PB[1][:L, :], bcbf2[d][:, 1, :L], Sbf[:, g * 512:(g + 1) * 512], True, True, [f"bcbf{d}", f"Sbf{g}"], ["pb1"])
                        for h in range(8):
                            MM(PB[2][:L, h * 64:(h + 1) * 64], WT2[d][:L, h * L:(h + 1) * L], xdt2[d][:L, h * 64:(h + 1) * 64], True, True,
                               [f"WT{d}", f"xdt{d}"], ["pb2"])
                        MM(PB[3][:, :], btok2[d][:L, :], xde2[d][:L, :], True, True, [f"btok{d}", f"xde{d}"], ["pb3"])
                        Sg = St[:, g * 512:(g + 1) * 512]
                        TT("pool", Sg.rearrange("p (h q) -> p h q", h=8), Sg.rearrange("p (h q) -> p h q", h=8),
                           edec[:, hs].unsqueeze(2).broadcast_to([128, 8, 64]), ALU.mult, [f"St{g}", "edec"], [f"St{g}"])
                        TT("dve", ytok[:L, :].rearrange("p (h q) -> p h q", h=8), PB[1][:L, :].rearrange("p (h q) -> p h q", h=8),
                           tokm[:L, 4, hs].unsqueeze(2).broadcast_to([L, 8, 64]), ALU.mult, ["pb1", "tokm"], ["ytok"])
                        TT("dve", ytok[:L, :], ytok[:L, :], PB[2][:L, :], ALU.add, ["pb2", "ytok"], ["ytok"])
                        TT("dve", Sg, Sg, PB[3][:, :], ALU.add, [f"St{g}", "pb3"], [f"St{g}"])
                        CP("act", Sbf[:, g * 512:(g + 1) * 512], Sg, [f"St{g}"], [f"Sbf{g}"])
                        for i in range(4):
                            TR(PB[1][:, i * 128:i * 128 + L], ytok[:L, i * 128:(i + 1) * 128], ident[:L, :L], ["ytok", "const"], ["pb1"])
                        for i in range(4):
                            ct = 4 * g + i
                            STT(yT[:, ct, :L], xcf[:, ct, :L], dsk[:, ct:ct + 1], PB[1][:, i * 128:i * 128 + L], ALU.mult, ALU.add,
                                [xk[i], "lw", "pb1"], [f"yT{g}"])
                        TT("pool", yT[:, 4 * g:4 * g + 4, :L], yT[:, 4 * g:4 * g + 4, :L], zs[:, 4 * g:4 * g + 4, :L], ALU.mult,
                           [f"yT{g}", f"zs{g}"], [f"yT{g}"])
                        TT("pool", sq[:, 4 * g:4 * g + 4, :L], yT[:, 4 * g:4 * g + 4, :L], yT[:, 4 * g:4 * g + 4, :L], ALU.mult,
                           [f"yT{g}"], ["sq"])
                        for i in range(4):
                            MM(PB[0][:, g * 128:g * 128 + L], ones_bf[:, :], sq[:, 4 * g + i, :L], i == 0, i == 3,
                               ["sq", "const"], ["pb0"])

                    scanA(0)
                    for g in range(4):
                        if g + 1 < 4:
                            scanA(g + 1)
                        scanB(g)
                    S.phase = "ssd_out"
                    ACT(rstd[:, :, :L], bank3(0, 4, L), AF.Ln, ["pb0", "const"], ["rstd"], scale=1.0 / 512, bias=epsb[:, 0:1])
                    ACT(rstd[:, :, :L], rstd[:, :, :L], AF.Exp, ["rstd"], ["rstd"], scale=-0.5)
                    for ct in range(16):
                        STT(ygn[:, ct, :L], yT[:, ct, :L], gnw[:, ct:ct + 1], rstd[:, ct // 4, :L], ALU.mult, ALU.mult,
                            [f"yT{ct // 4}", "rstd", "lw"], ["sq"])
                    for hf in range(2):
                        bk = 1 + hf
                        for sl in range(4):
                            dt_ = hf * 4 + sl
                            for kt in range(16):
                                MM(PB[bk][:, sl * 128:sl * 128 + L], Wout[:, kt, dt_ * 128:(dt_ + 1) * 128], ygn[:, kt, :L],
                                   kt == 0, kt == 15, ["sq", f"Wout{kt}"], [f"pb{bk}"])
                        TT("dve", ht[:, 4 * hf:4 * hf + 4, :L], ht[:, 4 * hf:4 * hf + 4, :L], bank3(bk, 4, L), ALU.add,
                           ["ht", f"pb{bk}"], ["ht"])
                    S.dma("sp", hbuf[hdst][:, :, off:off + L], ht[:, :, :L], reads=["ht"], writes=[f"h{hdst}_{ci}"])
                store_state(o_pssd, o_pssdc)

        def gdn_layer(j, hsrc, hdst, last):
            with ExitStack() as es:
                try:
                    gdn_body(mk(es), j, hsrc, hdst, last)
                except _Stop:
                    S.stop_at = None
                    S.stopped = True
                    S.barrier()
                    dbt = mk(es)("dbt", [128, 256])
                    for _ in range(int(os.environ.get("KDUMMY", "0"))):
                        MEMSET(os.environ.get("KDUMMYENG", "dve"), dbt[:, 0:8], 0.0, ["dbt"])
                    for qq in range(4):
                        bkq = 7 if qq < 2 else 6
                        CP("dve", dbt[:, :], PB[bkq][:, (qq % 2) * 256:(qq % 2) * 256 + 256], [f"pb{bkq}"], ["dbt"])
                        S.dma("sp", dbg_o[:, qq * 256:(qq + 1) * 256], dbt[:, :], reads=["dbt"], writes=["dbgo"])
            S.barrier()

        def gdn_body(sb, j, hsrc, hdst, last):
            if True:
                normw = sb("gnormw", [128, 8])
                convw = sb("gconvw", [128, 32, 4])
                dtb = sb("gdtb", [8, 1])
                aneg = sb("ganeg", [8, 1])
                onw = sb("onw", [128, 2])
                Sg = sb("Sg", [128, 8, 256])
                halo = sb("ghalo", [128, 3, 32])
                ht = sb("ght", [128, 8, 128])
                sq = sb("gsq", [128, 8, 128], BF16)
                rstd = sb("grstd", [128, 8, 128])
                xn = sb("gxn", [128, 8, 128], BF16)
                ctall = sb("gctall", [128, 4 * 272])
                tmp = [ctall[:, i * 272:(i + 1) * 272] for i in range(4)]
                cf = sb("cf", [128, 16, 128])
                qn = sb("qn", [128, 8, 128], BF16)
                kn = sb("kn", [128, 8, 128], BF16)
                qg = sb("qg", [128, 8, 128], BF16)
                kb = sb("kb", [128, 8, 128], BF16)
                gq = sb("gq", [8, 6, 128])
                onesr = sb("gonesr", [8, 128])
                BD = ctall
                tokg = sb("tokg", [128, 4, 8])
                egl = sb("egl", [128, 8])
                kbg = [sb(f"kbg{i}", [128, 128], BF16) for i in range(4)]
                kgt = [sb(f"kgt{i}", [128, 128], BF16) for i in range(4)]
                bv = [sb(f"bv{i}", [128, 256], BF16) for i in range(4)]
                QKd = [sb(f"QKd{i}", [128, 128], BF16) for i in range(4)]
                Sbf = [sb(f"gSbf{i}", [128, 256], BF16) for i in range(4)]
                Dqh = [sb(f"Dqh{i}", [128, 128]) for i in range(2)]
                decI = [sb(f"decI{i}", [128, 128]) for i in range(2)]
                decS = [sb(f"decS{i}", [128, 128]) for i in range(2)]
                negw = [sb(f"negw{i}", [128, 128], BF16) for i in range(2)]
                vnew = [sb(f"vnew{i}", [128, 256], BF16) for i in range(2)]
                Pm = [sb(f"Pm{i}", [128, 512]) for i in range(2)]
                Qm = [sb(f"Qm{i}", [128, 512]) for i in range(2)]
                Rm = sb("Rm", [128, 512])
                TTb = sb("TTb", [128, 512], BF16)
                Win = sb("gWin", [128, 8, GDN_IN], BF16)
                Wout = sb("gWout", [128, 16, D], BF16)
                zsb = [Pm[i][:, :].rearrange("p (k t) -> p k t", k=4) for i in range(2)]
                hstg = Rm[:96, 0:128]
                CTK = [f"ctmp{i}" for i in range(4)]
                ogn = qn
                ogn2 = kn

                for kt in range(8):
                    S.dma("pool", Win[:, kt, :], W["gdn_w_in"][j, kt * 128:(kt + 1) * 128, :], writes=[f"Win{kt}"])
                for kt in range(16):
                    S.dma("pool", Wout[:, kt, :], W["gdn_w_out"][j, kt * 128:(kt + 1) * 128, :], writes=[f"Wout{kt}"])
                LW = ["lw"]
                S.dma("sp", normw[:], W["gdn_norm_w"][j].rearrange("(k p) -> p k", p=128), writes=LW, slow=True)
                for k in range(4):
                    S.dma("sp", convw[:, :, k], W["gdn_conv_w"][j, k].rearrange("(c p) -> p c", p=128), writes=LW, slow=True)
                S.dma("sp", dtb[:], W["gdn_dt_bias"][j].rearrange("(h o) -> h o", o=1), writes=LW, slow=True)
                S.dma("sp", aneg[:], W["gdn_a_log"][j].rearrange("(h o) -> h o", o=1), writes=LW, slow=True)
                S.dma("sp", onw[:], W["gdn_onorm_w"][j].rearrange("(c p) -> p c", p=128), writes=LW, slow=True)
                ACT(aneg[:], aneg[:], AF.Exp, LW, LW)
                TS("dve", aneg[:], aneg[:], -1.0, ALU.mult, LW, LW)
                MEMSET("pool", onesr[:], 1.0, LW)

                def store_state(o_state, o_conv):
                    S.dma("sp", o_state[j].rearrange("h k v -> k h v"), Sg[:, :, :], reads=[f"Sg{h}" for h in range(8)],
                          writes=["ostate"])
                    TR(PB[6][:96, :128], halo[:, :, :].rearrange("p k c -> p (k c)"), ident[:, :], ["halo", "const"], ["pb6"])
                    CP("dve", hstg[:, :], PB[6][:96, :128], ["pb6"], ["Rm"])
                    for k in range(3):
                        S.dma("sp", o_conv[j, k].rearrange("(c p) -> c p", p=128), hstg[k * 32:(k + 1) * 32, :],
                              reads=["Rm"], writes=["oconv"])

                def proj_conv(bk_list, tiles, col0, L, dst_of):
                    for bi in range(len(tiles) // 4):
                        bk = bk_list[bi % 2]
                        for sl in range(4):
                            ct = tiles[bi * 4 + sl]
                            c0 = col0 + ct * 128
                            for kt in range(8):
                                MM(PB[bk][:, sl * 128:sl * 128 + L], Win[:, kt, c0:c0 + 128], xn[:, kt, :L], kt == 0, kt == 7,
                                   ["xn", f"Win{kt}"], [f"pb{bk}"])
                        for sl in range(4):
                            ct = tiles[bi * 4 + sl]
                            out_ap, key = dst_of(ct)
                            conv_tile(bk, sl, L, ct, tmp, halo, convw, None, out_ap, key)

                prev_stream = None
                for ci, (stream, off, L) in enumerate(chunks):
                    NLEV = {128: 6, 16: 3}[L]
                    if stream != prev_stream:
                        if prev_stream == "s":
                            store_state(o_sgdn, o_sgdnc)
                        if stream == "s":
                            S.dma("sp", Sg[:, :, :], st_gdn[j].rearrange("h k v -> k h v"), writes=[f"Sg{h}" for h in range(8)])
                            S.dma("sp", hstg[:, :], st_gdnc[j].rearrange("k (c p) -> (k c) p", p=128), writes=["Rm"])
                            TR(PB[6][:, :96], hstg[:, :], ident[:96, :96], ["Rm", "const"], ["pb6"])
                            CP("dve", halo[:, :, :], PB[6][:, :96].rearrange("p (k c) -> p k c", k=3), ["pb6"], ["halo"])
                        else:
                            MEMSET("pool", Sg[:, :, :], 0.0, [f"Sg{h}" for h in range(8)])
                            MEMSET("pool", halo[:, :, :], 0.0, ["halo"])
                        prev_stream = stream

                    front((ht, sq, rstd[:, 0, :], xn), hsrc, ci, off, L, normw)
                    if dbg and dbg <= 1:
                        break

                    S.phase = "gdn_gates"
                    for q, c0 in ((0, 6144), (1, 6152)):
                        for kt in range(8):
                            MM(PB[3][:8, q * 128:q * 128 + L], Win[:, kt, c0:c0 + 8], xn[:, kt, :L], kt == 0, kt == 7,
                               ["xn", f"Win{kt}"], ["pb3"])
                    ACT(gq[:, 0, :L], PB[3][:8, 0:L], AF.Exp, ["pb3", "lw"], ["gq"], bias=dtb[:, 0:1])
                    ACT(gq[:, 0, :L], gq[:, 0, :L], AF.Ln, ["gq", "const"], ["gq"], bias=oneb[:8, 0:1])
                    TS("dve", gq[:, 0, :L], gq[:, 0, :L], aneg[:, 0:1], ALU.mult, ["gq", "lw"], ["gq"])
                    S.op("dve", lambda e: e.tensor_tensor_scan(out=gq[:, 1, :L], data0=onesr[:, :L], data1=gq[:, 0, :L],
                                                               initial=0.0, op0=ALU.mult, op1=ALU.add),
                         reads=["gq", "lw"], writes=["gq"])
                    ACT(gq[:, 2, :L], PB[3][:8, 128:128 + L], AF.Exp, ["pb3"], ["gq"], scale=-1.0)
                    TS("dve", gq[:, 2, :L], gq[:, 2, :L], 1.0, ALU.add, ["gq"], ["gq"])
                    S.op("dve", lambda e: e.reciprocal(out=gq[:, 2, :L], in_=gq[:, 2, :L]), reads=["gq"], writes=["gq"])
                    ACT(gq[:, 3, :L], gq[:, 1, :L], AF.Exp, ["gq"], ["gq"])
                    TT("dve", gq[:, 4, :L], gq[:, 2, :L], gq[:, 3, :L], ALU.mult, ["gq"], ["gq"])
                    ACT(gq[:, 5, :L], gq[:, 1, :L], AF.Exp, ["gq"], ["gq"], scale=-1.0, bias=gq[:, 1, L - 1:L])
                    for qi, q in enumerate((0, 2, 4, 5)):
                        TR(PB[3][:L, 256 + qi * 8:256 + (qi + 1) * 8], gq[:, q, :L], ident[:8, :8], ["gq", "const"], ["pb3"])
                    CP("dve", tokg[:L, :, :], PB[3][:L, 256:288].rearrange("p (q h) -> p q h", q=4), ["pb3"], ["tokg"])
                    MM(PB[3][:, 320:328], ones_f[:L, :], tokg[:L, 0, :], True, True, ["tokg", "const"], ["pb3"])
                    ACT(egl[:, :], PB[3][:, 320:328], AF.Exp, ["pb3"], ["egl"])

                    if dbg and dbg <= 2:
                        break
                    S.phase = "gdn_qk"
                    proj_conv_b(Win, xn, list(range(16)), 0, L, tmp, halo, convw, None, lambda ct: (cf[:, ct, :L], f"cf{ct}"))
                    for qd in range(4):
                        t0 = qd * 4
                        cfk = [f"cf{t0 + i}" for i in range(4)]
                        TT("pool", sq[:, 0:4, :L], cf[:, t0:t0 + 4, :L], cf[:, t0:t0 + 4, :L], ALU.mult, cfk, ["sq"])
                        for i in range(4):
                            MM(PB[0][:, i * 128:i * 128 + L], ones_bf[:, :], sq[:, i, :L], True, True, ["sq", "const"], ["pb0"])
                        ACT(rstd[:, 0:4, :L], bank3(0, 4, L), AF.Ln, ["pb0", "const"], ["rstd"], bias=epsb[:, 0:1])
                        ACT(rstd[:, 0:4, :L], rstd[:, 0:4, :L], AF.Exp, ["rstd"], ["rstd"], scale=-0.5)
                        if qd < 2:
                            STT(qn[:, t0:t0 + 4, :L], cf[:, t0:t0 + 4, :L], float(GDN_DK ** -0.5), rstd[:, 0:4, :L], ALU.mult, ALU.mult,
                                cfk + ["rstd"], ["qn"])
                        else:
                            TT("dve", kn[:, t0 - 8:t0 - 4, :L], cf[:, t0:t0 + 4, :L], rstd[:, 0:4, :L], ALU.mult, cfk + ["rstd"], ["kn"])
                    if dbg and dbg <= 3:
                        break
                    S.phase = "gdn_rep"
                    for q, src, dst, dkey in ((2, kn, kb, "kb"), (3, qn, qg, "qg")):
                        TT("pool", BD[:8, 0:8 * L].rearrange("p (h t) -> p h t", h=8),
                           gq[:8, q, :L].unsqueeze(1).broadcast_to([8, 8, L]),
                           ident[:8, :8].unsqueeze(2).broadcast_to([8, 8, L]), ALU.mult, ["gq", "const"], CTK)
                        for hf in range(2):
                            MM(PB[4 + hf][:, 0:4 * L], ones_f[:8, :], BD[:8, 4 * hf * L:(4 * hf + 4) * L], True, True,
                               CTK + ["const"], [f"pb{4 + hf}"])
                            TT("dve", dst[:, 4 * hf:4 * hf + 4, :L], src[:, 4 * hf:4 * hf + 4, :L],
                               PB[4 + hf][:, 0:4 * L].rearrange("p (h t) -> p h t", h=4), ALU.mult,
                               [f"pb{4 + hf}", "qn" if q == 3 else "kn"], [dkey])
                    if dbg and dbg <= 4:
                        break
                    S.phase = "gdn_v"
                    proj_conv_b(Win, xn, list(range(16, 32)), 0, L, tmp, halo, convw, None, lambda ct: (cf[:, ct - 16, :L], f"cf{ct - 16}"))

                    if dbg and dbg <= 5:
                        break
                    S.phase = "gdn_phI"
                    for hb in range(2):
                        S.phase = "gdn_phI"
                        for pr in range(2):
                            HH = (2 * pr, 2 * pr + 1)
                            for hh in HH:
                                h = hb * 4 + hh
                                CP("act", Sbf[hh][:, :], Sg[:, h, :], [f"Sg{h}"], [f"Sbf{hh}"])
                            for hh in HH:
                                h = hb * 4 + hh
                                p = hh % 2
                                bA, bB = 4 + p, 6 + p
                                kA, kB = f"pb{bA}", f"pb{bB}"
                                MM(PB[bA][:L, 0:128], kn[:, h, :L], ident_bf[:, :], True, True, ["kn", "const"], [kA])
                                for vt in range(2):
                                    TR(PB[bA][:L, 128 + vt * 128:256 + vt * 128], cf[:, 2 * h + vt, :L], ident[:, :],
                                       [f"cf{2 * h + vt}", "const"], [kA])
                                MM(PB[bB][:L, 0:L], kn[:, h, :L], kb[:, h, :L], True, True, ["kn", "kb"], [kB])
                                MM(PB[bB][:L, 128:128 + L], kn[:, h, :L], qn[:, h, :L], True, True, ["kn", "qn"], [kB])
                            for hh in HH:
                                h = hb * 4 + hh
                                p = hh % 2
                                TS("dve", Dqh[p][:L, :L], incl[:L, :L], tokg[:L, 0, h:h + 1], ALU.mult, ["const", "tokg"], [f"Dqh{p}"])
                            for hh in HH:
                                p = hh % 2
                                bB = 6 + p
                                kB = f"pb{bB}"
                                MM(PB[bB][:L, 256:256 + L], tri[:L, :L], Dqh[p][:L, :L], True, False, [f"Dqh{p}", "const"], [kB])
                                MM(PB[bB][:L, 256:256 + L], ident_bf[:L, :L], mnegL[L][:L, 0:L], False, True, ["const"], [kB])
                                MM(PB[bB][:L, 384:384 + L], tri[:L, :L], Dqh[p][:L, :L], True, False, [f"Dqh{p}", "const"], [kB])
                                MM(PB[bB][:L, 384:384 + L], ident_bf[:L, :L], mnegsL[L][:L, 0:L], False, True, ["const"], [kB])
                            for hh in HH:
                                h = hb * 4 + hh
                                p = hh % 2
                                bA = 4 + p
                                kA = f"pb{bA}"
                                TS("dve", kbg[hh][:L, :], PB[bA][:L, 0:128], tokg[:L, 2, h:h + 1], ALU.mult, [kA, "tokg"], [f"kbg{hh}"])
                                TS("dve", kgt[hh][:L, :], PB[bA][:L, 0:128], tokg[:L, 3, h:h + 1], ALU.mult, [kA, "tokg"], [f"kgt{hh}"])
                                TS("dve", bv[hh][:L, :], PB[bA][:L, 128:384], tokg[:L, 1, h:h + 1], ALU.mult, [kA, "tokg"], [f"bv{hh}"])
                            for hh in HH:
                                p = hh % 2
                                bB = 6 + p
                                kB = f"pb{bB}"
                                ACT(decI[p][:L, :L], PB[bB][:L, 256:256 + L], AF.Exp, [kB], [f"decI{p}"])
                                ACT(decS[p][:L, :L], PB[bB][:L, 384:384 + L], AF.Exp, [kB], [f"decS{p}"])
                            for hh in HH:
                                p = hh % 2
                                bB = 6 + p
                                kB = f"pb{bB}"
                                TT("dve", QKd[hh][:L, :L], PB[bB][:L, 128:128 + L], decI[p][:L, :L], ALU.mult, [kB, f"decI{p}"], [f"QKd{hh}"])
                                STT(Pm[0][:L, hh * L:(hh + 1) * L], PB[bB][:L, 0:L], -1.0, decS[p][:L, :L], ALU.mult, ALU.mult,
                                    [kB, f"decS{p}"], ["Pm0"])
                        if dbg and dbg <= 6:
                            break
                        S.phase = "gdn_neu"
                        for hh in range(4):
                            sl = slice(hh * L, (hh + 1) * L)
                            TR(PB[1][:L, sl], Pm[0][:L, sl], ident[:L, :L], ["Pm0", "const"], ["pb1"])
                        CP("act", Qm[0][:L, 0:4 * L], PB[1][:L, 0:4 * L], ["pb1"], ["Qm0"])
                        i4 = ident[:L, :L].unsqueeze(1).broadcast_to([L, 4, L])
                        TT("pool", Rm[:L, 0:4 * L].rearrange("p (h t) -> p h t", h=4),
                           Pm[0][:L, 0:4 * L].rearrange("p (h t) -> p h t", h=4), i4, ALU.add, ["Pm0", "const"], ["Rm"])
                        for lev in range(1, NLEV + 1):
                            a, b = (lev - 1) % 2, lev % 2
                            lastlev = lev == NLEV
                            for hh in range(4):
                                sl = slice(hh * L, (hh + 1) * L)
                                MM(PB[1][:L, sl], Pm[a][:L, sl], Qm[a][:L, sl], True, True, [f"Qm{a}", f"Pm{a}"], ["pb1"])
                            CP("act", Qm[b][:L, 0:4 * L], PB[1][:L, 0:4 * L], ["pb1"], [f"Qm{b}"])
                            if not lastlev:
                                for hh in range(4):
                                    sl = slice(hh * L, (hh + 1) * L)
                                    MM(PB[0][:L, sl], Qm[a][:L, sl], Pm[a][:L, sl], True, True, [f"Qm{a}", f"Pm{a}"], ["pb0"])
                                CP("dve", Pm[b][:L, 0:4 * L], PB[0][:L, 0:4 * L], ["pb0"], [f"Pm{b}"])
                            for hh in range(4):
                                sl = slice(hh * L, (hh + 1) * L)
                                MM(PB[2 + lev % 2][:L, sl], Qm[b][:L, sl], Rm[:L, sl], True, True, ["Rm", f"Qm{b}"], [f"pb{2 + lev % 2}"])
                            TT("dve", Rm[:L, 0:4 * L], Rm[:L, 0:4 * L], PB[2 + lev % 2][:L, 0:4 * L], ALU.add,
                               ["Rm", f"pb{2 + lev % 2}"], ["Rm"])
                        CP("act", TTb[:L, 0:4 * L], Rm[:L, 0:4 * L], ["Rm"], ["TTb"])
                        if dbg and dbg <= 7:
                            break
                        S.phase = "gdn_phIII"
                        for pr in range(2):
                            HH = (2 * pr, 2 * pr + 1)

                            def ctx(hh):
                                h = hb * 4 + hh
                                p = hh % 2
                                return h, p, 4 + p, 6 + p, f"pb{4 + p}", f"pb{6 + p}", slice(hh * L, (hh + 1) * L)
                            for hh in HH:
                                h, p, bA, bB, kA, kB, sl = ctx(hh)
                                MM(PB[bA][:, 0:L], kbg[hh][:L, :], TTb[:L, sl], True, True, [f"kbg{hh}", "TTb"], [kA])
                            for hh in HH:
                                h, p, bA, bB, kA, kB, sl = ctx(hh)
                                ACT(negw[p][:, :L], PB[bA][:, 0:L], AF.Copy, [kA], [f"negw{p}"], scale=-1.0)
                            for hh in HH:
                                h, p, bA, bB, kA, kB, sl = ctx(hh)
                                MM(PB[bA][:L, 128:384], TTb[:L, sl], bv[hh][:L, :], True, False, [f"bv{hh}", "TTb"], [kA])
                                MM(PB[bA][:L, 128:384], negw[p][:, :L], Sbf[hh][:, :], False, True, [f"negw{p}", f"Sbf{hh}"], [kA])
                            for hh in HH:
                                h, p, bA, bB, kA, kB, sl = ctx(hh)
                                CP("dve", vnew[p][:L, :], PB[bA][:L, 128:384], [kA], [f"vnew{p}"])
                            for hh in HH:
                                h, p, bA, bB, kA, kB, sl = ctx(hh)
                                for vt in range(2):
                                    MM(PB[bB][:, vt * 128:vt * 128 + L], Sbf[hh][:, vt * 128:(vt + 1) * 128], qg[:, h, :L], True, False,
                                       [f"Sbf{hh}", "qg"], [kB])
                                    MM(PB[bB][:, vt * 128:vt * 128 + L], vnew[p][:L, vt * 128:(vt + 1) * 128], QKd[hh][:L, :L], False, True,
                                       [f"vnew{p}", f"QKd{hh}"], [kB])
                                MM(PB[bB][:, 256:512], kgt[hh][:L, :], vnew[p][:L, :], True, True, [f"kgt{hh}", f"vnew{p}"], [kB])
                            for hh in HH:
                                h, p, bA, bB, kA, kB, sl = ctx(hh)
                                CP("act", cf[:, 2 * h:2 * h + 2, :L], bank3(bB, 2, L), [kB], [f"cf{2 * h}", f"cf{2 * h + 1}"])
                            for hh in HH:
                                h, p, bA, bB, kA, kB, sl = ctx(hh)
                                STT(Sg[:, h, :], Sg[:, h, :], egl[:, h:h + 1], PB[bB][:, 256:512], ALU.mult, ALU.add,
                                    [f"Sg{h}", "egl", kB], [f"Sg{h}"])

                    if dbg and dbg <= 8:
                        break
                    S.phase = "gdn_out"
                    cfall = [f"cf{t}" for t in range(16)]
                    for hf in range(2):
                        TT("pool", sq[:, :, :L], cf[:, 8 * hf:8 * hf + 8, :L], cf[:, 8 * hf:8 * hf + 8, :L], ALU.mult, cfall, ["sq"])
                        bk = 0 if hf == 0 else 3
                        for hh in range(4):
                            for vt in range(2):
                                MM(PB[bk][:, hh * 128:hh * 128 + L], ones_bf[:, :], sq[:, 2 * hh + vt, :L], vt == 0, vt == 1,
                                   ["sq", "const"], [f"pb{bk}"])
                        ACT(rstd[:, 4 * hf:4 * hf + 4, :L], bank3(bk, 4, L), AF.Ln, [f"pb{bk}", "const"], ["rstd"],
                            scale=1.0 / GDN_DV, bias=epsb[:, 0:1])
                    ACT(rstd[:, :, :L], rstd[:, :, :L], AF.Exp, ["rstd"], ["rstd"], scale=-0.5)
                    for bi in range(4):
                        bk = 1 + bi % 2
                        zb = zsb[bi % 2]
                        for sl_ in range(4):
                            zt = bi * 4 + sl_
                            c0 = 4096 + zt * 128
                            for kt in range(8):
                                MM(PB[bk][:, sl_ * 128:sl_ * 128 + L], Win[:, kt, c0:c0 + 128], xn[:, kt, :L], kt == 0, kt == 7,
                                   ["xn", f"Win{kt}"], [f"pb{bk}"])
                        ACT(zb[:, :, :L], bank3(bk, 4, L), AF.Silu, [f"pb{bk}"], [f"Pm{bi % 2}"])
                        for sl_ in range(4):
                            zt = bi * 4 + sl_
                            STT(cf[:, zt, :L], cf[:, zt, :L], onw[:, zt % 2:zt % 2 + 1], rstd[:, zt // 2, :L], ALU.mult, ALU.mult,
                                [f"cf{zt}", "rstd", "lw"], [f"cf{zt}"])
                        dsto = (ogn if bi < 2 else ogn2)[:, (bi % 2) * 4:(bi % 2) * 4 + 4, :L]
                        TT("dve", dsto, cf[:, bi * 4:bi * 4 + 4, :L], zb[:, :, :L], ALU.mult,
                           [f"cf{bi * 4 + i}" for i in range(4)] + [f"Pm{bi % 2}"], ["qn" if bi < 2 else "kn"])
                    for hf in range(2):
                        bk = 1 + hf
                        for sl_ in range(4):
                            dt_ = hf * 4 + sl_
                            for kt in range(16):
                                src_o = (ogn if kt < 8 else ogn2)[:, kt % 8, :L]
                                MM(PB[bk][:, sl_ * 128:sl_ * 128 + L], Wout[:, kt, dt_ * 128:(dt_ + 1) * 128], src_o,
                                   kt == 0, kt == 15, ["qn", "kn", f"Wout{kt}"], [f"pb{bk}"])
                        TT("dve", ht[:, 4 * hf:4 * hf + 4, :L], ht[:, 4 * hf:4 * hf + 4, :L], bank3(bk, 4, L), ALU.add,
                           ["ht", f"pb{bk}"], ["ht"])
                    S.dma("sp", hbuf[hdst][:, :, off:off + L], ht[:, :, :L], reads=["ht"], writes=[f"h{hdst}_{ci}"])
                store_state(o_pgdn, o_pgdnc)

        cur = 0
        if dbg and dbg > 100:
            S.stop_at = dbg
        S.stopped = False
        for li in range(NL):
            if li % 2 == 0:
                ssd_layer(li // 2, cur, 1 - cur, li == NL - 1)
            else:
                gdn_layer(li // 2, cur, 1 - cur, li == NL - 1)
            cur = 1 - cur
            if S.stopped:
                print("stopped at", S.n_inst)
                break

        S.phase = "epilogue"
        with ExitStack() as es:
            sb = mk(es)
            htile = [sb(f"htile{i}", [128, 8, 128]) for i in range(2)]
            sq = [sb(f"sq{i}", [128, 8, 128], BF16) for i in range(2)]
            rstd = [sb(f"rstd{i}", [128, 128]) for i in range(2)]
            ytile = [sb(f"ytile{i}", [128, 8, 128]) for i in range(2)]
            ytok = [sb(f"ytok{i}", [128, D]) for i in range(2)]
            for ci, (stream, off, L) in enumerate(chunks):
                if stream == "p" and off == 0:
                    continue
                b = ci % 2
                S.dma("sp", htile[b][:, :, :L], hbuf[cur][:, :, off:off + L], reads=[f"h{cur}_{ci}"], writes=[f"htile{b}"])
                TT("pool", sq[b][:, :, :L], htile[b][:, :, :L], htile[b][:, :, :L], ALU.mult, [f"htile{b}"], [f"sq{b}"])
                bk = 4 + b
                for kt in range(8):
                    MM(PB[bk][:, :L], ones_bf[:, :], sq[b][:, kt, :L], kt == 0, kt == 7, [f"sq{b}", "const"], [f"pb{bk}"])
                ACT(rstd[b][:, :L], PB[bk][:, :L], AF.Ln, [f"pb{bk}", "const"], [f"rstd{b}"], scale=1.0 / D, bias=epsb[:, 0:1])
                ACT(rstd[b][:, :L], rstd[b][:, :L], AF.Exp, [f"rstd{b}"], [f"rstd{b}"], scale=-0.5)
                for kt in range(8):
                    STT(ytile[b][:, kt, :L], htile[b][:, kt, :L], fnw[:, kt:kt + 1], rstd[b][:, :L], ALU.mult, ALU.mult,
                        [f"htile{b}", "const", f"rstd{b}"], [f"ytile{b}"])
                for kt in range(8):
                    bk2 = 2 * b + kt // 4
                    TR(PB[bk2][:L, (kt % 4) * 128:(kt % 4 + 1) * 128], ytile[b][:, kt, :L], ident[:, :],
                       [f"ytile{b}", "const"], [f"pb{bk2}"])
                for half in range(2):
                    bk2 = 2 * b + half
                    CP("act" if half else "dve", ytok[b][:L, 512 * half:512 * half + 512], PB[bk2][:L, :], [f"pb{bk2}"], [f"ytok{b}"])
                dst = y_s[:, :] if stream == "s" else y_p[off - 16: off - 16 + L, :]
                S.dma("sp", dst, ytok[b][:L, :], reads=[f"ytok{b}"], writes=[f"yout{ci}"])
        S.finish("sp")
        print("instructions:", S.n_inst, "sems:", S.nsem)
    return nc


_PROG_CACHE = {}

_WNAMES = ["ssd_norm_w", "ssd_w_in", "ssd_conv_w", "ssd_conv_b", "ssd_dt_bias", "ssd_a_log", "ssd_d", "ssd_gnorm_w",
           "ssd_w_out", "gdn_norm_w", "gdn_w_in", "gdn_conv_w", "gdn_dt_bias", "gdn_a_log", "gdn_onorm_w", "gdn_w_out",
           "final_norm_w"]


def kernel(**inputs):
    f32 = lambda a: np.ascontiguousarray(np.asarray(a), dtype=np.float32)
    x_prompt = f32(inputs["x_prompt"])
    x_sample = f32(inputs["x_sample"])
    B, SEQ, _ = x_prompt.shape
    NS = x_sample.shape[0]
    n = 8
    assert NS == n and B <= n
    if SEQ not in _PROG_CACHE:
        _PROG_CACHE[SEQ] = build_program(SEQ)
    nc = _PROG_CACHE[SEQ]
    wts = {k: f32(inputs[k]) for k in _WNAMES}
    st = {k: f32(inputs[k]) for k in ["state_ssd", "state_ssd_conv", "state_gdn", "state_gdn_conv"]}
    meta = f32(inputs["meta_tokens"])
    in_maps = []
    for c in range(n):
        m = {"x_prompt": x_prompt[c % B], "x_sample": x_sample[c], "meta_tokens": meta}
        for k, v in st.items():
            m[k] = np.ascontiguousarray(v[:, c])
        m.update(wts)
        in_maps.append(m)
    res = run_bass_kernel_spmd(nc, in_maps, core_ids=list(range(n)))
    R = res.results
    y_prompt = np.stack([R[b]["y_prompt"] for b in range(B)])
    y_sample = np.stack([R[c]["y_sample"] for c in range(n)])
    outs = [y_prompt, y_sample]
    for nm in ["p_ssd", "p_ssdc", "p_gdn", "p_gdnc"]:
        outs.append(np.stack([R[b][nm] for b in range(B)], axis=1))
    for nm in ["s_ssd", "s_ssdc", "s_gdn", "s_gdnc"]:
        outs.append(np.stack([R[c][nm] for c in range(n)], axis=1))
    return tuple(np.ascontiguousarray(o, dtype=np.float32) for o in outs)
```

```python
import os
import numpy as np
from contextlib import ExitStack
import concourse.bass as bass
import concourse.mybir as mybir
from concourse.bass_utils import run_bass_kernel_spmd

F32 = mybir.dt.float32
BF16 = mybir.dt.bfloat16
AF = mybir.ActivationFunctionType
ALU = mybir.AluOpType

D = 1024
NMETA = 16
EPS = 1e-6
SSD_INNER = 2048
SSD_H = 32
SSD_P = 64
SSD_G = 4
SSD_N = 128
SSD_CONV = 3072
SSD_IN = 5152
GDN_H = 8
GDN_DK = 128
GDN_DV = 256
GDN_KEY = 1024
GDN_VAL = 2048
GDN_CONV = 4096
GDN_IN = 6160
NEG = -30000.0
NTMP = 4


class Sched:
    ROT = int(os.environ.get('KROT', '28000'))

    def __init__(self, nc, es):
        self.nc = nc
        self.es = es
        self.eng = {"pe": nc.tensor, "act": nc.scalar, "dve": nc.vector, "pool": nc.gpsimd, "sp": nc.sync}
        self.nsem = 0
        self.sems = []
        self.cur = {}
        for e in self.eng:
            self.cur[e] = [self._new_sem(), 0]
        self.dma_slots = {"hw": [[self._new_sem(), 0] for _ in range(12)],
                          "sw": [[self._new_sem(), 0] for _ in range(8)]}
        self.dma_rr = {"hw": 0, "sw": 0}
        self.waited = {e: {} for e in self.eng}
        self.last_w = {}
        self.readers = {}
        self.n_inst = 0

    def _new_sem(self):
        s = self.es.enter_context(self.nc.semaphore(f"sm{self.nsem}"))
        self.nsem += 1
        self.sems.append(s)
        return len(self.sems) - 1

    def _wait(self, e, deps):
        best = {}
        for (si, v) in deps:
            if best.get(si, 0) < v:
                best[si] = v
        for si, v in best.items():
            if e == "pe" and si == self.cur["pe"][0]:
                continue
            if NOSELF and si == self.cur[e][0]:
                continue
            if self.waited[e].get(si, 0) >= v:
                continue
            self.eng[e].wait_ge(self.sems[si], v)
            self.waited[e][si] = v

    def _deps(self, reads, writes):
        deps = []
        for k in reads:
            if k in self.last_w:
                deps.append(self.last_w[k])
        for k in writes:
            if k in self.last_w:
                deps.append(self.last_w[k])
            deps.extend(self.readers.get(k, ()))
        return deps

    def _stamp(self, stamp, reads, writes):
        for k in reads:
            r = self.readers.setdefault(k, {})
            if r.get(stamp[0], 0) < stamp[1]:
                r[stamp[0]] = stamp[1]
        for k in writes:
            self.last_w[k] = stamp
            self.readers[k] = {}

    def _deps2(self, reads, writes, e=None):
        deps = []
        for k in reads:
            if k in self.last_w:
                deps.append(self.last_w[k])
            if k.startswith("pb") and e is not None:
                own = self.cur[e][0]
                deps.extend((si, v) for si, v in self.readers.get(k, {}).items() if si != own)
        for k in writes:
            if k in self.last_w:
                deps.append(self.last_w[k])
            deps.extend(self.readers.get(k, {}).items())
        return deps

    stop_at = None
    names = None
    phase = "pre"

    def op(self, e, fn, reads=(), writes=()):
        if self.stop_at and self.n_inst >= self.stop_at:
            raise _Stop()
        self._wait(e, self._deps2(reads, writes, e))
        c = self.cur[e]
        if c[1] >= self.ROT:
            c[0] = self._new_sem()
            c[1] = 0
        inst = fn(self.eng[e])
        c[1] += 1
        if self.names is not None:
            try:
                self.names[str(inst.ins.name)] = self.phase
            except Exception:
                pass
        inst.then_inc(self.sems[c[0]], 1)
        self._stamp((c[0], c[1]), reads, writes)
        self.n_inst += 1
        return inst

    def dma(self, e, out, in_, reads=(), writes=(), slow=False):
        self._wait(e, self._deps2(reads, writes))
        kind = "sw" if e == "pool" else "hw"
        slot = self.dma_slots[kind][self.dma_rr[kind]]
        self.dma_rr[kind] = (self.dma_rr[kind] + 1) % len(self.dma_slots[kind])
        if slot[1] >= self.ROT:
            self._wait(e, [(slot[0], slot[1])])
            slot[0] = self._new_sem()
            slot[1] = 0
        if slot[1] > 0:
            self._wait(e, [(slot[0], slot[1])])
        if slow:
            inst = self.eng[e].dma_start(out=out, in_=in_, allow_slow_non_contiguous=True)
        else:
            inst = self.eng[e].dma_start(out=out, in_=in_)
        slot[1] += 16
        inst.then_inc(self.sems[slot[0]], 16)
        self._stamp((slot[0], slot[1]), reads, writes)
        self.n_inst += 1
        return inst

    def barrier(self):
        stamps = [(s[0], s[1]) for kind in self.dma_slots for s in self.dma_slots[kind] if s[1] > 0]
        for k, c in self.cur.items():
            if c[1] > 0:
                stamps.append((c[0], c[1]))
        for e in self.eng:
            self._wait(e, [st for st in stamps if st[0] != self.cur[e][0]])

    def finish(self, e="sp"):
        deps = [(s[0], s[1]) for kind in self.dma_slots for s in self.dma_slots[kind] if s[1] > 0]
        for k, c in self.cur.items():
            if c[1] > 0 and k != e:
                deps.append((c[0], c[1]))
        self._wait(e, deps)


SKIP = os.environ.get('KSKIP', '').split(',')
NOSELF = os.environ.get('KNOSELF', '0') == '1'


class _Stop(Exception):
    pass


def build_program(SEQ, NL=4, dbg=False):
    nc = bass.Bass("TRN2", target_bir_lowering=False)
    TP = NMETA + SEQ
    assert SEQ % 128 == 0
    NCH = SEQ // 128
    TT = TP + 16
    chunks = [("s", TP, 16)] + [("p", 0, 16)] + [("p", 16 + 128 * i, 128) for i in range(NCH)]

    din = {}

    def inp(name, shape):
        din[name] = nc.dram_tensor(name, list(shape), F32, kind="ExternalInput").ap()
        return din[name]

    def outp(name, shape):
        return nc.dram_tensor(name, list(shape), F32, kind="ExternalOutput").ap()

    xp = inp("x_prompt", [SEQ, D])
    xs = inp("x_sample", [16, D])
    meta = inp("meta_tokens", [NMETA, D])
    st_ssd = inp("state_ssd", [2, SSD_H, SSD_P, SSD_N])
    st_ssdc = inp("state_ssd_conv", [2, 3, SSD_CONV])
    st_gdn = inp("state_gdn", [2, GDN_H, GDN_DK, GDN_DV])
    st_gdnc = inp("state_gdn_conv", [2, 3, GDN_CONV])
    W = {}
    for nm, shp in [("ssd_norm_w", [2, D]), ("ssd_w_in", [2, D, SSD_IN]), ("ssd_conv_w", [2, 4, SSD_CONV]),
                    ("ssd_conv_b", [2, SSD_CONV]), ("ssd_dt_bias", [2, SSD_H]), ("ssd_a_log", [2, SSD_H]),
                    ("ssd_d", [2, SSD_H]), ("ssd_gnorm_w", [2, SSD_INNER]), ("ssd_w_out", [2, SSD_INNER, D]),
                    ("gdn_norm_w", [2, D]), ("gdn_w_in", [2, D, GDN_IN]), ("gdn_conv_w", [2, 4, GDN_CONV]),
                    ("gdn_dt_bias", [2, GDN_H]), ("gdn_a_log", [2, GDN_H]), ("gdn_onorm_w", [2, GDN_DV]),
                    ("gdn_w_out", [2, GDN_VAL, D]), ("final_norm_w", [D])]:
        W[nm] = inp(nm, shp)

    y_p = outp("y_prompt", [SEQ, D])
    y_s = outp("y_sample", [16, D])
    o_pssd = outp("p_ssd", [2, SSD_H, SSD_P, SSD_N])
    o_pssdc = outp("p_ssdc", [2, 3, SSD_CONV])
    o_pgdn = outp("p_gdn", [2, GDN_H, GDN_DK, GDN_DV])
    o_pgdnc = outp("p_gdnc", [2, 3, GDN_CONV])
    o_sssd = outp("s_ssd", [2, SSD_H, SSD_P, SSD_N])
    o_sssdc = outp("s_ssdc", [2, 3, SSD_CONV])
    o_sgdn = outp("s_gdn", [2, GDN_H, GDN_DK, GDN_DV])
    o_sgdnc = outp("s_gdnc", [2, 3, GDN_CONV])

    hbuf = [nc.dram_tensor(f"hbuf{i}", [128, 8, TT], F32, kind="Internal").ap() for i in range(2)]
    dbg_o = outp("dbg_o", [128, 1024]) if dbg else None

    with ExitStack() as es0:
        S = Sched(nc, es0)

        uid = [0]

        def mk(es):
            def sb(name, shape, dt=F32):
                uid[0] += 1
                return es.enter_context(nc.sbuf_tensor(f"{name}_u{uid[0]}", list(shape), dt))
            return sb

        sb0 = mk(es0)

        def nfree(ap):
            dims = [list(d) for d in ap.ap][1:]
            dims = [d for d in dims if d[1] != 1]
            merged = []
            for d in dims:
                if merged and merged[-1][0] == d[0] * d[1]:
                    merged[-1] = [d[0], merged[-1][1] * d[1]]
                else:
                    merged.append(d)
            return len(merged)

        def MM(out, lhsT, rhs, start, stop, r, w):
            assert nfree(rhs) <= 1 and nfree(lhsT) <= 1, (nfree(rhs), nfree(lhsT), rhs, lhsT)
            S.op("pe", lambda e: e.matmul(out, lhsT=lhsT, rhs=rhs, start=start, stop=stop), reads=r, writes=w)

        def TR(out, in_, idn, r, w):
            assert nfree(in_) <= 1 and nfree(idn) <= 1, (nfree(in_), in_)
            S.op("pe", lambda e: e.transpose(out, in_, idn), reads=r, writes=w)

        def ACT(out, in_, func, r, w, scale=None, bias=None):
            kw = {}
            if scale is not None:
                kw["scale"] = scale
            if bias is not None:
                kw["bias"] = bias
            S.op("act", lambda e: e.activation(out=out, in_=in_, func=func, **kw), reads=r, writes=w)

        def TT(eng, out, in0, in1, op, r, w):
            S.op(eng, lambda e: e.tensor_tensor(out=out, in0=in0, in1=in1, op=op), reads=r, writes=w)

        def TS(eng, out, in0, s1, op0, r, w, s2=None, op1=None):
            if op1 is None:
                S.op(eng, lambda e: e.tensor_scalar(out=out, in0=in0, scalar1=s1, scalar2=None, op0=op0), reads=r, writes=w)
            else:
                S.op(eng, lambda e: e.tensor_scalar(out=out, in0=in0, scalar1=s1, scalar2=s2, op0=op0, op1=op1),
                     reads=r, writes=w)

        def STT(out, in0, scalar, in1, op0, op1, r, w):
            S.op("dve", lambda e: e.scalar_tensor_tensor(out=out, in0=in0, scalar=scalar, in1=in1, op0=op0, op1=op1),
                 reads=r, writes=w)

        def CP(eng, out, in_, r, w):
            if eng == "act":
                ACT(out, in_, AF.Copy, r, w)
            else:
                S.op(eng, lambda e: e.tensor_copy(out=out, in_=in_), reads=r, writes=w)

        def MEMSET(eng, ap, val, w):
            S.op(eng, lambda e: e.memset(ap, val), writes=w)

        ident = sb0("ident", [128, 128])
        ident_bf = sb0("ident_bf", [128, 128], BF16)
        ones_bf = sb0("ones_bf", [128, 128], BF16)
        ones_f = sb0("ones_f", [128, 128])
        tri = sb0("tri", [128, 128])
        incl = sb0("incl", [128, 128])
        mneg_f = sb0("mneg_f", [128, 512])
        mnegL = {128: sb0("mneg128", [128, 512], BF16), 16: sb0("mneg16", [128, 64], BF16)}
        mnegsL = {128: sb0("mnegs128", [128, 512], BF16), 16: sb0("mnegs16", [128, 64], BF16)}
        epsb = sb0("epsb", [128, 1])
        oneb = sb0("oneb", [128, 1])
        fnw = sb0("fnw", [128, 8])
        CONST = ["const"]
        MEMSET("pool", ident[:], 1.0, CONST)
        S.op("pool", lambda e: e.affine_select(out=ident[:], in_=ident[:], pattern=[[1, 128]], compare_op=ALU.is_equal,
                                               fill=0.0, base=0, channel_multiplier=-1), reads=CONST, writes=CONST)
        CP("pool", ident_bf[:], ident[:], CONST, CONST)
        MEMSET("pool", ones_bf[:], 1.0, CONST)
        MEMSET("pool", ones_f[:], 1.0, CONST)
        MEMSET("pool", tri[:], 1.0, CONST)
        S.op("pool", lambda e: e.affine_select(out=tri[:], in_=tri[:], pattern=[[-1, 128]], compare_op=ALU.is_ge,
                                               fill=0.0, base=-1, channel_multiplier=1), reads=CONST, writes=CONST)
        MEMSET("pool", incl[:], 1.0, CONST)
        S.op("pool", lambda e: e.affine_select(out=incl[:], in_=incl[:], pattern=[[1, 128]], compare_op=ALU.is_ge,
                                               fill=0.0, base=0, channel_multiplier=-1), reads=CONST, writes=CONST)
        for LL in (128, 16):
            for strict, dst in ((False, mnegL[LL]), (True, mnegsL[LL])):
                MEMSET("pool", mneg_f[:, 0:4 * LL], 0.0, CONST)
                S.op("pool", lambda e: e.affine_select(out=mneg_f[:, 0:4 * LL], in_=mneg_f[:, 0:4 * LL],
                                                       pattern=[[0, 4], [1, LL]],
                                                       compare_op=(ALU.is_gt if strict else ALU.is_ge), fill=NEG, base=0,
                                                       channel_multiplier=-1), reads=CONST, writes=CONST)
                CP("pool", dst[:, :], mneg_f[:, 0:4 * LL], CONST, CONST)
        MEMSET("pool", epsb[:], EPS, CONST)
        MEMSET("pool", oneb[:], 1.0, CONST)
        S.dma("sp", fnw[:], W["final_norm_w"].rearrange("(k p) -> p k", p=128), writes=CONST, slow=True)

        PB = [es0.enter_context(nc.psum_tensor(f"pb{i}", [128, 512], F32)) for i in range(8)]

        def bank3(i, n, L):
            return PB[i][:, 0:n * 128].rearrange("p (k t) -> p k t", k=n)[:, :, :L]

        S.phase = "prologue"
        with ExitStack() as es:
            sb = mk(es)
            xtok = [sb(f"xtok{i}", [128, D]) for i in range(2)]
            htile = [sb(f"htile{i}", [128, 8, 128]) for i in range(2)]
            for ci, (stream, off, L) in enumerate(chunks):
                b = ci % 2
                if stream == "s":
                    src = xs[:, :]
                elif off == 0:
                    src = meta[:, :]
                else:
                    src = xp[off - 16: off - 16 + L, :]
                S.dma("sp", xtok[b][:L, :], src, writes=[f"xtok{b}"])
                for kt in range(8):
                    bk = 2 * b + kt // 4
                    TR(PB[bk][:, (kt % 4) * 128:(kt % 4) * 128 + L], xtok[b][:L, kt * 128:(kt + 1) * 128], ident[:L, :L],
                       [f"xtok{b}", "const"], [f"pb{bk}"])
                for half in range(2):
                    bk = 2 * b + half
                    CP("act" if half else "dve", htile[b][:, 4 * half:4 * half + 4, :L], bank3(bk, 4, L),
                       [f"pb{bk}"], [f"htile{b}"])
                S.dma("sp", hbuf[0][:, :, off:off + L], htile[b][:, :, :L], reads=[f"htile{b}"], writes=[f"h0_{ci}"])
        S.barrier()

        def front(sb_t, hsrc, ci, off, L, normw, sqk=("sq",), rsk=("rstd",)):
            S.phase = "front"
            sqk, rsk = list(sqk), list(rsk)
            ht, sq, rstd, xn = sb_t
            S.dma("sp", ht[:, :, :L], hbuf[hsrc][:, :, off:off + L], reads=[f"h{hsrc}_{ci}"], writes=["ht"])
            TT("pool", sq[:, :, :L], ht[:, :, :L], ht[:, :, :L], ALU.mult, ["ht"], sqk)
            for kt in range(8):
                MM(PB[0][:, :L], ones_bf[:, :], sq[:, kt, :L], kt == 0, kt == 7, sqk + ["const"], ["pb0"])
            ACT(rstd[:, :L], PB[0][:, :L], AF.Ln, ["pb0", "const"], rsk, scale=1.0 / D, bias=epsb[:, 0:1])
            ACT(rstd[:, :L], rstd[:, :L], AF.Exp, rsk, rsk, scale=-0.5)
            for kt in range(8):
                STT(xn[:, kt, :L], ht[:, kt, :L], normw[:, kt:kt + 1], rstd[:, :L], ALU.mult, ALU.mult,
                    ["ht", "lw"] + rsk, ["xn"])

        def proj_conv_b(Win, xn, tiles, col0, L, ctall, nt, halo, convw, convb, dst4, banks=(1, 2)):
            nb = len(tiles) // 4
            skew = nt >= 8

            def tset(bi):
                s0 = (bi % 2) * 4 if skew else 0
                v = ctall[:, s0 * 272:(s0 + 4) * 272].rearrange("p (c w) -> p c w", c=4)
                return s0, v, [f"ctmp{s0 + i}" for i in range(4)]

            def stA(bi):
                bk = banks[bi % 2]
                ct0 = tiles[bi * 4]
                s0, v, keys = tset(bi)
                for sl in range(4):
                    ct = tiles[bi * 4 + sl]
                    assert ct == ct0 + sl
                    c0 = col0 + ct * 128
                    for kt in range(8):
                        MM(PB[bk][:, sl * 128:sl * 128 + L], Win[:, kt, c0:c0 + 128], xn[:, kt, :L], kt == 0, kt == 7,
                           ["xn", f"Win{kt}"], [f"pb{bk}"])
                CP("act", v[:, :, 3:3 + L], bank3(bk, 4, L), [f"pb{bk}"], keys)
                CP("pool", v[:, :, 0:3], halo[:, :, ct0:ct0 + 4].rearrange("p k c -> p c k"), ["halo"], keys)

            def stA2(bi):
                ct0 = tiles[bi * 4]
                s0, v, keys = tset(bi)
                for sl in range(4):
                    ct = ct0 + sl
                    if convb is not None:
                        ACT(v[:, sl, 136:136 + L], v[:, sl, 0:L], AF.Identity, [keys[sl], "lw"], [keys[sl]],
                            scale=convw[:, ct, 0:1], bias=convb[:, ct:ct + 1])
                    else:
                        ACT(v[:, sl, 136:136 + L], v[:, sl, 0:L], AF.Identity, [keys[sl], "lw"], [keys[sl]],
                            scale=convw[:, ct, 0:1])

            def stB(bi):
                ct0 = tiles[bi * 4]
                s0, v, keys = tset(bi)
                for k in range(1, 4):
                    for sl in range(4):
                        ct = ct0 + sl
                        acc = v[:, sl, 136:136 + L]
                        STT(acc, v[:, sl, k:k + L], convw[:, ct, k:k + 1], acc, ALU.mult, ALU.add, [keys[sl], "lw"], [keys[sl]])
                CP("pool", halo[:, :, ct0:ct0 + 4].rearrange("p k c -> p c k"), v[:, :, L:L + 3], keys, ["halo"])

            def stC(bi):
                ct0 = tiles[bi * 4]
                s0, v, keys = tset(bi)
                out_ap, okeys = dst4(ct0)
                ACT(out_ap, v[:, :, 136:136 + L], AF.Silu, keys, okeys)

            if skew:
                stA(0)
                stA2(0)
                for bi in range(nb):
                    if bi + 1 < nb:
                        stA(bi + 1)
                        stA2(bi + 1)
                    stB(bi)
                    stC(bi)
            else:
                for bi in range(nb):
                    stA(bi)
                    stA2(bi)
                    stB(bi)
                    stC(bi)

        def ssd_layer(j, hsrc, hdst, last):
            with ExitStack() as es:
                try:
                    ssd_body(mk(es), j, hsrc, hdst, last)
                except _Stop:
                    S.stop_at = None
                    S.stopped = True
            S.barrier()

        def ssd_body(sb, j, hsrc, hdst, last):
            if True:
                Win = sb("Win", [128, 8, SSD_IN], BF16)
                Wout = sb("Wout", [128, 16, D], BF16)
                normw = sb("normw", [128, 8])
                convw = sb("convw", [128, 24, 4])
                convb = sb("convb", [128, 24])
                dtb = sb("dtb", [32, 1])
                aneg = sb("aneg", [32, 1])
                dsk = sb("dsk", [128, 16])
                gnw = sb("gnw", [128, 16])
                St = sb("St", [128, 2048])
                Sbf = sb("Sbf", [128, 2048], BF16)
                halo = sb("halo", [128, 3, 24])
                ht = sb("ht", [128, 8, 128])
                sq = sb("sq", [128, 16, 128], BF16)
                rstd = sb("rstd", [128, 4, 128])
                xn = sb("xn", [128, 8, 128], BF16)
                zs = sb("zs", [128, 16, 128])
                ctall = sb("ctall", [128, 8 * 272])
                tmp = [ctall[:, i * 272:(i + 1) * 272] for i in range(8)]
                xcf = sb("xcf", [128, 24, 128])
                bcbf2 = [sb(f"bcbf{i}", [128, 2, 128], BF16) for i in range(2)]
                dtT = sb("dtT", [32, 5, 128])
                onesr = sb("onesr", [32, 128])
                tokm = sb("tokm", [128, 5, 32])
                edec = sb("edec", [128, 32])
                xdt2 = [sb(f"xdt{i}", [128, 512], BF16) for i in range(2)]
                xde2 = [sb(f"xde{i}", [128, 512], BF16) for i in range(2)]
                btok2 = [sb(f"btok{i}", [128, 128], BF16) for i in range(2)]
                cbT = sb("cbT", [128, 128])
                Dq = sb("Dq", [128, 1024])
                dec = sb("dec", [128, 1024])
                WT2 = [sb(f"WT{i}", [128, 1024], BF16) for i in range(2)]
                ytok = sb("ytok", [128, 512])
                yT = sb("yT", [128, 16, 128])
                ygn = sq
                stg = yT
                hstg = sb("hstg", [72, 128])

                for kt in range(8):
                    S.dma("pool", Win[:, kt, :], W["ssd_w_in"][j, kt * 128:(kt + 1) * 128, :], writes=[f"Win{kt}"])
                for kt in range(16):
                    S.dma("pool", Wout[:, kt, :], W["ssd_w_out"][j, kt * 128:(kt + 1) * 128, :], writes=[f"Wout{kt}"])
                LW = ["lw"]
                S.dma("sp", normw[:], W["ssd_norm_w"][j].rearrange("(k p) -> p k", p=128), writes=LW, slow=True)
                for k in range(4):
                    S.dma("sp", convw[:, :, k], W["ssd_conv_w"][j, k].rearrange("(c p) -> p c", p=128), writes=LW, slow=True)
                S.dma("sp", convb[:], W["ssd_conv_b"][j].rearrange("(c p) -> p c", p=128), writes=LW, slow=True)
                S.dma("sp", dtb[:], W["ssd_dt_bias"][j].rearrange("(h o) -> h o", o=1), writes=LW, slow=True)
                S.dma("sp", aneg[:], W["ssd_a_log"][j].rearrange("(h o) -> h o", o=1), writes=LW, slow=True)
                S.dma("sp", gnw[:], W["ssd_gnorm_w"][j].rearrange("(c p) -> p c", p=128), writes=LW, slow=True)
                d2 = W["ssd_d"][j].rearrange("(c two) -> two c", two=2)
                for two in range(2):
                    S.dma("sp", dsk[two * 64:(two + 1) * 64, :], d2[two].partition_broadcast(64), writes=LW, slow=True)
                ACT(aneg[:], aneg[:], AF.Exp, LW, LW)
                TS("dve", aneg[:], aneg[:], -1.0, ALU.mult, LW, LW)
                MEMSET("pool", onesr[:], 1.0, LW)
                WinK = [f"Win{kt}" for kt in range(8)]
                WoutK = [f"Wout{kt}" for kt in range(16)]

                def store_state(o_state, o_conv):
                    S.barrier()
                    for c in range(16):
                        bk = 4 + (c // 4) % 2
                        TR(PB[bk][:, (c % 4) * 128:(c % 4 + 1) * 128], St[:, c * 128:(c + 1) * 128], ident[:, :],
                           ["St", "const"], [f"pb{bk}"])
                        if c % 4 == 3:
                            CP("act" if (c // 4) % 2 else "dve", stg[:, c - 3:c + 1, :], bank3(bk, 4, 128), [f"pb{bk}"], ["stg"])
                    S.dma("sp", o_state[j].rearrange("(c two) p n -> (two p) c n", two=2), stg[:, :, :], reads=["stg"],
                          writes=["ostate"])
                    TR(PB[6][:72, :128], halo[:, :, :].rearrange("p k c -> p (k c)"), ident[:, :], ["halo", "const"], ["pb6"])
                    CP("dve", hstg[:, :], PB[6][:72, :128], ["pb6"], ["hstg"])
                    for k in range(3):
                        S.dma("sp", o_conv[j, k].rearrange("(c p) -> c p", p=128), hstg[k * 24:(k + 1) * 24, :],
                              reads=["hstg"], writes=["oconv"])
                    S.barrier()

                prev_stream = None
                for ci, (stream, off, L) in enumerate(chunks):
                    if stream != prev_stream:
                        if prev_stream == "s":
                            store_state(o_sssd, o_sssdc)
                        if stream == "s":
                            S.dma("sp", stg[:, :, :], st_ssd[j].rearrange("(c two) p n -> (two p) c n", two=2),
                                  reads=[], writes=["stg"])
                            for c in range(16):
                                bk = 4 + (c // 4) % 2
                                TR(PB[bk][:, (c % 4) * 128:(c % 4 + 1) * 128], stg[:, c, :], ident[:, :],
                                   ["stg", "const"], [f"pb{bk}"])
                                if c % 4 == 3:
                                    CP("act" if (c // 4) % 2 else "dve", St[:, (c - 3) * 128:(c + 1) * 128], PB[bk][:, :],
                                       [f"pb{bk}"], ["St"])
                            CP("pool", Sbf[:, :], St[:, :], ["St"], ["Sbf"])
                            S.dma("sp", hstg[:, :], st_ssdc[j].rearrange("k (c p) -> (k c) p", p=128), writes=["hstg"])
                            TR(PB[6][:, :72], hstg[:, :], ident[:72, :72], ["hstg", "const"], ["pb6"])
                            CP("dve", halo[:, :, :], PB[6][:, :72].rearrange("p (k c) -> p k c", k=3), ["pb6"], ["halo"])
                            S.barrier()
                        else:
                            MEMSET("pool", St[:, :], 0.0, ["St"])
                            MEMSET("pool", Sbf[:, :], 0.0, ["Sbf"])
                            MEMSET("pool", halo[:, :, :], 0.0, ["halo"])
                        prev_stream = stream

                    front((ht, sq[:, 0:8, :], rstd[:, 0, :], xn), hsrc, ci, off, L, normw)

                    S.phase = "ssd_inproj"
                    for kt in range(8):
                        MM(PB[3][:32, :L], Win[:, kt, 5120:5152], xn[:, kt, :L], kt == 0, kt == 7, ["xn", f"Win{kt}"], ["pb3"])
                    ACT(dtT[:, 0, :L], PB[3][:32, :L], AF.Exp, ["pb3", "lw"], ["dtT"], bias=dtb[:, 0:1])
                    ACT(dtT[:, 0, :L], dtT[:, 0, :L], AF.Ln, ["dtT", "const"], ["dtT"], bias=oneb[:32, 0:1])
                    TS("dve", dtT[:, 1, :L], dtT[:, 0, :L], aneg[:, 0:1], ALU.mult, ["dtT", "lw"], ["dtT"])
                    S.op("dve", lambda e: e.tensor_tensor_scan(out=dtT[:, 2, :L], data0=onesr[:, :L], data1=dtT[:, 1, :L],
                                                               initial=0.0, op0=ALU.mult, op1=ALU.add),
                         reads=["dtT", "lw"], writes=["dtT"])
                    ACT(dtT[:, 3, :L], dtT[:, 2, :L], AF.Exp, ["dtT"], ["dtT"], scale=-1.0, bias=dtT[:, 2, L - 1:L])
                    TT("dve", dtT[:, 3, :L], dtT[:, 3, :L], dtT[:, 0, :L], ALU.mult, ["dtT"], ["dtT"])
                    ACT(dtT[:, 4, :L], dtT[:, 2, :L], AF.Exp, ["dtT"], ["dtT"])
                    order = list(range(16, 24)) + list(range(0, 16))
                    proj_conv_b(Win, xn, order, 2048, L, ctall, 8, halo, convw, convb,
                                lambda c0: (xcf[:, c0:c0 + 4, :L], [f"xcf{c0 + i}" for i in range(4)]))
                    for bi in range(4):
                        bk = 1 + bi % 2
                        for sl in range(4):
                            zt = bi * 4 + sl
                            for kt in range(8):
                                MM(PB[bk][:, sl * 128:sl * 128 + L], Win[:, kt, zt * 128:(zt + 1) * 128], xn[:, kt, :L],
                                   kt == 0, kt == 7, ["xn", f"Win{kt}"], [f"pb{bk}"])
                        ACT(zs[:, bi * 4:bi * 4 + 4, :L], bank3(bk, 4, L), AF.Silu, [f"pb{bk}"], [f"zs{bi}"])

                    for q in range(5):
                        TR(PB[3][:L, 64 + q * 32:64 + (q + 1) * 32], dtT[:, q, :L], ident[:32, :32], ["dtT", "const"], ["pb3"])
                    CP("dve", tokm[:L, :, :], PB[3][:L, 64:224].rearrange("p (q h) -> p q h", q=5), ["pb3"], ["tokm"])
                    MM(PB[3][:, 256:288], ones_f[:L, :], tokm[:L, 1, :], True, True, ["tokm", "const"], ["pb3"])
                    ACT(edec[:, :], PB[3][:, 256:288], AF.Exp, ["pb3"], ["edec"])

                    S.phase = "ssd_scan"
                    def scanA(g):
                        d = g % 2
                        hs = slice(8 * g, 8 * g + 8)
                        xk = [f"xcf{4 * g + i}" for i in range(4)]
                        CP("act", bcbf2[d][:, 0, :L], xcf[:, 16 + g, :L], [f"xcf{16 + g}"], [f"bcbf{d}"])
                        CP("act", bcbf2[d][:, 1, :L], xcf[:, 20 + g, :L], [f"xcf{20 + g}"], [f"bcbf{d}"])
                        for i in range(4):
                            TR(PB[4][:L, i * 128:(i + 1) * 128], xcf[:, 4 * g + i, :L], ident[:, :], [xk[i], "const"], ["pb4"])
                        TR(PB[5][:L, 0:128], xcf[:, 16 + g, :L], ident[:, :], [f"xcf{16 + g}", "const"], ["pb5"])
                        MM(PB[5][:L, 128:128 + L], bcbf2[d][:, 0, :L], bcbf2[d][:, 1, :L], True, True, [f"bcbf{d}"], ["pb5"])
                        TT("pool", Dq[:L, 0:8 * L].rearrange("p (h t) -> p h t", h=8), incl[:L, :L].unsqueeze(1).broadcast_to([L, 8, L]),
                           tokm[:L, 1, hs].unsqueeze(2).broadcast_to([L, 8, L]), ALU.mult, ["const", "tokm"], ["Dq"])
                        x3 = PB[4][:L, :].rearrange("p (h q) -> p h q", h=8)
                        TT("dve", xdt2[d][:L, :].rearrange("p (h q) -> p h q", h=8), x3,
                           tokm[:L, 0, hs].unsqueeze(2).broadcast_to([L, 8, 64]), ALU.mult, ["pb4", "tokm"], [f"xdt{d}"])
                        TT("dve", xde2[d][:L, :].rearrange("p (h q) -> p h q", h=8), x3,
                           tokm[:L, 3, hs].unsqueeze(2).broadcast_to([L, 8, 64]), ALU.mult, ["pb4", "tokm"], [f"xde{d}"])
                        CP("act", btok2[d][:L, :], PB[5][:L, 0:128], ["pb5"], [f"btok{d}"])
                        CP("act", cbT[:L, :L], PB[5][:L, 128:128 + L], ["pb5"], ["cbT"])
                        for hf in range(2):
                            bk = 6 + hf
                            MM(PB[bk][:L, 0:4 * L], tri[:L, :L], Dq[:L, 4 * hf * L:(4 * hf + 4) * L], True, False,
                               ["Dq", "const"], [f"pb{bk}"])
                            MM(PB[bk][:L, 0:4 * L], ident_bf[:L, :L], mnegL[L][:L, :], False, True, ["const"], [f"pb{bk}"])
                        for hf in range(2):
                            bk = 6 + hf
                            ACT(dec[:L, 4 * hf * L:(4 * hf + 4) * L], PB[bk][:L, 0:4 * L], AF.Exp, [f"pb{bk}"], ["dec"])
                        TT("dve", WT2[d][:L, 0:8 * L].rearrange("p (h t) -> p h t", h=8),
                           dec[:L, 0:8 * L].rearrange("p (h t) -> p h t", h=8),
                           cbT[:L, :L].unsqueeze(1).broadcast_to([L, 8, L]), ALU.mult, ["dec", "cbT"], [f"WT{d}"])

                    def scanB(g):
                        d = g % 2
                        hs = slice(8 * g, 8 * g + 8)
                        xk = [f"xcf{4 * g + i}" for i in range(4)]
                        MM(PB[1][:L, :], bcbf2[d][:, 1, :L], Sbf[:, g * 512:(g + 1) * 512], True, True, [f"bcbf{d}", f"Sbf{g}"], ["pb1"])
                        for h in range(8):
                            MM(PB[2][:L, h * 64:(h + 1) * 64], WT2[d][:L, h * L:(h + 1) * L], xdt2[d][:L, h * 64:(h + 1) * 64], True, True,
                               [f"WT{d}", f"xdt{d}"], ["pb2"])
                        MM(PB[3][:, :], btok2[d][:L, :], xde2[d][:L, :], True, True, [f"btok{d}", f"xde{d}"], ["pb3"])
                        Sg = St[:, g * 512:(g + 1) * 512]
                        TT("pool", Sg.rearrange("p (h q) -> p h q", h=8), Sg.rearrange("p (h q) -> p h q", h=8),
                           edec[:, hs].unsqueeze(2).broadcast_to([128, 8, 64]), ALU.mult, [f"St{g}", "edec"], [f"St{g}"])
                        TT("dve", ytok[:L, :].rearrange("p (h q) -> p h q", h=8), PB[1][:L, :].rearrange("p (h q) -> p h q", h=8),
                           tokm[:L, 4, hs].unsqueeze(2).broadcast_to([L, 8, 64]), ALU.mult, ["pb1", "tokm"], ["ytok"])
                        TT("dve", ytok[:L, :], ytok[:L, :], PB[2][:L, :], ALU.add, ["pb2", "ytok"], ["ytok"])
                        TT("dve", Sg, Sg, PB[3][:, :], ALU.add, [f"St{g}", "pb3"], [f"St{g}"])
                        CP("act", Sbf[:, g * 512:(g + 1) * 512], Sg, [f"St{g}"], [f"Sbf{g}"])
                        for i in range(4):
                            TR(PB[1][:, i * 128:i * 128 + L], ytok[:L, i * 128:(i + 1) * 128], ident[:L, :L], ["ytok", "const"], ["pb1"])
                        for i in range(4):
                            ct = 4 * g + i
                            STT(yT[:, ct, :L], xcf[:, ct, :L], dsk[:, ct:ct + 1], PB[1][:, i * 128:i * 128 + L], ALU.mult, ALU.add,
                                [xk[i], "lw", "pb1"], [f"yT{g}"])
                        TT("pool", yT[:, 4 * g:4 * g + 4, :L], yT[:, 4 * g:4 * g + 4, :L], zs[:, 4 * g:4 * g + 4, :L], ALU.mult,
                           [f"yT{g}", f"zs{g}"], [f"yT{g}"])
                        TT("pool", sq[:, 4 * g:4 * g + 4, :L], yT[:, 4 * g:4 * g + 4, :L], yT[:, 4 * g:4 * g + 4, :L], ALU.mult,
                           [f"yT{g}"], ["sq"])
                        for i in range(4):
                            MM(PB[0][:, g * 128:g * 128 + L], ones_bf[:, :], sq[:, 4 * g + i, :L], i == 0, i == 3,
                               ["sq", "const"], ["pb0"])

                    scanA(0)
                    for g in range(4):
                        if g + 1 < 4:
                            scanA(g + 1)
                        scanB(g)
                    S.phase = "ssd_out"
                    ACT(rstd[:, :, :L], bank3(0, 4, L), AF.Ln, ["pb0", "const"], ["rstd"], scale=1.0 / 512, bias=epsb[:, 0:1])
                    ACT(rstd[:, :, :L], rstd[:, :, :L], AF.Exp, ["rstd"], ["rstd"], scale=-0.5)
                    for ct in range(16):
                        STT(ygn[:, ct, :L], yT[:, ct, :L], gnw[:, ct:ct + 1], rstd[:, ct // 4, :L], ALU.mult, ALU.mult,
                            [f"yT{ct // 4}", "rstd", "lw"], ["sq"])
                    for hf in range(2):
                        bk = 1 + hf
                        for sl in range(4):
                            dt_ = hf * 4 + sl
                            for kt in range(16):
                                MM(PB[bk][:, sl * 128:sl * 128 + L], Wout[:, kt, dt_ * 128:(dt_ + 1) * 128], ygn[:, kt, :L],
                                   kt == 0, kt == 15, ["sq", f"Wout{kt}"], [f"pb{bk}"])
                        TT("dve", ht[:, 4 * hf:4 * hf + 4, :L], ht[:, 4 * hf:4 * hf + 4, :L], bank3(bk, 4, L), ALU.add,
                           ["ht", f"pb{bk}"], ["ht"])
                    S.dma("sp", hbuf[hdst][:, :, off:off + L], ht[:, :, :L], reads=["ht"], writes=[f"h{hdst}_{ci}"])
                store_state(o_pssd, o_pssdc)

        def gdn_layer(j, hsrc, hdst, last):
            with ExitStack() as es:
                try:
                    gdn_body(mk(es), j, hsrc, hdst, last)
                except _Stop:
                    S.stop_at = None
                    S.stopped = True
                    S.barrier()
                    dbt = mk(es)("dbt", [128, 256])
                    for _ in range(int(os.environ.get("KDUMMY", "0"))):
                        MEMSET(os.environ.get("KDUMMYENG", "dve"), dbt[:, 0:8], 0.0, ["dbt"])
                    for qq in range(4):
                        bkq = 7 if qq < 2 else 6
                        CP("dve", dbt[:, :], PB[bkq][:, (qq % 2) * 256:(qq % 2) * 256 + 256], [f"pb{bkq}"], ["dbt"])
                        S.dma("sp", dbg_o[:, qq * 256:(qq + 1) * 256], dbt[:, :], reads=["dbt"], writes=["dbgo"])
            S.barrier()

        def gdn_body(sb, j, hsrc, hdst, last):
            if True:
                normw = sb("gnormw", [128, 8])
                convw = sb("gconvw", [128, 32, 4])
                dtb = sb("gdtb", [8, 1])
                aneg = sb("ganeg", [8, 1])
                onw = sb("onw", [128, 2])
                Sg = sb("Sg", [128, 8, 256])
                halo = sb("ghalo", [128, 3, 32])
                ht = sb("ght", [128, 8, 128])
                sq = sb("gsq", [128, 8, 128], BF16)
                rstd = sb("grstd", [128, 8, 128])
                xn = sb("gxn", [128, 8, 128], BF16)
                ctall = sb("gctall", [128, 4 * 272])
                tmp = [ctall[:, i * 272:(i + 1) * 272] for i in range(4)]
                cf = sb("cf", [128, 16, 128])
                qn = sb("qn", [128, 8, 128], BF16)
                kn = sb("kn", [128, 8, 128], BF16)
                qg = sb("qg", [128, 8, 128], BF16)
                kb = sb("kb", [128, 8, 128], BF16)
                gq = sb("gq", [8, 6, 128])
                onesr = sb("gonesr", [8, 128])
                BD = ctall
                tokg = sb("tokg", [128, 4, 8])
                egl = sb("egl", [128, 8])
                kbg = [sb(f"kbg{i}", [128, 128], BF16) for i in range(4)]
                kgt = [sb(f"kgt{i}", [128, 128], BF16) for i in range(4)]
                bv = [sb(f"bv{i}", [128, 256], BF16) for i in range(4)]
                QKd = [sb(f"QKd{i}", [128, 128], BF16) for i in range(4)]
                Sbf = [sb(f"gSbf{i}", [128, 256], BF16) for i in range(4)]
                Dqh = [sb(f"Dqh{i}", [128, 128]) for i in range(2)]
                decI = [sb(f"decI{i}", [128, 128]) for i in range(2)]
                decS = [sb(f"decS{i}", [128, 128]) for i in range(2)]
                negw = [sb(f"negw{i}", [128, 128], BF16) for i in range(2)]
                vnew = [sb(f"vnew{i}", [128, 256], BF16) for i in range(2)]
                Pm = [sb(f"Pm{i}", [128, 512]) for i in range(2)]
                Qm = [sb(f"Qm{i}", [128, 512]) for i in range(2)]
                Rm = sb("Rm", [128, 512])
                TTb = sb("TTb", [128, 512], BF16)
                Win = sb("gWin", [128, 8, GDN_IN], BF16)
                Wout = sb("gWout", [128, 16, D], BF16)
                zsb = [Pm[i][:, :].rearrange("p (k t) -> p k t", k=4) for i in range(2)]
                hstg = Rm[:96, 0:128]
                CTK = [f"ctmp{i}" for i in range(4)]
                SQK = ["sq", "sq0", "sq1"]
                RSK = ["rstd", "rstd0", "rstd1"]
                ogn = qn
                ogn2 = kn

                for kt in range(8):
                    S.dma("pool", Win[:, kt, :], W["gdn_w_in"][j, kt * 128:(kt + 1) * 128, :], writes=[f"Win{kt}"])
                for kt in range(16):
                    S.dma("pool", Wout[:, kt, :], W["gdn_w_out"][j, kt * 128:(kt + 1) * 128, :], writes=[f"Wout{kt}"])
                LW = ["lw"]
                S.dma("sp", normw[:], W["gdn_norm_w"][j].rearrange("(k p) -> p k", p=128), writes=LW, slow=True)
                for k in range(4):
                    S.dma("sp", convw[:, :, k], W["gdn_conv_w"][j, k].rearrange("(c p) -> p c", p=128), writes=LW, slow=True)
                S.dma("sp", dtb[:], W["gdn_dt_bias"][j].rearrange("(h o) -> h o", o=1), writes=LW, slow=True)
                S.dma("sp", aneg[:], W["gdn_a_log"][j].rearrange("(h o) -> h o", o=1), writes=LW, slow=True)
                S.dma("sp", onw[:], W["gdn_onorm_w"][j].rearrange("(c p) -> p c", p=128), writes=LW, slow=True)
                ACT(aneg[:], aneg[:], AF.Exp, LW, LW)
                TS("dve", aneg[:], aneg[:], -1.0, ALU.mult, LW, LW)
                MEMSET("pool", onesr[:], 1.0, LW)

                def store_state(o_state, o_conv):
                    S.dma("sp", o_state[j].rearrange("h k v -> k h v"), Sg[:, :, :], reads=[f"Sg{h}" for h in range(8)],
                          writes=["ostate"])
                    TR(PB[6][:96, :128], halo[:, :, :].rearrange("p k c -> p (k c)"), ident[:, :], ["halo", "const"], ["pb6"])
                    CP("dve", hstg[:, :], PB[6][:96, :128], ["pb6"], ["Rm"])
                    for k in range(3):
                        S.dma("sp", o_conv[j, k].rearrange("(c p) -> c p", p=128), hstg[k * 32:(k + 1) * 32, :],
                              reads=["Rm"], writes=["oconv"])

                def proj_conv(bk_list, tiles, col0, L, dst_of):
                    for bi in range(len(tiles) // 4):
                        bk = bk_list[bi % 2]
                        for sl in range(4):
                            ct = tiles[bi * 4 + sl]
                            c0 = col0 + ct * 128
                            for kt in range(8):
                                MM(PB[bk][:, sl * 128:sl * 128 + L], Win[:, kt, c0:c0 + 128], xn[:, kt, :L], kt == 0, kt == 7,
                                   ["xn", f"Win{kt}"], [f"pb{bk}"])
                        for sl in range(4):
                            ct = tiles[bi * 4 + sl]
                            out_ap, key = dst_of(ct)
                            conv_tile(bk, sl, L, ct, tmp, halo, convw, None, out_ap, key)

                prev_stream = None
                for ci, (stream, off, L) in enumerate(chunks):
                    NLEV = {128: 6, 16: 3}[L]
                    if stream != prev_stream:
                        if prev_stream == "s":
                            store_state(o_sgdn, o_sgdnc)
                        if stream == "s":
                            S.dma("sp", Sg[:, :, :], st_gdn[j].rearrange("h k v -> k h v"), writes=[f"Sg{h}" for h in range(8)])
                            S.dma("sp", hstg[:, :], st_gdnc[j].rearrange("k (c p) -> (k c) p", p=128), writes=["Rm"])
                            TR(PB[6][:, :96], hstg[:, :], ident[:96, :96], ["Rm", "const"], ["pb6"])
                            CP("dve", halo[:, :, :], PB[6][:, :96].rearrange("p (k c) -> p k c", k=3), ["pb6"], ["halo"])
                        else:
                            MEMSET("pool", Sg[:, :, :], 0.0, [f"Sg{h}" for h in range(8)])
                            MEMSET("pool", halo[:, :, :], 0.0, ["halo"])
                        prev_stream = stream

                    front((ht, sq, rstd[:, 0, :], xn), hsrc, ci, off, L, normw, SQK, RSK)
                    if dbg and dbg <= 1:
                        break

                    S.phase = "gdn_gates"
                    for q, c0 in ((0, 6144), (1, 6152)):
                        for kt in range(8):
                            MM(PB[3][:8, q * 128:q * 128 + L], Win[:, kt, c0:c0 + 8], xn[:, kt, :L], kt == 0, kt == 7,
                               ["xn", f"Win{kt}"], ["pb3"])
                    ACT(gq[:, 0, :L], PB[3][:8, 0:L], AF.Exp, ["pb3", "lw"], ["gq"], bias=dtb[:, 0:1])
                    ACT(gq[:, 0, :L], gq[:, 0, :L], AF.Ln, ["gq", "const"], ["gq"], bias=oneb[:8, 0:1])
                    TS("dve", gq[:, 0, :L], gq[:, 0, :L], aneg[:, 0:1], ALU.mult, ["gq", "lw"], ["gq"])
                    S.op("dve", lambda e: e.tensor_tensor_scan(out=gq[:, 1, :L], data0=onesr[:, :L], data1=gq[:, 0, :L],
                                                               initial=0.0, op0=ALU.mult, op1=ALU.add),
                         reads=["gq", "lw"], writes=["gq"])
                    ACT(gq[:, 2, :L], PB[3][:8, 128:128 + L], AF.Exp, ["pb3"], ["gq"], scale=-1.0)
                    TS("dve", gq[:, 2, :L], gq[:, 2, :L], 1.0, ALU.add, ["gq"], ["gq"])
                    S.op("dve", lambda e: e.reciprocal(out=gq[:, 2, :L], in_=gq[:, 2, :L]), reads=["gq"], writes=["gq"])
                    ACT(gq[:, 3, :L], gq[:, 1, :L], AF.Exp, ["gq"], ["gq"])
                    TT("dve", gq[:, 4, :L], gq[:, 2, :L], gq[:, 3, :L], ALU.mult, ["gq"], ["gq"])
                    ACT(gq[:, 5, :L], gq[:, 1, :L], AF.Exp, ["gq"], ["gq"], scale=-1.0, bias=gq[:, 1, L - 1:L])
                    if dbg and dbg <= 2:
                        break
                    S.phase = "gdn_qk"
                    proj_conv_b(Win, xn, list(range(16)), 0, L, ctall, 4, halo, convw, None,
                                lambda c0: (cf[:, c0:c0 + 4, :L], [f"cf{c0 + i}" for i in range(4)]))
                    for qi, q in enumerate((0, 2, 4, 5)):
                        TR(PB[3][:L, 256 + qi * 8:256 + (qi + 1) * 8], gq[:, q, :L], ident[:8, :8], ["gq", "const"], ["pb3"])
                    CP("dve", tokg[:L, :, :], PB[3][:L, 256:288].rearrange("p (q h) -> p q h", q=4), ["pb3"], ["tokg"])
                    MM(PB[3][:, 320:328], ones_f[:L, :], tokg[:L, 0, :], True, True, ["tokg", "const"], ["pb3"])
                    ACT(egl[:, :], PB[3][:, 320:328], AF.Exp, ["pb3"], ["egl"])

                    for qd in range(4):
                        t0 = qd * 4
                        d2 = qd % 2
                        bkq = 0 if d2 == 0 else 3
                        sqv = sq[:, 4 * d2:4 * d2 + 4, :L]
                        rsv = rstd[:, 4 * d2:4 * d2 + 4, :L]
                        cfk = [f"cf{t0 + i}" for i in range(4)]
                        TT("pool", sqv, cf[:, t0:t0 + 4, :L], cf[:, t0:t0 + 4, :L], ALU.mult, cfk, [f"sq{d2}"])
                        for i in range(4):
                            MM(PB[bkq][:, i * 128:i * 128 + L], ones_bf[:, :], sq[:, 4 * d2 + i, :L], True, True,
                               [f"sq{d2}", "const"], [f"pb{bkq}"])
                        ACT(rsv, bank3(bkq, 4, L), AF.Ln, [f"pb{bkq}", "const"], [f"rstd{d2}"], bias=epsb[:, 0:1])
                        ACT(rsv, rsv, AF.Exp, [f"rstd{d2}"], [f"rstd{d2}"], scale=-0.5)
                        if qd < 2:
                            STT(qn[:, t0:t0 + 4, :L], cf[:, t0:t0 + 4, :L], float(GDN_DK ** -0.5), rsv, ALU.mult, ALU.mult,
                                cfk + [f"rstd{d2}"], ["qn"])
                        else:
                            TT("dve", kn[:, t0 - 8:t0 - 4, :L], cf[:, t0:t0 + 4, :L], rsv, ALU.mult, cfk + [f"rstd{d2}"], ["kn"])
                    if dbg and dbg <= 3:
                        break
                    S.phase = "gdn_rep"
                    for q, src, dst, dkey in ((2, kn, kb, "kb"), (3, qn, qg, "qg")):
                        TT("pool", BD[:8, 0:8 * L].rearrange("p (h t) -> p h t", h=8),
                           gq[:8, q, :L].unsqueeze(1).broadcast_to([8, 8, L]),
                           ident[:8, :8].unsqueeze(2).broadcast_to([8, 8, L]), ALU.mult, ["gq", "const"], CTK)
                        for hf in range(2):
                            MM(PB[4 + hf][:, 0:4 * L], ones_f[:8, :], BD[:8, 4 * hf * L:(4 * hf + 4) * L], True, True,
                               CTK + ["const"], [f"pb{4 + hf}"])
                            TT("dve", dst[:, 4 * hf:4 * hf + 4, :L], src[:, 4 * hf:4 * hf + 4, :L],
                               PB[4 + hf][:, 0:4 * L].rearrange("p (h t) -> p h t", h=4), ALU.mult,
                               [f"pb{4 + hf}", "qn" if q == 3 else "kn"], [dkey])
                    if dbg and dbg <= 4:
                        break
                    S.phase = "gdn_v"
                    proj_conv_b(Win, xn, list(range(16, 32)), 0, L, ctall, 4, halo, convw, None,
                                lambda c0: (cf[:, c0 - 16:c0 - 12, :L], [f"cf{c0 - 16 + i}" for i in range(4)]))

                    if dbg and dbg <= 5:
                        break
                    S.phase = "gdn_phI"
                    for hb in range(2):
                        S.phase = "gdn_phI"
                        for pr in range(2):
                            HH = (2 * pr, 2 * pr + 1)
                            for hh in HH:
                                h = hb * 4 + hh
                                CP("act", Sbf[hh][:, :], Sg[:, h, :], [f"Sg{h}"], [f"Sbf{hh}"])
                            for hh in HH:
                                h = hb * 4 + hh
                                p = hh % 2
                                bA, bB = 4 + p, 6 + p
                                kA, kB = f"pb{bA}", f"pb{bB}"
                                MM(PB[bA][:L, 0:128], kn[:, h, :L], ident_bf[:, :], True, True, ["kn", "const"], [kA])
                                for vt in range(2):
                                    TR(PB[bA][:L, 128 + vt * 128:256 + vt * 128], cf[:, 2 * h + vt, :L], ident[:, :],
                                       [f"cf{2 * h + vt}", "const"], [kA])
                                MM(PB[bB][:L, 0:L], kn[:, h, :L], kb[:, h, :L], True, True, ["kn", "kb"], [kB])
                                MM(PB[bB][:L, 128:128 + L], kn[:, h, :L], qn[:, h, :L], True, True, ["kn", "qn"], [kB])
                            for hh in HH:
                                h = hb * 4 + hh
                                p = hh % 2
                                TS("dve", Dqh[p][:L, :L], incl[:L, :L], tokg[:L, 0, h:h + 1], ALU.mult, ["const", "tokg"], [f"Dqh{p}"])
                            for hh in HH:
                                p = hh % 2
                                bB = 6 + p
                                kB = f"pb{bB}"
                                MM(PB[bB][:L, 256:256 + L], tri[:L, :L], Dqh[p][:L, :L], True, False, [f"Dqh{p}", "const"], [kB])
                                MM(PB[bB][:L, 256:256 + L], ident_bf[:L, :L], mnegL[L][:L, 0:L], False, True, ["const"], [kB])
                                MM(PB[bB][:L, 384:384 + L], tri[:L, :L], Dqh[p][:L, :L], True, False, [f"Dqh{p}", "const"], [kB])
                                MM(PB[bB][:L, 384:384 + L], ident_bf[:L, :L], mnegsL[L][:L, 0:L], False, True, ["const"], [kB])
                            for hh in HH:
                                h = hb * 4 + hh
                                p = hh % 2
                                bA = 4 + p
                                kA = f"pb{bA}"
                                TS("dve", kbg[hh][:L, :], PB[bA][:L, 0:128], tokg[:L, 2, h:h + 1], ALU.mult, [kA, "tokg"], [f"kbg{hh}"])
                                TS("dve", kgt[hh][:L, :], PB[bA][:L, 0:128], tokg[:L, 3, h:h + 1], ALU.mult, [kA, "tokg"], [f"kgt{hh}"])
                                TS("dve", bv[hh][:L, :], PB[bA][:L, 128:384], tokg[:L, 1, h:h + 1], ALU.mult, [kA, "tokg"], [f"bv{hh}"])
                            for hh in HH:
                                p = hh % 2
                                bB = 6 + p
                                kB = f"pb{bB}"
                                ACT(decI[p][:L, :L], PB[bB][:L, 256:256 + L], AF.Exp, [kB], [f"decI{p}"])
                                ACT(decS[p][:L, :L], PB[bB][:L, 384:384 + L], AF.Exp, [kB], [f"decS{p}"])
                            for hh in HH:
                                p = hh % 2
                                bB = 6 + p
                                kB = f"pb{bB}"
                                TT("dve", QKd[hh][:L, :L], PB[bB][:L, 128:128 + L], decI[p][:L, :L], ALU.mult, [kB, f"decI{p}"], [f"QKd{hh}"])
                                STT(Pm[0][:L, hh * L:(hh + 1) * L], PB[bB][:L, 0:L], -1.0, decS[p][:L, :L], ALU.mult, ALU.mult,
                                    [kB, f"decS{p}"], ["Pm0"])
                        if dbg and dbg <= 6:
                            break
                        S.phase = "gdn_neu"
                        for hh in range(4):
                            sl = slice(hh * L, (hh + 1) * L)
                            TR(PB[1][:L, sl], Pm[0][:L, sl], ident[:L, :L], ["Pm0", "const"], ["pb1"])
                        CP("act", Qm[0][:L, 0:4 * L], PB[1][:L, 0:4 * L], ["pb1"], ["Qm0"])
                        i4 = ident[:L, :L].unsqueeze(1).broadcast_to([L, 4, L])
                        TT("pool", Rm[:L, 0:4 * L].rearrange("p (h t) -> p h t", h=4),
                           Pm[0][:L, 0:4 * L].rearrange("p (h t) -> p h t", h=4), i4, ALU.add, ["Pm0", "const"], ["Rm"])
                        for lev in range(1, NLEV + 1):
                            a, b = (lev - 1) % 2, lev % 2
                            lastlev = lev == NLEV
                            for hh in range(4):
                                sl = slice(hh * L, (hh + 1) * L)
                                MM(PB[1][:L, sl], Pm[a][:L, sl], Qm[a][:L, sl], True, True, [f"Qm{a}", f"Pm{a}"], ["pb1"])
                            CP("act", Qm[b][:L, 0:4 * L], PB[1][:L, 0:4 * L], ["pb1"], [f"Qm{b}"])
                            if not lastlev:
                                for hh in range(4):
                                    sl = slice(hh * L, (hh + 1) * L)
                                    MM(PB[0][:L, sl], Qm[a][:L, sl], Pm[a][:L, sl], True, True, [f"Qm{a}", f"Pm{a}"], ["pb0"])
                                CP("dve", Pm[b][:L, 0:4 * L], PB[0][:L, 0:4 * L], ["pb0"], [f"Pm{b}"])
                            for hh in range(4):
                                sl = slice(hh * L, (hh + 1) * L)
                                MM(PB[2 + lev % 2][:L, sl], Qm[b][:L, sl], Rm[:L, sl], True, True, ["Rm", f"Qm{b}"], [f"pb{2 + lev % 2}"])
                            TT("dve", Rm[:L, 0:4 * L], Rm[:L, 0:4 * L], PB[2 + lev % 2][:L, 0:4 * L], ALU.add,
                               ["Rm", f"pb{2 + lev % 2}"], ["Rm"])
                        CP("act", TTb[:L, 0:4 * L], Rm[:L, 0:4 * L], ["Rm"], ["TTb"])
                        if dbg and dbg <= 7:
                            break
                        S.phase = "gdn_phIII"
                        for pr in range(2):
                            HH = (2 * pr, 2 * pr + 1)

                            def ctx(hh):
                                h = hb * 4 + hh
                                p = hh % 2
                                return h, p, 4 + p, 6 + p, f"pb{4 + p}", f"pb{6 + p}", slice(hh * L, (hh + 1) * L)
                            for hh in HH:
                                h, p, bA, bB, kA, kB, sl = ctx(hh)
                                MM(PB[bA][:, 0:L], kbg[hh][:L, :], TTb[:L, sl], True, True, [f"kbg{hh}", "TTb"], [kA])
                            for hh in HH:
                                h, p, bA, bB, kA, kB, sl = ctx(hh)
                                ACT(negw[p][:, :L], PB[bA][:, 0:L], AF.Copy, [kA], [f"negw{p}"], scale=-1.0)
                            for hh in HH:
                                h, p, bA, bB, kA, kB, sl = ctx(hh)
                                MM(PB[bA][:L, 128:384], TTb[:L, sl], bv[hh][:L, :], True, False, [f"bv{hh}", "TTb"], [kA])
                                MM(PB[bA][:L, 128:384], negw[p][:, :L], Sbf[hh][:, :], False, True, [f"negw{p}", f"Sbf{hh}"], [kA])
                            for hh in HH:
                                h, p, bA, bB, kA, kB, sl = ctx(hh)
                                CP("dve", vnew[p][:L, :], PB[bA][:L, 128:384], [kA], [f"vnew{p}"])
                            for hh in HH:
                                h, p, bA, bB, kA, kB, sl = ctx(hh)
                                for vt in range(2):
                                    MM(PB[bB][:, vt * 128:vt * 128 + L], Sbf[hh][:, vt * 128:(vt + 1) * 128], qg[:, h, :L], True, False,
                                       [f"Sbf{hh}", "qg"], [kB])
                                    MM(PB[bB][:, vt * 128:vt * 128 + L], vnew[p][:L, vt * 128:(vt + 1) * 128], QKd[hh][:L, :L], False, True,
                                       [f"vnew{p}", f"QKd{hh}"], [kB])
                                MM(PB[bB][:, 256:512], kgt[hh][:L, :], vnew[p][:L, :], True, True, [f"kgt{hh}", f"vnew{p}"], [kB])
                            for hh in HH:
                                h, p, bA, bB, kA, kB, sl = ctx(hh)
                                CP("act", cf[:, 2 * h:2 * h + 2, :L], bank3(bB, 2, L), [kB], [f"cf{2 * h}", f"cf{2 * h + 1}"])
                            for hh in HH:
                                h, p, bA, bB, kA, kB, sl = ctx(hh)
                                STT(Sg[:, h, :], Sg[:, h, :], egl[:, h:h + 1], PB[bB][:, 256:512], ALU.mult, ALU.add,
                                    [f"Sg{h}", "egl", kB], [f"Sg{h}"])

                    if dbg and dbg <= 8:
                        break
                    S.phase = "gdn_out"
                    cfall = [f"cf{t}" for t in range(16)]

                    def zmm(bi):
                        bk = 1 + bi % 2
                        for sl_ in range(4):
                            zt = bi * 4 + sl_
                            c0 = 4096 + zt * 128
                            for kt in range(8):
                                MM(PB[bk][:, sl_ * 128:sl_ * 128 + L], Win[:, kt, c0:c0 + 128], xn[:, kt, :L], kt == 0, kt == 7,
                                   ["xn", f"Win{kt}"], [f"pb{bk}"])

                    def zgate(bi):
                        bk = 1 + bi % 2
                        zb = zsb[bi % 2]
                        ACT(zb[:, :, :L], bank3(bk, 4, L), AF.Silu, [f"pb{bk}"], [f"Pm{bi % 2}"])
                        for sl_ in range(4):
                            zt = bi * 4 + sl_
                            STT(cf[:, zt, :L], cf[:, zt, :L], onw[:, zt % 2:zt % 2 + 1], rstd[:, zt // 2, :L], ALU.mult, ALU.mult,
                                [f"cf{zt}", "lw"] + RSK, [f"cf{zt}"])
                        dsto = (ogn if bi < 2 else ogn2)[:, (bi % 2) * 4:(bi % 2) * 4 + 4, :L]
                        TT("dve", dsto, cf[:, bi * 4:bi * 4 + 4, :L], zb[:, :, :L], ALU.mult,
                           [f"cf{bi * 4 + i}" for i in range(4)] + [f"Pm{bi % 2}"], ["qn" if bi < 2 else "kn"])

                    zmm(0)
                    zmm(1)
                    for hf in range(2):
                        TT("pool", sq[:, :, :L], cf[:, 8 * hf:8 * hf + 8, :L], cf[:, 8 * hf:8 * hf + 8, :L], ALU.mult, cfall, SQK)
                        bk = 0 if hf == 0 else 3
                        for hh in range(4):
                            for vt in range(2):
                                MM(PB[bk][:, hh * 128:hh * 128 + L], ones_bf[:, :], sq[:, 2 * hh + vt, :L], vt == 0, vt == 1,
                                   SQK + ["const"], [f"pb{bk}"])
                        ACT(rstd[:, 4 * hf:4 * hf + 4, :L], bank3(bk, 4, L), AF.Ln, [f"pb{bk}", "const"], RSK,
                            scale=1.0 / GDN_DV, bias=epsb[:, 0:1])
                    ACT(rstd[:, :, :L], rstd[:, :, :L], AF.Exp, RSK, RSK, scale=-0.5)
                    zgate(0)
                    zmm(2)
                    zgate(1)
                    zmm(3)
                    zgate(2)
                    zgate(3)
                    for hf in range(2):
                        bk = 1 + hf
                        for sl_ in range(4):
                            dt_ = hf * 4 + sl_
                            for kt in range(16):
                                src_o = (ogn if kt < 8 else ogn2)[:, kt % 8, :L]
                                MM(PB[bk][:, sl_ * 128:sl_ * 128 + L], Wout[:, kt, dt_ * 128:(dt_ + 1) * 128], src_o,
                                   kt == 0, kt == 15, ["qn", "kn", f"Wout{kt}"], [f"pb{bk}"])
                        TT("dve", ht[:, 4 * hf:4 * hf + 4, :L], ht[:, 4 * hf:4 * hf + 4, :L], bank3(bk, 4, L), ALU.add,
                           ["ht", f"pb{bk}"], ["ht"])
                    S.dma("sp", hbuf[hdst][:, :, off:off + L], ht[:, :, :L], reads=["ht"], writes=[f"h{hdst}_{ci}"])
                store_state(o_pgdn, o_pgdnc)

        cur = 0
        if dbg and dbg > 100:
            S.stop_at = dbg
        S.stopped = False
        for li in range(NL):
            if li % 2 == 0:
                ssd_layer(li // 2, cur, 1 - cur, li == NL - 1)
            else:
                gdn_layer(li // 2, cur, 1 - cur, li == NL - 1)
            cur = 1 - cur
            if S.stopped:
                print("stopped at", S.n_inst)
                break

        S.phase = "epilogue"
        with ExitStack() as es:
            sb = mk(es)
            htile = [sb(f"htile{i}", [128, 8, 128]) for i in range(2)]
            sq = [sb(f"sq{i}", [128, 8, 128], BF16) for i in range(2)]
            rstd = [sb(f"rstd{i}", [128, 128]) for i in range(2)]
            ytile = [sb(f"ytile{i}", [128, 8, 128]) for i in range(2)]
            ytok = [sb(f"ytok{i}", [128, D]) for i in range(2)]
            for ci, (stream, off, L) in enumerate(chunks):
                if stream == "p" and off == 0:
                    continue
                b = ci % 2
                S.dma("sp", htile[b][:, :, :L], hbuf[cur][:, :, off:off + L], reads=[f"h{cur}_{ci}"], writes=[f"htile{b}"])
                TT("pool", sq[b][:, :, :L], htile[b][:, :, :L], htile[b][:, :, :L], ALU.mult, [f"htile{b}"], [f"sq{b}"])
                bk = 4 + b
                for kt in range(8):
                    MM(PB[bk][:, :L], ones_bf[:, :], sq[b][:, kt, :L], kt == 0, kt == 7, [f"sq{b}", "const"], [f"pb{bk}"])
                ACT(rstd[b][:, :L], PB[bk][:, :L], AF.Ln, [f"pb{bk}", "const"], [f"rstd{b}"], scale=1.0 / D, bias=epsb[:, 0:1])
                ACT(rstd[b][:, :L], rstd[b][:, :L], AF.Exp, [f"rstd{b}"], [f"rstd{b}"], scale=-0.5)
                for kt in range(8):
                    STT(ytile[b][:, kt, :L], htile[b][:, kt, :L], fnw[:, kt:kt + 1], rstd[b][:, :L], ALU.mult, ALU.mult,
                        [f"htile{b}", "const", f"rstd{b}"], [f"ytile{b}"])
                for kt in range(8):
                    bk2 = 2 * b + kt // 4
                    TR(PB[bk2][:L, (kt % 4) * 128:(kt % 4 + 1) * 128], ytile[b][:, kt, :L], ident[:, :],
                       [f"ytile{b}", "const"], [f"pb{bk2}"])
                for half in range(2):
                    bk2 = 2 * b + half
                    CP("act" if half else "dve", ytok[b][:L, 512 * half:512 * half + 512], PB[bk2][:L, :], [f"pb{bk2}"], [f"ytok{b}"])
                dst = y_s[:, :] if stream == "s" else y_p[off - 16: off - 16 + L, :]
                S.dma("sp", dst, ytok[b][:L, :], reads=[f"ytok{b}"], writes=[f"yout{ci}"])
        S.finish("sp")
        print("instructions:", S.n_inst, "sems:", S.nsem)
    return nc


_PROG_CACHE = {}

_WNAMES = ["ssd_norm_w", "ssd_w_in", "ssd_conv_w", "ssd_conv_b", "ssd_dt_bias", "ssd_a_log", "ssd_d", "ssd_gnorm_w",
           "ssd_w_out", "gdn_norm_w", "gdn_w_in", "gdn_conv_w", "gdn_dt_bias", "gdn_a_log", "gdn_onorm_w", "gdn_w_out",
           "final_norm_w"]


def kernel(**inputs):
    f32 = lambda a: np.ascontiguousarray(np.asarray(a), dtype=np.float32)
    x_prompt = f32(inputs["x_prompt"])
    x_sample = f32(inputs["x_sample"])
    B, SEQ, _ = x_prompt.shape
    NS = x_sample.shape[0]
    n = 8
    assert NS == n and B <= n
    if SEQ not in _PROG_CACHE:
        _PROG_CACHE[SEQ] = build_program(SEQ)
    nc = _PROG_CACHE[SEQ]
    wts = {k: f32(inputs[k]) for k in _WNAMES}
    st = {k: f32(inputs[k]) for k in ["state_ssd", "state_ssd_conv", "state_gdn", "state_gdn_conv"]}
    meta = f32(inputs["meta_tokens"])
    in_maps = []
    for c in range(n):
        m = {"x_prompt": x_prompt[c % B], "x_sample": x_sample[c], "meta_tokens": meta}
        for k, v in st.items():
            m[k] = np.ascontiguousarray(v[:, c])
        m.update(wts)
        in_maps.append(m)
    res = run_bass_kernel_spmd(nc, in_maps, core_ids=list(range(n)))
    R = res.results
    y_prompt = np.stack([R[b]["y_prompt"] for b in range(B)])
    y_sample = np.stack([R[c]["y_sample"] for c in range(n)])
    outs = [y_prompt, y_sample]
    for nm in ["p_ssd", "p_ssdc", "p_gdn", "p_gdnc"]:
        outs.append(np.stack([R[b][nm] for b in range(B)], axis=1))
    for nm in ["s_ssd", "s_ssdc", "s_gdn", "s_gdnc"]:
        outs.append(np.stack([R[c][nm] for c in range(n)], axis=1))
    return tuple(np.ascontiguousarray(o, dtype=np.float32) for o in outs)
```
